# Optimizing a Trainium2 kernel written in Bass

```python
import jax
import jax.numpy as jnp
from jax import lax
import numpy as np

D_MODEL = 1024
BATCH = 2
SEQ = 8192
DEPTH = 4

GRID_W = 64
CTX_LEN = 256
N_MIXERS = 4
N_MLA_LAYERS = (DEPTH + 3) // 4
N_GQA_LAYERS = (DEPTH + 2) // 4
N_NA_LAYERS = (DEPTH + 1) // 4
N_RET_LAYERS = DEPTH // 4
Q_BLOCK = 128
ROPE_THETA = 10000.0
EPS = 1e-6
MLA_HEADS = 8
MLA_Q_RANK = 384
MLA_KV_RANK = 256
MLA_NOPE = 128
MLA_ROPE = 64
MLA_QK = MLA_NOPE + MLA_ROPE
MLA_V = 128
GQA_HEADS = 8
GQA_KV_HEADS = 2
GQA_HEAD_DIM = 128
NA_HEADS = 16
NA_HEAD_DIM = 64
NA_WIN_ROWS = 8
NA_WIN_COLS = 16
RET_HEADS = 4
RET_QK_DIM = 256
RET_V_DIM = 512
RET_CHUNK = 128
FFN_HIDDEN = ((8 * D_MODEL + 3 * 256 - 1) // (3 * 256)) * 256
F32 = jnp.float32

kernel_name = 'hybrid_diffusion_interleaved_mixers'


def rms_norm(x, gain):
    xf = x.astype(F32)
    y = xf * lax.rsqrt(jnp.mean(xf * xf, axis=-1, keepdims=True) + EPS)
    return (y * gain.astype(F32)).astype(x.dtype)


def rope_tables(length, dim):
    t = jnp.arange(length)
    row = (t // GRID_W).astype(F32)
    col = (t % GRID_W).astype(F32)
    quarter = dim // 4
    inv_freq = ROPE_THETA ** (-jnp.arange(quarter, dtype=F32) / quarter)
    ang = jnp.concatenate([row[:, None] * inv_freq, col[:, None] * inv_freq], axis=-1)
    return jnp.cos(ang), jnp.sin(ang)


def apply_rope(x, cos, sin):
    half = x.shape[-1] // 2
    xf = x.astype(F32)
    x1, x2 = xf[..., :half], xf[..., half:]
    c, s = cos[None, :, None, :], sin[None, :, None, :]
    return jnp.concatenate([x1 * c - x2 * s, x1 * s + x2 * c], axis=-1).astype(x.dtype)


def dense_attention(q, k, v, scale):
    b, s, h, dq = q.shape
    kvh, dv = k.shape[2], v.shape[-1]
    g = h // kvh
    qb = q.reshape(b, s // Q_BLOCK, Q_BLOCK, kvh, g, dq).transpose(1, 0, 2, 3, 4, 5)

    def block(q_blk):
        sc = jnp.einsum('bqkgd,btkd->bkgqt', q_blk, k).astype(F32) * scale
        p = jax.nn.softmax(sc, axis=-1).astype(v.dtype)
        return jnp.einsum('bkgqt,btkd->bqkgd', p, v)

    o = lax.map(block, qb)
    return o.transpose(1, 0, 2, 3, 4, 5).reshape(b, s, h, dv)


def joint_attention(ql, kl, vl, qc, kc, vc, scale):
    ol = dense_attention(ql, jnp.concatenate([kc, kl], axis=1), jnp.concatenate([vc, vl], axis=1), scale)
    oc = dense_attention(qc, kc, vc, scale) if qc is not None else None
    return ol, oc


def merge_heads(o, w_o):
    return o.reshape(o.shape[0], o.shape[1], -1) @ w_o


def mla_mixer(xl, xc, w_down, q_lora_norm, kv_lora_norm, w_uq, w_ukv, q_norm, k_norm, w_o, need_ctx):
    b, s, _ = xl.shape
    cos, sin = rope_tables(s, MLA_ROPE)

    def queries(c_q):
        q = (rms_norm(c_q, q_lora_norm) @ w_uq).reshape(b, c_q.shape[1], MLA_HEADS, MLA_QK)
        return rms_norm(q, q_norm)

    def keys_values(c_kv, k_rope):
        l = c_kv.shape[1]
        kv = (rms_norm(c_kv, kv_lora_norm) @ w_ukv).reshape(b, l, MLA_HEADS, MLA_NOPE + MLA_V)
        k = jnp.concatenate([kv[..., :MLA_NOPE],
                             jnp.broadcast_to(k_rope[:, :, None, :], (b, l, MLA_HEADS, MLA_ROPE))], axis=-1)
        return rms_norm(k, k_norm), kv[..., MLA_NOPE:]

    def rotate(t):
        return jnp.concatenate([t[..., :MLA_NOPE], apply_rope(t[..., MLA_NOPE:], cos, sin)], axis=-1)

    dl = xl @ w_down
    ql = rotate(queries(dl[..., :MLA_Q_RANK]))
    kl, vl = keys_values(dl[..., MLA_Q_RANK:MLA_Q_RANK + MLA_KV_RANK], dl[..., MLA_Q_RANK + MLA_KV_RANK:])
    kl = rotate(kl)
    if need_ctx:
        dc = xc @ w_down
        qc = queries(dc[..., :MLA_Q_RANK])
        dkv = dc[..., MLA_Q_RANK:]
    else:
        qc = None
        dkv = xc @ w_down[:, MLA_Q_RANK:]
    kc, vc = keys_values(dkv[..., :MLA_KV_RANK], dkv[..., MLA_KV_RANK:])
    ol, oc = joint_attention(ql, kl, vl, qc, kc, vc, MLA_QK ** -0.5)
    return merge_heads(ol, w_o), (merge_heads(oc, w_o) if need_ctx else None)


def gqa_project(x, w_qkv, q_norm, k_norm, with_q):
    b, l, _ = x.shape
    nq = GQA_HEADS * GQA_HEAD_DIM
    if with_q:
        p = x @ w_qkv
        q = rms_norm(p[..., :nq].reshape(b, l, GQA_HEADS, GQA_HEAD_DIM), q_norm)
        p = p[..., nq:]
    else:
        q = None
        p = x @ w_qkv[:, nq:]
    kv = p.reshape(b, l, 2, GQA_KV_HEADS, GQA_HEAD_DIM)
    return q, rms_norm(kv[:, :, 0], k_norm), kv[:, :, 1]


def gqa_mixer(xl, xc, w_qkv, q_norm, k_norm, w_o, need_ctx):
    cos, sin = rope_tables(xl.shape[1], GQA_HEAD_DIM)
    ql, kl, vl = gqa_project(xl, w_qkv, q_norm, k_norm, True)
    ql, kl = apply_rope(ql, cos, sin), apply_rope(kl, cos, sin)
    qc, kc, vc = gqa_project(xc, w_qkv, q_norm, k_norm, need_ctx)
    ol, oc = joint_attention(ql, kl, vl, qc, kc, vc, GQA_HEAD_DIM ** -0.5)
    return merge_heads(ol, w_o), (merge_heads(oc, w_o) if need_ctx else None)


def na_project(x, w_qkv, q_norm, k_norm, with_q):
    b, l, _ = x.shape
    nq = NA_HEADS * NA_HEAD_DIM
    if with_q:
        p = x @ w_qkv
        q = rms_norm(p[..., :nq].reshape(b, l, NA_HEADS, NA_HEAD_DIM), q_norm)
        p = p[..., nq:]
    else:
        q = None
        p = x @ w_qkv[:, nq:]
    kv = p.reshape(b, l, 2, NA_HEADS, NA_HEAD_DIM)
    return q, rms_norm(kv[:, :, 0], k_norm), kv[:, :, 1]


def na_mixer(xl, xc, w_qkv, q_norm, k_norm, rpb, w_o, need_ctx):
    b, s, _ = xl.shape
    rows = s // GRID_W
    wr = min(NA_WIN_ROWS, rows)
    wc = NA_WIN_COLS
    scale = NA_HEAD_DIM ** -0.5
    ql, kl, vl = na_project(xl, w_qkv, q_norm, k_norm, True)
    qc, kc, vc = na_project(xc, w_qkv, q_norm, k_norm, need_ctx)
    qg = ql.reshape(b, rows, GRID_W, NA_HEADS, NA_HEAD_DIM).transpose(1, 0, 2, 3, 4)
    kg = kl.reshape(b, rows, GRID_W, NA_HEADS, NA_HEAD_DIM)
    vg = vl.reshape(b, rows, GRID_W, NA_HEADS, NA_HEAD_DIM)
    col = jnp.arange(GRID_W)
    col_start = jnp.clip(col - wc // 2, 0, GRID_W - wc)
    col_idx = col_start[:, None] + jnp.arange(wc)[None, :]
    col_rel = col_idx - col[:, None] + (NA_WIN_COLS - 1)

    def one_row(args):
        r, q_r = args
        r0 = jnp.clip(r - wr // 2, 0, rows - wr)
        k_win = lax.dynamic_slice_in_dim(kg, r0, wr, axis=1)[:, :, col_idx]
        v_win = lax.dynamic_slice_in_dim(vg, r0, wr, axis=1)[:, :, col_idx]
        row_rel = r0 + jnp.arange(wr) - r + (NA_WIN_ROWS - 1)
        bias = rpb[:, row_rel[:, None, None], col_rel[None, :, :]]
        s_loc = jnp.einsum('bqhd,bwqjhd->bhqwj', q_r, k_win).astype(F32) * scale
        s_loc = s_loc + bias.transpose(0, 2, 1, 3)[None].astype(F32)
        s_ctx = jnp.einsum('bqhd,bthd->bhqt', q_r, kc).astype(F32) * scale
        sc = jnp.concatenate([s_loc.reshape(b, NA_HEADS, GRID_W, wr * wc), s_ctx], axis=-1)
        p = jax.nn.softmax(sc, axis=-1).astype(vl.dtype)
        p_loc = p[..., :wr * wc].reshape(b, NA_HEADS, GRID_W, wr, wc)
        p_ctx = p[..., wr * wc:]
        return (jnp.einsum('bhqwj,bwqjhd->bqhd', p_loc, v_win)
                + jnp.einsum('bhqt,bthd->bqhd', p_ctx, vc))

    o = lax.map(one_row, (jnp.arange(rows), qg))
    ol = o.transpose(1, 0, 2, 3, 4).reshape(b, s, NA_HEADS * NA_HEAD_DIM) @ w_o
    oc = merge_heads(dense_attention(qc, kc, vc, scale), w_o) if need_ctx else None
    return ol, oc


def ret_project(x, w_qkvg, with_qg):
    b, l, _ = x.shape
    nk, nv = RET_HEADS * RET_QK_DIM, RET_HEADS * RET_V_DIM
    if with_qg:
        p = x @ w_qkvg
        q = p[..., :nk].reshape(b, l, RET_HEADS, RET_QK_DIM)
        g = p[..., 2 * nk + nv:]
        p = p[..., nk:2 * nk + nv]
    else:
        q = g = None
        p = x @ w_qkvg[:, nk:2 * nk + nv]
    k = p[..., :nk].reshape(b, l, RET_HEADS, RET_QK_DIM) * (RET_QK_DIM ** -0.5)
    v = p[..., nk:].reshape(b, l, RET_HEADS, RET_V_DIM)
    return q, k, v, g


def ret_context_state(k, v, log_decay, reverse):
    lc = k.shape[1]
    pos = jnp.arange(lc, dtype=F32)
    age = pos if reverse else (lc - 1 - pos)
    w = jnp.exp(age[:, None] * log_decay.astype(F32)[None, :])
    return jnp.einsum('bthk,bthv->bhkv', k.astype(F32) * w[None, :, :, None], v.astype(F32))


def chunk_retention(q, k, v, log_decay, init_state):
    b, l, h, dk = q.shape
    dv = v.shape[-1]
    n = l // RET_CHUNK
    ld = log_decay.astype(F32)
    i = jnp.arange(RET_CHUNK, dtype=F32)
    dist = i[:, None] - i[None, :]
    intra = jnp.where(dist >= 0, jnp.exp(jnp.maximum(dist, 0.0)[None] * ld[:, None, None]), 0.0)
    q_decay = jnp.exp((i + 1.0)[:, None] * ld[None, :])
    k_decay = jnp.exp((RET_CHUNK - 1.0 - i)[:, None] * ld[None, :])
    chunk_decay = jnp.exp(RET_CHUNK * ld)

    def split_chunks(t):
        return t.reshape(b, n, RET_CHUNK, h, t.shape[-1]).transpose(1, 0, 2, 3, 4)

    def step(state, blk):
        qc, kc, vc = blk
        qf, kf, vf = qc.astype(F32), kc.astype(F32), vc.astype(F32)
        sc = jnp.einsum('bqhk,bthk->bhqt', qf, kf) * intra[None]
        inner = jnp.einsum('bhqt,bthv->bqhv', sc, vf)
        cross = jnp.einsum('bqhk,bhkv->bqhv', qf, state) * q_decay[None, :, :, None]
        new_state = (state * chunk_decay[None, :, None, None]
                     + jnp.einsum('bthk,bthv->bhkv', kf * k_decay[None, :, :, None], vf))
        return new_state, (inner + cross).astype(q.dtype)

    _, out = lax.scan(step, init_state, (split_chunks(q), split_chunks(k), split_chunks(v)))
    return out.transpose(1, 0, 2, 3, 4).reshape(b, l, h, dv)


def ret_output(y, g, out_norm, w_o, dtype):
    b, l, h, dv = y.shape
    mu = jnp.mean(y, axis=-1, keepdims=True)
    var = jnp.mean(jnp.square(y - mu), axis=-1, keepdims=True)
    yn = ((y - mu) * lax.rsqrt(var + EPS)).reshape(b, l, h * dv) * out_norm.astype(F32)
    return (yn * jax.nn.silu(g.astype(F32))).astype(dtype) @ w_o


def ret_mixer(xl, xc, w_qkvg, log_decay_fwd, log_decay_bwd, out_norm, w_o, need_ctx):
    cos, sin = rope_tables(xl.shape[1], RET_QK_DIM)
    ql, kl, vl, gl = ret_project(xl, w_qkvg, True)
    ql, kl = apply_rope(ql, cos, sin), apply_rope(kl, cos, sin)
    qc, kc, vc, gc = ret_project(xc, w_qkvg, need_ctx)
    state_f = ret_context_state(kc, vc, log_decay_fwd, False)
    state_b = ret_context_state(kc, vc, log_decay_bwd, True)
    yf = chunk_retention(ql, kl, vl, log_decay_fwd, state_f)
    yb = jnp.flip(chunk_retention(jnp.flip(ql, 1), jnp.flip(kl, 1), jnp.flip(vl, 1),
                                  log_decay_bwd, state_b), 1)
    ol = ret_output(yf.astype(F32) + yb.astype(F32), gl, out_norm, w_o, xl.dtype)
    oc = None
    if need_ctx:
        lc = xc.shape[1]
        pos = jnp.arange(lc, dtype=F32)
        dist = pos[:, None] - pos[None, :]
        lf = log_decay_fwd.astype(F32)[:, None, None]
        lb = log_decay_bwd.astype(F32)[:, None, None]
        dec = (jnp.where(dist >= 0, jnp.exp(jnp.maximum(dist, 0.0)[None] * lf), 0.0)
               + jnp.where(dist <= 0, jnp.exp(jnp.maximum(-dist, 0.0)[None] * lb), 0.0))
        sc = jnp.einsum('bqhk,bthk->bhqt', qc, kc).astype(F32) * dec[None]
        yc = jnp.einsum('bhqt,bthv->bqhv', sc, vc.astype(F32))
        oc = ret_output(yc, gc, out_norm, w_o, xc.dtype)
    return ol, oc


def swiglu(x, w13, w2):
    a = x @ w13
    return (jax.nn.silu(a[..., :FFN_HIDDEN]) * a[..., FFN_HIDDEN:]) @ w2


def setup_inputs(seed: int = 0) -> dict:
    key = jax.random.key(seed)
    keys = iter(jax.random.split(key, 40))
    D = D_MODEL

    def nrm(shape, scale):
        return scale * jax.random.normal(next(keys), shape, F32)

    def gain(shape):
        return 1.0 + nrm(shape, 0.02)

    base_decay = jnp.log(1.0 - 2.0 ** (-5.0 - jnp.arange(RET_HEADS, dtype=F32)))
    nk, nv = RET_HEADS * RET_QK_DIM, RET_HEADS * RET_V_DIM
    return {
        'x': nrm((BATCH, SEQ, D), 1.0),
        'c': nrm((BATCH, D), 1.0),
        'ctx': nrm((BATCH, CTX_LEN, D), 1.0),
        'c_ctx': nrm((D,), 1.0),
        'mod_w': nrm((DEPTH, D, 6 * D), 0.5 * D ** -0.5),
        'mod_b': nrm((DEPTH, 6 * D), 0.01),
        'norm1': gain((DEPTH, D)),
        'norm2': gain((DEPTH, D)),
        'ffn_w13': nrm((DEPTH, D, 2 * FFN_HIDDEN), D ** -0.5),
        'ffn_w2': nrm((DEPTH, FFN_HIDDEN, D), FFN_HIDDEN ** -0.5),
        'mla_w_down': nrm((N_MLA_LAYERS, D, MLA_Q_RANK + MLA_KV_RANK + MLA_ROPE), D ** -0.5),
        'mla_q_lora_norm': gain((N_MLA_LAYERS, MLA_Q_RANK)),
        'mla_kv_lora_norm': gain((N_MLA_LAYERS, MLA_KV_RANK)),
        'mla_w_uq': nrm((N_MLA_LAYERS, MLA_Q_RANK, MLA_HEADS * MLA_QK), MLA_Q_RANK ** -0.5),
        'mla_w_ukv': nrm((N_MLA_LAYERS, MLA_KV_RANK, MLA_HEADS * (MLA_NOPE + MLA_V)), MLA_KV_RANK ** -0.5),
        'mla_q_norm': gain((N_MLA_LAYERS, MLA_QK)),
        'mla_k_norm': gain((N_MLA_LAYERS, MLA_QK)),
        'mla_w_o': nrm((N_MLA_LAYERS, MLA_HEADS * MLA_V, D), (MLA_HEADS * MLA_V) ** -0.5),
        'gqa_w_qkv': nrm((N_GQA_LAYERS, D, (GQA_HEADS + 2 * GQA_KV_HEADS) * GQA_HEAD_DIM), D ** -0.5),
        'gqa_q_norm': gain((N_GQA_LAYERS, GQA_HEAD_DIM)),
        'gqa_k_norm': gain((N_GQA_LAYERS, GQA_HEAD_DIM)),
        'gqa_w_o': nrm((N_GQA_LAYERS, GQA_HEADS * GQA_HEAD_DIM, D), (GQA_HEADS * GQA_HEAD_DIM) ** -0.5),
        'na_w_qkv': nrm((N_NA_LAYERS, D, 3 * NA_HEADS * NA_HEAD_DIM), D ** -0.5),
        'na_q_norm': gain((N_NA_LAYERS, NA_HEAD_DIM)),
        'na_k_norm': gain((N_NA_LAYERS, NA_HEAD_DIM)),
        'na_rpb': nrm((N_NA_LAYERS, NA_HEADS, 2 * NA_WIN_ROWS - 1, 2 * NA_WIN_COLS - 1), 0.1),
        'na_w_o': nrm((N_NA_LAYERS, NA_HEADS * NA_HEAD_DIM, D), (NA_HEADS * NA_HEAD_DIM) ** -0.5),
        'ret_w_qkvg': nrm((N_RET_LAYERS, D, 2 * nk + 2 * nv), D ** -0.5),
        'ret_log_decay_fwd': base_decay * (1.0 + nrm((N_RET_LAYERS, RET_HEADS), 0.05)),
        'ret_log_decay_bwd': base_decay * (1.0 + nrm((N_RET_LAYERS, RET_HEADS), 0.05)),
        'ret_out_norm': gain((N_RET_LAYERS, nv)),
        'ret_w_o': nrm((N_RET_LAYERS, nv, D), nv ** -0.5),
    }


def reference(x, c, ctx, c_ctx, mod_w, mod_b, norm1, norm2, ffn_w13, ffn_w2,
              mla_w_down, mla_q_lora_norm, mla_kv_lora_norm, mla_w_uq, mla_w_ukv, mla_q_norm, mla_k_norm,
              mla_w_o, gqa_w_qkv, gqa_q_norm, gqa_k_norm, gqa_w_o,
              na_w_qkv, na_q_norm, na_k_norm, na_rpb, na_w_o,
              ret_w_qkvg, ret_log_decay_fwd, ret_log_decay_bwd, ret_out_norm, ret_w_o):
    h, hc = x, ctx
    silu_c = jax.nn.silu(c)
    silu_cc = jax.nn.silu(c_ctx)
    for i in range(DEPTH):
        need_ctx = i < DEPTH - 1
        ml = (silu_c @ mod_w[i] + mod_b[i])[:, None, :]
        mc = silu_cc @ mod_w[i] + mod_b[i]
        sh1, sc1, g1, sh2, sc2, g2 = jnp.split(ml, 6, axis=-1)
        csh1, csc1, cg1, csh2, csc2, cg2 = jnp.split(mc, 6)
        xl = rms_norm(h, norm1[i]) * (1.0 + sc1) + sh1
        xc = rms_norm(hc, norm1[i]) * (1.0 + csc1) + csh1
        kind, j = i % N_MIXERS, i // N_MIXERS
        if kind == 0:
            ol, oc = mla_mixer(xl, xc, mla_w_down[j], mla_q_lora_norm[j], mla_kv_lora_norm[j], mla_w_uq[j],
                               mla_w_ukv[j], mla_q_norm[j], mla_k_norm[j], mla_w_o[j], need_ctx)
        elif kind == 1:
            ol, oc = gqa_mixer(xl, xc, gqa_w_qkv[j], gqa_q_norm[j], gqa_k_norm[j], gqa_w_o[j], need_ctx)
        elif kind == 2:
            ol, oc = na_mixer(xl, xc, na_w_qkv[j], na_q_norm[j], na_k_norm[j], na_rpb[j], na_w_o[j], need_ctx)
        else:
            ol, oc = ret_mixer(xl, xc, ret_w_qkvg[j], ret_log_decay_fwd[j], ret_log_decay_bwd[j],
                               ret_out_norm[j], ret_w_o[j], need_ctx)
        h = h + g1 * ol
        h = h + g2 * swiglu(rms_norm(h, norm2[i]) * (1.0 + sc2) + sh2, ffn_w13[i], ffn_w2[i])
        if need_ctx:
            hc = hc + cg1 * oc
            hc = hc + cg2 * swiglu(rms_norm(hc, norm2[i]) * (1.0 + csc2) + csh2, ffn_w13[i], ffn_w2[i])
    return h
```

```python
import bisect
import ml_dtypes


import numpy as np
import concourse.bass as bass
import concourse.mybir as mybir
from concourse.bass_utils import run_bass_kernel_spmd

F32 = mybir.dt.float32
BF16 = mybir.dt.bfloat16
AF = mybir.ActivationFunctionType
ALU = mybir.AluOpType
AX = mybir.AxisListType

SEM_CHUNK = 30000
N_DMA_SEMS = 12


DTSIZE = {}


def _box(ap):
    t = ap.tensor
    dims = list(ap.ap)
    off = ap.offset
    shape = list(t.shape)
    cls = type(t).__name__
    if cls.startswith('DRam'):
        lo = off
        hi = off
        for st, cnt in dims:
            if st >= 0:
                hi += st * (cnt - 1)
            else:
                lo += st * (cnt - 1)
        return (0, 1, lo, hi + 1)
    row = 1
    for s in shape[1:]:
        row *= s
    if cls.startswith('PSum'):
        return (0, 128, 0, 1 << 20)
    pst, pcnt = dims[0]
    if pst == 0:
        pst = row
    p0 = off // row
    f0 = off % row
    assert pst % row == 0 or pcnt == 1, (pst, row, dims)
    p1 = p0 + (pst // row) * (pcnt - 1) + 1
    f1 = f0
    for st, cnt in dims[1:]:
        assert st >= 0
        f1 += st * (cnt - 1)
    if t.name == 'arena':
        z = mybir.dt.size(ap.dtype)
        return (p0, p1, f0 * z, (f1 + 1) * z)
    return (p0, p1, f0, f1 + 1)


def _ovl(a, b):
    return a[0] < b[1] and b[0] < a[1] and a[2] < b[3] and b[2] < a[3]


def _cov(a, b):
    return a[0] <= b[0] and a[1] >= b[1] and a[2] <= b[2] and a[3] >= b[3]


class Op:
    __slots__ = ('eng', 'idx', 'fn', 'deps', 'is_dma', 'signal', 'dsem', 'dval', 'dprev', 'sigcount')

    def __init__(self, eng, idx, fn, is_dma):
        self.eng = eng
        self.idx = idx
        self.fn = fn
        self.deps = []
        self.is_dma = is_dma
        self.signal = False
        self.dsem = None
        self.dval = 0
        self.dprev = None
        self.sigcount = 0


class Sched:
    ENGS = ('pe', 'act', 'dve', 'pool', 'sp')

    def __init__(self, nc, same_engine_sync=True):
        self.nc = nc
        self.ops = {e: [] for e in self.ENGS}
        self.rec = {}
        self.same = same_engine_sync
        self.wm = {e: {} for e in self.ENGS}
        self.dma_rr = {e: 0 for e in self.ENGS}
        self.dma_last = {}
        self.dma_cnt = {}
        self.out_dmas = []
        self.nalloc = 0
        self.final_names = None
        self.arena_bytes = 0
        self.arena = None
        self.arena_views = {}
        self.bump = 0
        self.phase = 0
        self.allocs = []
        self.banks = None
        self.nbank = 0
        self.ccs = []

    def use_arena(self, nbytes):
        self.arena_bytes = nbytes
        self.arena = self.nc.alloc_sbuf_tensor('arena', [128, nbytes // 2], BF16)
        self.arena_views = {BF16: self.arena, F32: self.arena.bitcast(F32)}
        self.banks = [self.nc.alloc_psum_tensor('bank%d' % i, [128, 512], F32) for i in range(8)]

    def new_phase(self):
        self.barrier()
        self.phase += 1
        self.bump = 0
        self.allocs = []
        self.nbank = 0

    def barrier(self):
        lasts = {}
        for e in self.ENGS:
            for o in reversed(self.ops[e]):
                if (not o.is_dma) and o.fn is not None and o.fn != 'final':
                    lasts[e] = o
                    break
        for e in self.ENGS:
            b = Op(e, len(self.ops[e]), None, False)
            for e2, o in lasts.items():
                if e2 != e:
                    self._add_dep(b, o)
            for key, o in self.dma_last.items():
                self._add_dep(b, o)
            for o in self.ccs:
                self._add_dep(b, o)
            self.ops[e].append(b)

    def sb(self, name, shape, dt):
        self.nalloc += 1
        if self.arena is None:
            return self.nc.alloc_sbuf_tensor('sb_' + name, list(shape), dt)
        z = mybir.dt.size(dt)
        n = 1
        for x in shape[1:]:
            n *= x
        nbytes = (n * z + 63) // 64 * 64
        off = self.bump
        self.bump += nbytes
        assert self.bump <= self.arena_bytes, ('arena overflow', name, self.bump)
        self.allocs.append((off, off + nbytes))
        h = self.arena_views[dt]
        row = self.arena_bytes // z
        dims = [[row, shape[0]]]
        st = n
        for x in shape[1:]:
            st //= x
            dims.append([st, x])
        return bass.AP(h, off // z, dims)

    def ps(self, name, shape, dt=F32):
        if self.banks is None:
            return self.nc.alloc_psum_tensor('ps_' + name, list(shape), dt)
        b = self.banks[self.nbank]
        self.nbank += 1
        assert self.nbank <= 8
        if dt == BF16:
            h = b.bitcast(BF16)
            return bass.AP(h, 0, [[1024, shape[0]], [1, shape[1]]])
        return bass.AP(b, 0, [[512, shape[0]], [1, shape[1]]])

    def _key(self, ap, bx):
        nm = ap.tensor.name
        if nm != 'arena':
            return nm
        i = bisect.bisect_right(self.allocs, (bx[2], 1 << 60)) - 1
        assert i >= 0 and self.allocs[i][0] <= bx[2] and bx[3] <= self.allocs[i][1], (bx, self.allocs[i] if i >= 0 else None)
        return 'a%d_%d' % (self.phase, i)

    def cc(self, kind, groups, in_ap, out_ap):
        o = Op('pool', len(self.ops['pool']), None, True)
        o.dsem = 'cc%d' % len(self.ccs)
        o.dval = 1
        o.fn = ('cc', kind, groups, in_ap, out_ap)
        self.dma_cnt[('pool', o.dsem)] = 1
        self._track(o, [in_ap], [out_ap])
        self.ops['pool'].append(o)
        self.ccs.append(o)
        return o

    def _add_dep(self, op, dep):
        if dep is op:
            return
        if dep.is_dma:
            key = ('d', dep.eng, dep.dsem)
            val = dep.dval
        else:
            if dep.eng == op.eng and not op.is_dma:
                if not self.same:
                    return
                if op.eng == 'pe':
                    return
            key = ('e', dep.eng)
            val = dep.idx
        w = self.wm[op.eng]
        if w.get(key, -1) >= val:
            return
        w[key] = val
        op.deps.append(dep)
        dep.signal = True

    def _track(self, op, aps_r, aps_w):
        for ap in aps_r:
            bx = _box(ap)
            nm = self._key(ap, bx)
            lst = self.rec.setdefault(nm, [])
            isps = type(ap.tensor).__name__.startswith('PSum')
            for r in lst:
                if r[2] and _ovl(r[0], bx):
                    self._add_dep(op, r[1])
                elif isps and (not r[2]) and r[1].eng != op.eng:
                    self._add_dep(op, r[1])
        for ap in aps_w:
            bx = _box(ap)
            nm = self._key(ap, bx)
            lst = self.rec.setdefault(nm, [])
            for r in lst:
                if _ovl(r[0], bx):
                    self._add_dep(op, r[1])
        for ap in aps_r:
            bx = _box(ap)
            nm = self._key(ap, bx)
            lst = self.rec[nm]
            new = []
            for r in lst:
                if (not r[2]) and r[1].eng == op.eng and (not r[1].is_dma) and (not op.is_dma) and _cov(bx, r[0]):
                    continue
                new.append(r)
            new.append([bx, op, False])
            self.rec[nm] = new
        for ap in aps_w:
            bx = _box(ap)
            nm = self._key(ap, bx)
            lst = self.rec[nm]
            new = [r for r in lst if not _cov(bx, r[0])]
            new.append([bx, op, True])
            self.rec[nm] = new

    def op(self, eng, fn, reads=(), writes=()):
        o = Op(eng, len(self.ops[eng]), fn, False)
        self._track(o, list(reads), list(writes))
        self.ops[eng].append(o)
        return o

    def dma(self, eng, out, in_, is_output=False):
        o = Op(eng, len(self.ops[eng]), None, True)
        slot = self.dma_rr[eng] % N_DMA_SEMS
        self.dma_rr[eng] += 1
        o.dsem = slot
        prev = self.dma_last.get((eng, slot))
        o.dprev = prev
        cnt = self.dma_cnt.get((eng, slot), 0) + 1
        self.dma_cnt[(eng, slot)] = cnt
        o.dval = cnt
        self.dma_last[(eng, slot)] = o
        o.fn = (out, in_)
        if prev is not None:
            w = self.wm[eng]
            key = ('d', eng, slot)
            if w.get(key, -1) < prev.dval:
                w[key] = prev.dval
                o.deps.append(prev)
        self._track(o, [in_], [out])
        self.ops[eng].append(o)
        if is_output and (self.final_names is None or out.tensor.name in self.final_names):
            self.out_dmas.append(o)
        return o

    def emit(self):
        nc = self.nc
        engobj = {'pe': nc.tensor, 'act': nc.scalar, 'dve': nc.vector, 'pool': nc.gpsimd, 'sp': nc.sync}
        fin = Op('sp', len(self.ops['sp']), 'final', False)
        for d in self.out_dmas:
            fin.deps.append(d)
        self.ops['sp'].append(fin)
        sems = {}
        nsig = {}
        for e in self.ENGS:
            k = 0
            for o in self.ops[e]:
                if (not o.is_dma) and o.signal:
                    k += 1
                    o.sigcount = k
            nsig[e] = k
            sems[e] = [nc.alloc_semaphore(f'sem_{e}_{i}') for i in range((k + SEM_CHUNK - 1) // SEM_CHUNK)]
        dsems = {}
        for (e, slot) in self.dma_cnt:
            dsems[(e, slot)] = nc.alloc_semaphore(f'dsem_{e}_{slot}')
        self.stats = {e: (len(self.ops[e]), nsig[e]) for e in self.ENGS}

        def emit_engine(e, eng):
            for o in self.ops[e]:
                for d in o.deps:
                    if d.is_dma:
                        mult = 1 if isinstance(d.dsem, str) else 16
                        eng.wait_ge(dsems[(d.eng, d.dsem)], mult * d.dval)
                    else:
                        k = d.sigcount - 1
                        eng.wait_ge(sems[d.eng][k // SEM_CHUNK], k % SEM_CHUNK + 1)
                if o.fn == 'final' or o.fn is None:
                    continue
                if o.is_dma and isinstance(o.fn[0], str):
                    _, kind, groups, in_ap, out_ap = o.fn
                    eng.collective_compute(kind, ALU.bypass, groups, [in_ap], [out_ap]).then_inc(dsems[(e, o.dsem)])
                    continue
                if o.is_dma:
                    out, in_ = o.fn
                    eng.dma_start(out=out, in_=in_, allow_slow_non_contiguous=True).then_inc(dsems[(e, o.dsem)], 16)
                else:
                    ins = o.fn(eng)
                    if o.signal:
                        k = o.sigcount - 1
                        ins.then_inc(sems[e][k // SEM_CHUNK], 1)

        with nc.Block() as block:
            @block.tensor
            def _(eng):
                emit_engine('pe', eng)

            @block.scalar
            def _(eng):
                emit_engine('act', eng)

            @block.vector
            def _(eng):
                emit_engine('dve', eng)

            @block.gpsimd
            def _(eng):
                emit_engine('pool', eng)

            @block.sync
            def _(eng):
                emit_engine('sp', eng)


def V(t, off, *dims, p0=0, npart=128):
    if isinstance(t, bass.AP):
        row = t.ap[0][0]
        return bass.AP(t.tensor, t.offset + p0 * row + off, [[row, npart]] + [list(d) for d in dims])
    row = 1
    for s in list(t.shape)[1:]:
        row *= s
    return bass.AP(t, p0 * row + off, [[row, npart]] + [list(d) for d in dims])


def o_tt(S, eng, out, in0, in1, op):
    return S.op(eng, lambda e: e.tensor_tensor(out=out, in0=in0, in1=in1, op=op), [in0, in1], [out])


def o_ts(S, eng, out, in0, s1, s2, op0, op1=None):
    rd = [in0] + [x for x in (s1, s2) if isinstance(x, bass.AP)]
    if op1 is None:
        return S.op(eng, lambda e: e.tensor_scalar(out=out, in0=in0, scalar1=s1, scalar2=None, op0=op0), rd, [out])
    return S.op(eng, lambda e: e.tensor_scalar(out=out, in0=in0, scalar1=s1, scalar2=s2, op0=op0, op1=op1), rd, [out])


def o_stt(S, eng, out, in0, scalar, in1, op0, op1):
    rd = [in0, in1] + ([scalar] if isinstance(scalar, bass.AP) else [])
    return S.op(eng, lambda e: e.scalar_tensor_tensor(out=out, in0=in0, scalar=scalar, in1=in1, op0=op0, op1=op1), rd, [out])


def o_act(S, out, in_, func, bias=None, scale=1.0, accum=None):
    rd = [in_] + [x for x in (bias, scale) if isinstance(x, bass.AP)]
    wr = [out] + ([accum] if accum is not None else [])
    kw = {}
    if bias is not None:
        kw['bias'] = bias
    if accum is not None:
        kw['accum_out'] = accum
    return S.op('act', lambda e: e.activation(out=out, in_=in_, func=func, scale=scale, **kw), rd, wr)


def o_copy(S, eng, out, in_):
    if eng == 'act':
        return S.op('act', lambda e: e.copy(out=out, in_=in_), [in_], [out])
    return S.op(eng, lambda e: e.tensor_copy(out=out, in_=in_), [in_], [out])


def o_red(S, eng, out, in_, op=None):
    op = op or ALU.add
    return S.op(eng, lambda e: e.tensor_reduce(out=out, in_=in_, axis=AX.X, op=op), [in_], [out])


def o_recip(S, out, in_):
    return S.op('dve', lambda e: e.reciprocal(out=out, in_=in_), [in_], [out])


def o_memset(S, eng, out, val):
    return S.op(eng, lambda e: e.memset(out, val), [], [out])


def o_mm(S, out, lhsT, rhs, start, stop):
    return S.op('pe', lambda e: e.matmul(out, lhsT=lhsT, rhs=rhs, start=start, stop=stop), [lhsT, rhs], [out])


def o_tr(S, out, in_, ident):
    return S.op('pe', lambda e: e.transpose(out=out, in_=in_, identity=ident), [in_, ident], [out])


class Ctx:
    def __init__(self, S, D, npT=2):
        self.S = S
        self.identf = S.sb('identf', [128, 128], F32)
        self.ident = S.sb('identb', [128, 128], BF16)
        self.eps = S.sb('eps', [128, 1], F32)
        self.ones = S.sb('ones', [128, 128], BF16)
        self.junk = S.sb('junk', [128, 1024], F32)
        S.dma('sp', self.identf[:, :], D['ident'])
        o_copy(S, 'dve', self.ident[:, :], self.identf[:, :])
        o_memset(S, 'dve', self.eps[:, :], 1e-6)
        o_memset(S, 'dve', self.ones[:, :], 1.0)
        self.pT = [S.ps('pT%d' % i, [128, 1024], BF16) for i in range(npT)]
        self.nT = 0
        self.evac_rr = 0

    def next_pT(self):
        t = self.pT[self.nT % 2]
        self.nT += 1
        return t

    def evac_eng(self):
        self.evac_rr += 1
        return 'act' if self.evac_rr % 2 else 'dve'


def rstd_rows(S, C, ss, out, n, inv_d):
    o_act(S, out, ss, AF.Sqrt, bias=C.eps[:, 0:1], scale=inv_d)
    o_recip(S, out, out)


def norm_T(S, C, hsrc, xT_dst, Gp, Sh, scr):
    ss = scr['ss']
    rs = scr['rs']
    xs = scr['xs']
    o_act(S, C.junk[:, :], hsrc, AF.Square, scale=1.0 / 32.0, accum=ss[:, 0:1])
    rstd_rows(S, C, ss[:, 0:1], rs[:, 0:1], 1, 1.0)
    o_act(S, xs[:, :], hsrc, AF.Copy, scale=rs[:, 0:1])
    pT = C.next_pT()
    for k in range(8):
        o_tr(S, pT[:, k * 128:(k + 1) * 128], xs[:, k * 128:(k + 1) * 128], C.ident[:, :])
    tmp = scr['xtmp']
    gb = bass.AP(Gp.tensor, Gp.offset, [list(Gp.ap[0]), [1, 8], [0, 128]])
    sb_ = bass.AP(Sh.tensor, Sh.offset, [list(Sh.ap[0]), [1, 8], [0, 128]])
    pv = V(pT, 0, (128, 8), (1, 128))
    tv = V(tmp, 0, (128, 8), (1, 128))
    o_tt(S, 'dve', tv, pv, gb, ALU.mult)
    o_tt(S, 'pool', xT_dst, tv, sb_, ALU.add)


def load_mod(S, D, layer, scr_name, which, norm_ap):
    t = S.sb(scr_name, [128, 2, 2, 8], F32)
    raw = S.sb(scr_name + '_raw', [128, 2, 2, 8], F32)
    nrm = S.sb(scr_name + '_n', [128, 8], F32)
    modv = D['modv']
    S.dma('sp', nrm[:, :], norm_ap)
    base = which * 3 * 1024
    for s in range(2):
        S.dma('sp', raw[:, s, 1, :], modv[layer, s, base:base + 1024].rearrange('(k p) -> p k', p=128))
        S.dma('sp', raw[:, s, 0, :], modv[layer, s, base + 1024:base + 2048].rearrange('(k p) -> p k', p=128))
        o_stt(S, 'dve', t[:, s, 0, :], raw[:, s, 0, :], 1.0, nrm[:, :], ALU.add, ALU.mult)
        o_copy(S, 'dve', t[:, s, 1, :], raw[:, s, 1, :])
    return t


def load_gate_rep(S, D, layer, name, which):
    t = S.sb(name, [128, 2, 1024], F32)
    modv = D['modv']
    base = (which * 3 + 2) * 1024
    for s in range(2):
        S.dma('sp', t[:, s, :], modv[layer, s, base:base + 1024].partition_broadcast(128))
    return t


def build_M(S, D, ncol=768, nrow=3):
    C = Ctx(S, D)
    cin = S.sb('cin', [128, 8, nrow], F32)
    sc = S.sb('sc', [128, 8, nrow], BF16)
    S.dma('sp', cin[:, :, :], D['cin'])
    scf = S.sb('scf', [128, 8, nrow], F32)
    o_act(S, scf[:, :, :], cin[:, :, :], AF.Silu)
    o_copy(S, 'dve', sc[:, :, :], scf[:, :, :])
    bias = S.sb('mbias', [nrow, 4, ncol], F32)
    res = S.sb('mres', [nrow, 4, ncol], F32)
    for s in range(nrow):
        S.dma('sp', bias[s:s + 1, :, :], D['mod_b'].partition_broadcast(1))
    wt = [S.sb('mw%d' % i, [128, 8, ncol], BF16) for i in range(2)]
    pM = [S.ps('pM%d' % i, [128, 512], F32) for i in range(2)]
    n = 0
    for i in range(4):
        w = wt[i % 2]
        S.dma('pool', w[:, :, :], D['mod_w'][i].rearrange('(k p) m -> p k m', p=128))
        c0 = 0
        while c0 < ncol:
            wd = min(512, ncol - c0)
            p = pM[n % 2]
            n += 1
            for k in range(8):
                o_mm(S, p[0:nrow, 0:wd], sc[:, k, :], w[:, k, c0:c0 + wd], k == 0, k == 7)
            o_tt(S, 'dve', res[0:nrow, i, c0:c0 + wd], p[0:nrow, 0:wd], bias[0:nrow, i, c0:c0 + wd], ALU.add)
            c0 += wd
    S.dma('sp', D['modv'], res[0:nrow, :, :], is_output=True)


def build_C(S, D, layer, ntiles=18, nctx=2):
    C = Ctx(S, D)
    md = load_mod(S, D, layer, 'md2', 1, D['norm2'])
    gate = load_gate_rep(S, D, layer, 'g2rep', 1)
    w13s = [S.sb('w13_%d' % i, [128, 8, 2, 1408], BF16) for i in range(2)]
    w2s = [S.sb('w2_%d' % i, [128, 11, 1024], BF16) for i in range(2)]
    scr = dict(ss=S.sb('ss', [128, 1], F32), rs=S.sb('rs', [128, 1], F32), xs=S.sb('xs', [128, 1024], BF16),
               xtmp=S.sb('xtmp', [128, 1024], F32))
    ht = [S.sb('ht%d' % i, [128, 1024], F32) for i in range(2)]
    hacc = [S.sb('hacc%d' % i, [128, 1024], F32) for i in range(1)]
    xT = S.sb('xT', [128, 8, 512], BF16)
    gT = S.sb('gT', [128, 11, 512], BF16)
    s1 = [S.sb('s1_%d' % i, [128, 512], F32) for i in range(2)]
    otmp = [S.sb('otmp%d' % i, [128, 512], F32) for i in range(2)]
    ho = [S.sb('ho%d' % i, [128, 1024], F32) for i in range(1)]
    pA = [S.ps('pA%d' % i, [128, 512], F32) for i in range(4)]
    pO = [S.ps('pO%d' % i, [128, 512], F32) for i in range(2)]
    hin = D['hin']
    hout = D['hout']
    nblk = (ntiles + 3) // 4
    na = 0
    no = 0
    nh = 0
    for ps_ in range(2):
        for k in range(8):
            for u in range(2):
                S.dma('pool', w13s[ps_][:, k, u, :], D['w13'][k * 128:(k + 1) * 128, u * 2816 + ps_ * 1408:u * 2816 + (ps_ + 1) * 1408])
        S.dma('pool', w2s[ps_][:, :, :], D['w2'][ps_ * 1408:(ps_ + 1) * 1408, :].rearrange('(f p) m -> p f m', p=128))
    for ps_ in range(2):
        w13 = w13s[ps_]
        w2 = w2s[ps_]
        for b in range(nblk):
            t0 = b * 4
            nt = min(4, ntiles - t0)
            ntok = nt * 128
            for j in range(nt):
                s = 1 if (t0 + j) < nctx else 0
                h = ht[nh % 2]
                nh += 1
                S.dma('sp', h[:, :], hin[(t0 + j) * 128:(t0 + j + 1) * 128, :])
                norm_T(S, C, h[:, :], V(xT, j * 128, (512, 8), (1, 128)), md[:, s, 0, :], md[:, s, 1, :], scr)
            for f in range(11):
                p1 = pA[na % 4]
                p3 = pA[(na + 1) % 4]
                na += 2
                for k in range(8):
                    o_mm(S, p1[:, 0:ntok], w13[:, k, 0, f * 128:(f + 1) * 128], xT[:, k, 0:ntok], k == 0, k == 7)
                for k in range(8):
                    o_mm(S, p3[:, 0:ntok], w13[:, k, 1, f * 128:(f + 1) * 128], xT[:, k, 0:ntok], k == 0, k == 7)
                st = s1[f % 2]
                o_act(S, st[:, 0:ntok], p1[:, 0:ntok], AF.Silu)
                o_tt(S, 'dve', gT[:, f, 0:ntok], p3[:, 0:ntok], st[:, 0:ntok], ALU.mult)
            for j in range(nt):
                s = 1 if (t0 + j) < nctx else 0
                ha = hacc[0]
                hob = ho[0]
                src = hin if ps_ == 0 else hout
                S.dma('sp', ha[:, :], src[(t0 + j) * 128:(t0 + j + 1) * 128, :])
                for hf in range(2):
                    p = pO[(2 * no + hf) % 2]
                    ot = otmp[(2 * no + hf) % 2]
                    for f in range(11):
                        o_mm(S, p[:, :], gT[:, f, j * 128:(j + 1) * 128], w2[:, f, hf * 512:(hf + 1) * 512], f == 0, f == 10)
                    o_tt(S, 'dve', ot[:, :], p[:, :], gate[:, s, hf * 512:(hf + 1) * 512], ALU.mult)
                    o_tt(S, 'pool', hob[:, hf * 512:(hf + 1) * 512], ot[:, :], ha[:, hf * 512:(hf + 1) * 512], ALU.add)
                no += 1
                S.dma('sp', hout[(t0 + j) * 128:(t0 + j + 1) * 128, :], hob[:, :], is_output=True)


def proj_tm(S, C, pP, np_, xTt, W, ncols, Psb):
    c0 = 0
    while c0 < ncols:
        w = min(512, ncols - c0)
        p = pP[np_[0] % len(pP)]
        np_[0] += 1
        for k in range(8):
            o_mm(S, p[:, 0:w], xTt[:, k, :], W[:, k, c0:c0 + w], k == 0, k == 7)
        o_copy(S, C.evac_eng(), Psb[:, c0:c0 + w], p[:, 0:w])
        c0 += w


def head_norm(S, C, src, nh, d, gain_rep, dst, scr):
    (st, so) = src
    (dt_, do) = dst
    sq = scr['sq']
    ss = scr['ssh']
    rs = scr['rsh']
    sv = V(st, so, (d, nh), (1, d))
    qv = V(sq, 0, (d, nh), (1, d))
    o_tt(S, 'pool', qv, sv, sv, ALU.mult)
    o_red(S, 'dve', ss[:, 0:nh], qv)
    rstd_rows(S, C, ss[:, 0:nh], rs[:, 0:nh], nh, 1.0 / d)
    rb = V(rs, 0, (1, nh), (0, d))
    o_tt(S, 'dve', qv, sv, rb, ALU.mult)
    gv = V(gain_rep, 0, (d, nh), (1, d))
    dv_ = V(dt_, do, (d, nh), (1, d))
    o_tt(S, 'pool', dv_, qv, gv, ALU.mult)


def rope_tm(S, src, nh, d, cos, sin, dst, scr):
    (st, so) = src
    (dt_, do) = dst
    hd = d // 2
    x1 = V(st, so, (d, nh), (1, hd))
    x2 = V(st, so + hd, (d, nh), (1, hd))
    o1 = V(dt_, do, (d, nh), (1, hd))
    o2 = V(dt_, do + hd, (d, nh), (1, hd))
    cb = bass.AP(cos.tensor, cos.offset, [list(cos.ap[0]), [0, nh], [1, hd]])
    sb_ = bass.AP(sin.tensor, sin.offset, [list(sin.ap[0]), [0, nh], [1, hd]])
    ta = V(scr['ra'], 0, (hd, nh), (1, hd))
    tb = V(scr['rb'], 0, (hd, nh), (1, hd))
    tc = V(scr['rc'], 0, (hd, nh), (1, hd))
    td = V(scr['rd'], 0, (hd, nh), (1, hd))
    o_tt(S, 'dve', ta, x1, cb, ALU.mult)
    o_tt(S, 'pool', tb, x2, sb_, ALU.mult)
    o_tt(S, 'dve', o1, ta, tb, ALU.subtract)
    o_tt(S, 'pool', tc, x1, sb_, ALU.mult)
    o_tt(S, 'dve', td, x2, cb, ALU.mult)
    o_tt(S, 'pool', o2, tc, td, ALU.add)


def transposes_to(S, C, src_bf, ncolblk, dsts):
    i = 0
    while i < ncolblk:
        n = min(8, ncolblk - i)
        pT = C.next_pT()
        for j in range(n):
            o_tr(S, pT[:, j * 128:(j + 1) * 128], src_bf[:, (i + j) * 128:(i + j + 1) * 128], C.ident[:, :])
        for j in range(n):
            o_copy(S, C.evac_eng(), dsts[i + j], pT[:, j * 128:(j + 1) * 128])
        i += n


def build_A_gqa(S, D, layer, ntiles=18):
    C = Ctx(S, D)
    md = load_mod(S, D, layer, 'md1', 0, D['norm1'])
    W = S.sb('wqkv', [128, 8, 1536], BF16)
    for k in range(8):
        S.dma('pool', W[:, k, :], D['wqkv'][k * 128:(k + 1) * 128, :])
    gq = S.sb('gq', [128, 1280], F32)
    S.dma('sp', gq[:, :], D['gq'])
    ct = S.sb('cost', [128, ntiles - 2, 64], F32)
    sn = S.sb('sint', [128, ntiles - 2, 64], F32)
    S.dma('sp', ct[:, :, :], D['cos'])
    S.dma('sp', sn[:, :, :], D['sin'])
    scr = dict(ss=S.sb('ss', [128, 1], F32), rs=S.sb('rs', [128, 1], F32), xs=S.sb('xs', [128, 1024], BF16),
               xtmp=S.sb('xtmp', [128, 1024], F32), sq=S.sb('sq', [128, 1280], F32), ssh=S.sb('ssh', [128, 16], F32),
               rsh=S.sb('rsh', [128, 16], F32), ra=S.sb('ra', [128, 640], F32), rb=S.sb('rb', [128, 640], F32),
               rc=S.sb('rc', [128, 640], F32), rd=S.sb('rd', [128, 640], F32))
    ht = [S.sb('ht%d' % i, [128, 1024], F32) for i in range(2)]
    xTt = S.sb('xTt', [128, 8, 128], BF16)
    Psb = S.sb('Psb', [128, 1536], F32)
    qn = S.sb('qn', [128, 1280], F32)
    qb = S.sb('qb', [128, 1280], BF16)
    QTs = S.sb('QTs', [128, 8, ntiles * 128], BF16)
    KTs = S.sb('KTs', [128, 2, ntiles * 128], BF16)
    Vs = S.sb('Vs', [128, ntiles, 256], BF16)
    pP = [S.ps('pP%d' % i, [128, 512], F32) for i in range(3)]
    np_ = [0]
    Psb2 = [Psb, S.sb('Psb_b', [128, 1536], F32)]

    def front(t):
        s = 1 if t < 2 else 0
        h = ht[t % 2]
        S.dma('sp', h[:, :], D['hin'][t * 128:(t + 1) * 128, :])
        norm_T(S, C, h[:, :], V(xTt, 0, (128, 8), (1, 128)), md[:, s, 0, :], md[:, s, 1, :], scr)
        proj_tm(S, C, pP, np_, xTt, W, 1536, Psb2[t % 2])

    def back(t):
        Psb = Psb2[t % 2]
        head_norm(S, C, (Psb, 0), 10, 128, gq, (qn, 0), scr)
        if t >= 2:
            rope_tm(S, (qn, 0), 10, 128, ct[:, t - 2, :], sn[:, t - 2, :], (qb, 0), scr)
        else:
            o_copy(S, 'dve', qb[:, :], qn[:, :])
        dsts = [QTs[:, hh, t * 128:(t + 1) * 128] for hh in range(8)] + [KTs[:, hh, t * 128:(t + 1) * 128] for hh in range(2)]
        transposes_to(S, C, qb, 10, dsts)
        o_copy(S, 'act', Vs[:, t, :], Psb[:, 1280:1536])

    front(0)
    for t in range(ntiles):
        if t + 1 < ntiles:
            front(t + 1)
        back(t)
    S.dma('sp', D['QT'].rearrange('h p n -> p h n'), QTs[:, :, :], is_output=True)
    S.dma('sp', D['KT'].rearrange('h p n -> p h n'), KTs[:, :, :], is_output=True)
    S.dma('sp', D['Vt'].rearrange('(t p) m -> p t m', p=128), Vs[:, :, :], is_output=True)


class AttnBufs:
    def __init__(self, S, nS=2):
        self.pS = [S.ps('pS%d' % i, [128, 512], F32) for i in range(nS)]
        self.pO = [S.ps('pO%d' % i, [128, 512], F32) for i in range(2)]
        self.pM = [S.ps('pSm%d' % i, [128, 512], F32) for i in range(2)]
        self.pt = [S.sb('pt%d' % i, [128, 512], BF16) for i in range(4)]
        self.rc = [S.sb('rcp%d' % i, [128, 512], F32) for i in range(2)]
        self.acc = [S.sb('sacc%d' % i, [128, 512], F32) for i in range(2)]
        self.hi = S.sb('shi', [128, 512], BF16)
        self.lo = S.sb('slo', [128, 512], BF16)
        self.ns = 0
        self.no = 0


def attn_head_block(S, C, A, nq, chunks, out_ap, dv, scale, orow=None):
    pO = A.pO[A.no % 2]
    pM = A.pM[A.no % 2]
    rc = A.rc[A.no % 2]
    A.no += 1
    n = len(chunks)
    nS = len(A.pS)
    PD = nS - 1
    base = A.ns
    A.ns += n

    def scores(ci):
        pieces, v_ap, nk = chunks[ci]
        pS = A.pS[(base + ci) % nS]
        for pi, (l, r) in enumerate(pieces):
            o_mm(S, pS[0:nk, 0:nq], l, r, pi == 0, pi == len(pieces) - 1)

    for ci in range(min(PD, n)):
        scores(ci)
    acc = A.acc[(A.no - 1) % 2]
    for ci, (pieces, v_ap, nk) in enumerate(chunks):
        if ci + PD < n:
            scores(ci + PD)
        pS = A.pS[(base + ci) % nS]
        pt = A.pt[(base + ci) % len(A.pt)]
        o_act(S, pt[0:nk, 0:nq], pS[0:nk, 0:nq], AF.Exp, scale=scale)
        o_mm(S, pO[0:dv, 0:nq], v_ap, pt[0:nk, 0:nq], ci == 0, ci == n - 1)
        if ci % 2 == 0:
            o_mm(S, pM[0:dv, 0:nq], C.ones[0:nk, 0:dv], pt[0:nk, 0:nq], ci == 0, False)
        elif ci == 1:
            o_copy(S, 'dve', acc[0:nk, 0:nq], pt[0:nk, 0:nq])
        else:
            o_tt(S, 'dve', acc[0:nk, 0:nq], acc[0:nk, 0:nq], pt[0:nk, 0:nq], ALU.add)
    nk0 = chunks[0][2]
    hi = A.hi
    lo = A.lo
    o_copy(S, 'dve', hi[0:nk0, 0:nq], acc[0:nk0, 0:nq])
    o_tt(S, 'dve', lo[0:nk0, 0:nq], acc[0:nk0, 0:nq], hi[0:nk0, 0:nq], ALU.subtract)
    o_mm(S, pM[0:dv, 0:nq], C.ones[0:nk0, 0:dv], hi[0:nk0, 0:nq], False, False)
    o_mm(S, pM[0:dv, 0:nq], C.ones[0:nk0, 0:dv], lo[0:nk0, 0:nq], False, True)
    r0, r1 = orow if orow is not None else (0, dv)
    o_recip(S, rc[r0:r1, 0:nq], pM[r0:r1, 0:nq])
    o_tt(S, 'dve', out_ap, pO[r0:r1, 0:nq], rc[r0:r1, 0:nq], ALU.mult)


def out_proj_residual(S, C, A, D, OT, nk, wo, gate, ntiles, hin, hout, kslice=None):
    ht = [S.sb('oht%d' % i, [128, 1024], F32) for i in range(2)]
    ho = [S.sb('oho%d' % i, [128, 1024], F32) for i in range(2)]
    ot = [S.sb('oot%d' % i, [128, 512], F32) for i in range(2)]
    n = 0
    for t in range(ntiles):
        s = 1 if t < 2 else 0
        h = ht[t % 2]
        hb = ho[t % 2]
        S.dma('sp', h[:, :], hin[t * 128:(t + 1) * 128, :])
        for hf in range(2):
            p = A.pS[n % 2]
            o_ = ot[n % 2]
            n += 1
            for k in range(nk):
                lhs = OT[:, k, t * 128:(t + 1) * 128] if kslice is None else kslice(k, t)
                o_mm(S, p[:, :], lhs, wo[:, k, hf * 512:(hf + 1) * 512], k == 0, k == nk - 1)
            o_tt(S, 'dve', o_[:, :], p[:, :], gate[:, s, hf * 512:(hf + 1) * 512], ALU.mult)
            o_tt(S, 'pool', hb[:, hf * 512:(hf + 1) * 512], o_[:, :], h[:, hf * 512:(hf + 1) * 512], ALU.add)
        S.dma('sp', hout[t * 128:(t + 1) * 128, :], hb[:, :], is_output=True)


def build_B_gqa(S, D, layer, ntiles=18, nkeys=8448):
    C = Ctx(S, D)
    A = AttnBufs(S)
    gate = load_gate_rep(S, D, layer, 'g1rep', 0)
    ntok = ntiles * 128
    nch = nkeys // 128
    QT = S.sb('QT', [128, 8, ntok], BF16)
    KT = S.sb('KT', [128, 2, nkeys], BF16)
    Vf = S.sb('Vf', [128, nch, 256], BF16)
    OT = S.sb('OT', [128, 8, ntok], BF16)
    wo = S.sb('wo', [128, 8, 1024], BF16)
    S.dma('sp', QT[:, :, :], D['QT'].rearrange('h p n -> p h n'))
    S.dma('sp', KT[:, :, :], D['KTf'].rearrange('h p n -> p h n'))
    S.dma('sp', Vf[:, :, :], D['Vf'].rearrange('(c p) m -> p c m', p=128))
    S.dma('pool', wo[:, :, :], D['wo'].rearrange('(k p) m -> p k m', p=128))
    scale = 128.0 ** -0.5
    for h in range(8):
        kv = h // 4
        chunks = [([(KT[:, kv, c * 128:(c + 1) * 128], QT[:, h, 0:256])], Vf[:, c, kv * 128:(kv + 1) * 128], 128) for c in range(2)]
        attn_head_block(S, C, A, 256, chunks, OT[:, h, 0:256], 128, scale)
        nqb = (ntok - 256) // 512
        for qb in range(nqb):
            q0 = 256 + qb * 512
            chunks = [([(KT[:, kv, c * 128:(c + 1) * 128], QT[:, h, q0:q0 + 512])], Vf[:, c, kv * 128:(kv + 1) * 128], 128) for c in range(nch)]
            attn_head_block(S, C, A, 512, chunks, OT[:, h, q0:q0 + 512], 128, scale)
    out_proj_residual(S, C, A, D, OT, 8, wo, gate, ntiles, D['hin'], D['hmid'])


def build_A_na(S, D, layer, ntiles=18):
    C = Ctx(S, D)
    md = load_mod(S, D, layer, 'md1', 0, D['norm1'])
    W = S.sb('wqkv', [128, 8, 3072], BF16)
    for k in range(8):
        S.dma('pool', W[:, k, :], D['wqkv'][k * 128:(k + 1) * 128, :])
    gq = S.sb('gq', [128, 2048], F32)
    S.dma('sp', gq[:, :], D['gq'])
    o_ts(S, 'dve', gq[:, 0:1024], gq[:, 0:1024], 0.125, None, ALU.mult)
    scr = dict(ss=S.sb('ss', [128, 1], F32), rs=S.sb('rs', [128, 1], F32), xs=S.sb('xs', [128, 1024], BF16),
               xtmp=S.sb('xtmp', [128, 1024], F32), sq=S.sb('sq', [128, 2048], F32), ssh=S.sb('ssh', [128, 32], F32),
               rsh=S.sb('rsh', [128, 32], F32))
    ht = [S.sb('ht%d' % i, [128, 1024], F32) for i in range(2)]
    xTt = S.sb('xTt', [128, 8, 128], BF16)
    Psb = S.sb('Psb', [128, 3072], F32)
    qn = S.sb('qn', [128, 2048], F32)
    qb = S.sb('qb', [128, 2048], BF16)
    QTs = S.sb('QTs', [128, 8, ntiles * 128], BF16)
    KTs = S.sb('KTs', [128, 8, ntiles * 128], BF16)
    Vs = S.sb('Vs', [128, 2, 1024], BF16)
    pP = [S.ps('pP%d' % i, [128, 512], F32) for i in range(3)]
    np_ = [0]
    Psb2 = [Psb, S.sb('Psb_b', [128, 3072], F32)]

    def front(t):
        s = 1 if t < 2 else 0
        h = ht[t % 2]
        S.dma('sp', h[:, :], D['hin'][t * 128:(t + 1) * 128, :])
        norm_T(S, C, h[:, :], V(xTt, 0, (128, 8), (1, 128)), md[:, s, 0, :], md[:, s, 1, :], scr)
        proj_tm(S, C, pP, np_, xTt, W, 3072, Psb2[t % 2])

    def back(t):
        Psb = Psb2[t % 2]
        head_norm(S, C, (Psb, 0), 32, 64, gq, (qn, 0), scr)
        o_copy(S, 'dve', qb[:, :], qn[:, :])
        dsts = [QTs[:, hh, t * 128:(t + 1) * 128] for hh in range(8)] + [KTs[:, hh, t * 128:(t + 1) * 128] for hh in range(8)]
        transposes_to(S, C, qb, 16, dsts)
        o_copy(S, 'act', Vs[:, t % 2, :], Psb[:, 2048:3072])
        S.dma('sp', D['Vt'][t * 128:(t + 1) * 128, :], Vs[:, t % 2, :], is_output=True)
        if 'Vh' in D:
            if t in (2, 3):
                S.dma('sp', D['Vh'][(t - 2) * 128:(t - 1) * 128, :], Vs[:, t % 2, :])
            if t in (ntiles - 2, ntiles - 1):
                S.dma('sp', D['Vh'][256 + (t - ntiles + 2) * 128:256 + (t - ntiles + 3) * 128, :], Vs[:, t % 2, :])

    front(0)
    for t in range(ntiles):
        if t + 1 < ntiles:
            front(t + 1)
        back(t)
    S.dma('sp', D['QT'].rearrange('h p n -> p h n'), QTs[:, :, :], is_output=True)
    S.dma('sp', D['KT'].rearrange('h p n -> p h n'), KTs[:, :, :], is_output=True)
    if 'KTh' in D:
        S.dma('sp', D['KTh'][:, :, 0:256].rearrange('h p n -> p h n'), KTs[:, :, 256:512])
        S.dma('sp', D['KTh'][:, :, 256:512].rearrange('h p n -> p h n'), KTs[:, :, ntiles * 128 - 256:ntiles * 128])


def build_B_na(S, D, layer, ntiles=18):
    C = Ctx(S, D)
    A = AttnBufs(S)
    gate = load_gate_rep(S, D, layer, 'g1rep', 0)
    ntok = ntiles * 128
    nqb = (ntok - 256) // 512
    QT = S.sb('QT', [128, 8, ntok], BF16)
    KTp = [S.sb('KTp%d' % i, [128, 2816], BF16) for i in range(2)]
    Vp = [S.sb('Vp%d' % i, [128, 22, 128], BF16) for i in range(2)]
    wo = S.sb('wo', [128, 8, 1024], BF16)
    TT = [S.sb('TT%d' % i, [128, 2, 1408], BF16) for i in range(2)]
    mK = S.sb('mK', [2, 128], BF16)
    mQs = [S.sb('mQ%d' % i, [2, 8 * 512], BF16) for i in range(2)]
    S.dma('sp', QT[:, :, :], D['QT'].rearrange('h p n -> p h n'))
    S.dma('pool', wo[:, :, :], D['wo'].rearrange('(k p) m -> p k m', p=128))
    S.dma('sp', mK[:, :], D['maskK'])
    nm = 0
    for pair in range(8):
        tt = TT[pair % 2]
        KT = KTp[pair % 2]
        Vf = Vp[pair % 2]
        S.dma('sp', tt[:, :, :], D['TT'][2 * pair:2 * pair + 2].rearrange('h p n -> p h n'))
        S.dma('sp', KT[:, 0:256], D['KTc'][pair])
        S.dma('sp', KT[:, 256:2816], D['KTl'][pair])
        S.dma('sp', Vf[:, 0:2, :], D['Vc'][:, pair * 128:(pair + 1) * 128].rearrange('(c p) m -> p c m', p=128))
        S.dma('sp', Vf[:, 2:22, :], D['Vl'][:, pair * 128:(pair + 1) * 128].rearrange('(c p) m -> p c m', p=128))
        for hh in range(2):
            pb = hh * 64
            chunks = [([(KT[pb:pb + 64, c * 128:(c + 1) * 128], QT[pb:pb + 64, pair, 0:256])],
                       Vf[:, c, :], 128) for c in range(2)]
            attn_head_block(S, C, A, 256, chunks, QT[pb:pb + 64, pair, 0:256], 128, 1.0, orow=(pb, pb + 64))
            for qb in range(nqb):
                q0 = 256 + qb * 512
                mQ = mQs[nm % 2]
                nm += 1
                S.dma('sp', mQ[:, :], D['maskQ'][:, qb * 4096:(qb + 1) * 4096])
                chunks = []
                for c in range(2):
                    chunks.append(([(KT[pb:pb + 64, c * 128:(c + 1) * 128], QT[pb:pb + 64, pair, q0:q0 + 512])],
                                   Vf[:, c, :], 128))
                for c in range(8):
                    m = 4 * qb + c
                    i0 = 14 - 2 * c
                    pieces = [(KT[pb:pb + 64, 256 + m * 128:256 + (m + 1) * 128], QT[pb:pb + 64, pair, q0:q0 + 512]),
                              (C.ident[:, :], tt[:, hh, i0 * 64:i0 * 64 + 512]),
                              (mK[0:2, :], mQ[0:2, c * 512:(c + 1) * 512])]
                    chunks.append((pieces, Vf[:, 2 + m, :], 128))
                attn_head_block(S, C, A, 512, chunks, QT[pb:pb + 64, pair, q0:q0 + 512], 128, 1.0, orow=(pb, pb + 64))
    out_proj_residual(S, C, A, D, QT, 8, wo, gate, ntiles, D['hin'], D['hmid'])


def proj_tm2(S, C, pP, np_, lhs_list, Wfn, ncols, Psb, pcol0=0):
    c0 = 0
    nk = len(lhs_list)
    while c0 < ncols:
        w = min(512, ncols - c0)
        p = pP[np_[0] % len(pP)]
        np_[0] += 1
        for k in range(nk):
            o_mm(S, p[:, 0:w], lhs_list[k], Wfn(k, c0, w), k == 0, k == nk - 1)
        o_copy(S, C.evac_eng(), Psb[:, pcol0 + c0:pcol0 + c0 + w], p[:, 0:w])
        c0 += w


def rope_gen(S, src, dst, nh, hstride_s, hstride_d, hd, cos, sin, scr):
    (st, so) = src
    (dt_, do) = dst
    x1 = V(st, so, (hstride_s, nh), (1, hd))
    x2 = V(st, so + hd, (hstride_s, nh), (1, hd))
    o1 = V(dt_, do, (hstride_d, nh), (1, hd))
    o2 = V(dt_, do + hd, (hstride_d, nh), (1, hd))
    cb = bass.AP(cos.tensor, cos.offset, [list(cos.ap[0]), [0, nh], [1, hd]])
    sb_ = bass.AP(sin.tensor, sin.offset, [list(sin.ap[0]), [0, nh], [1, hd]])
    ta = V(scr['ra'], 0, (hd, nh), (1, hd))
    tb = V(scr['rb'], 0, (hd, nh), (1, hd))
    tc = V(scr['rc'], 0, (hd, nh), (1, hd))
    td = V(scr['rd'], 0, (hd, nh), (1, hd))
    o_tt(S, 'dve', ta, x1, cb, ALU.mult)
    o_tt(S, 'pool', tb, x2, sb_, ALU.mult)
    o_tt(S, 'dve', o1, ta, tb, ALU.subtract)
    o_tt(S, 'pool', tc, x1, sb_, ALU.mult)
    o_tt(S, 'dve', td, x2, cb, ALU.mult)
    o_tt(S, 'pool', o2, tc, td, ALU.add)


def transposes_gen(S, C, items):
    i = 0
    n_it = len(items)
    while i < n_it:
        n = min(8, n_it - i)
        pT = C.next_pT()
        for j in range(n):
            src, dst = items[i + j]
            w = src.shape[-1]
            o_tr(S, pT[0:w, j * 128:(j + 1) * 128], src, C.ident[:, :])
        for j in range(n):
            src, dst = items[i + j]
            w = src.shape[-1]
            o_copy(S, C.evac_eng(), dst, pT[0:w, j * 128:(j + 1) * 128])
        i += n


def build_A_mla(S, D, layer, ntiles=18):
    C = Ctx(S, D)
    md = load_mod(S, D, layer, 'md1', 0, D['norm1'])
    Wd = S.sb('wdown', [128, 8, 704], BF16)
    Wq = S.sb('wuq', [128, 3, 1536], BF16)
    Wkv = S.sb('wukv', [128, 2, 2048], BF16)
    S.dma('pool', Wd[:, :, :], D['wdown'].rearrange('(k p) m -> p k m', p=128))
    S.dma('pool', Wq[:, :, :], D['wuq'].rearrange('(k p) m -> p k m', p=128))
    S.dma('pool', Wkv[:, :, :], D['wukv'].rearrange('(k p) m -> p k m', p=128))
    lg = S.sb('lg', [128, 5], F32)
    S.dma('sp', lg[:, :], D['lg'])
    gq = S.sb('gq', [128, 1536], F32)
    gk = S.sb('gk', [128, 192], F32)
    S.dma('sp', gq[:, :], D['gq'])
    S.dma('sp', gk[:, :], D['gk'])
    o_ts(S, 'dve', gq[:, :], gq[:, :], 192.0 ** -0.5, None, ALU.mult)
    ct = S.sb('cost', [128, ntiles - 2, 32], F32)
    sn = S.sb('sint', [128, ntiles - 2, 32], F32)
    S.dma('sp', ct[:, :, :], D['cos'])
    S.dma('sp', sn[:, :, :], D['sin'])
    scr = dict(ss=S.sb('ss', [128, 1], F32), rs=S.sb('rs', [128, 1], F32), xs=S.sb('xs', [128, 1024], BF16),
               xtmp=S.sb('xtmp', [128, 1024], F32), sq=S.sb('sq', [128, 1536], F32), ssh=S.sb('ssh', [128, 16], F32),
               rsh=S.sb('rsh', [128, 16], F32), ra=S.sb('ra', [128, 256], F32), rb=S.sb('rb', [128, 256], F32),
               rc=S.sb('rc', [128, 256], F32), rd=S.sb('rd', [128, 256], F32))
    ht = [S.sb('ht%d' % i, [128, 1024], F32) for i in range(2)]
    xTt = S.sb('xTt', [128, 8, 128], BF16)
    Psb = S.sb('Psb', [128, 704], F32)
    ss2 = S.sb('ss2', [128, 2], F32)
    rs2 = S.sb('rs2', [128, 2], F32)
    cs = S.sb('cs', [128, 640], BF16)
    cT = S.sb('cT', [128, 5, 128], BF16)
    Qsb = S.sb('Qsb', [128, 1536], F32)
    KVsb = S.sb('KVsb', [128, 2048], F32)
    qn = S.sb('qn', [128, 1536], F32)
    qb = S.sb('qb', [128, 1536], BF16)
    ksq = S.sb('ksq', [128, 1024], F32)
    ssk = S.sb('ssk', [128, 8], F32)
    ssr = S.sb('ssr', [128, 1], F32)
    rsk = S.sb('rsk', [128, 8], F32)
    kr = S.sb('kr', [128, 8, 64], F32)
    kb = S.sb('kb', [128, 8, 192], BF16)
    vb = [S.sb('vb%d' % i, [128, 1024], BF16) for i in range(2)]
    stg = [dict(qn=S.sb('sqn%d' % i, [128, 8, 128], BF16), qr=S.sb('sqr%d' % i, [128, 8, 128], BF16),
                kn=S.sb('skn%d' % i, [128, 8, 128], BF16), kr=S.sb('skr%d' % i, [128, 8, 128], BF16)) for i in range(2)]
    pP = [S.ps('pP%d' % i, [128, 512], F32) for i in range(3)]
    np_ = [0]
    for t in range(ntiles):
        s = 1 if t < 2 else 0
        h = ht[t % 2]
        S.dma('sp', h[:, :], D['hin'][t * 128:(t + 1) * 128, :])
        norm_T(S, C, h[:, :], V(xTt, 0, (128, 8), (1, 128)), md[:, s, 0, :], md[:, s, 1, :], scr)
        proj_tm2(S, C, pP, np_, [xTt[:, k, :] for k in range(8)], lambda k, c0, w: Wd[:, k, c0:c0 + w], 704, Psb)
        o_act(S, C.junk[:, 0:384], Psb[:, 0:384], AF.Square, scale=384.0 ** -0.5, accum=ss2[:, 0:1])
        o_act(S, C.junk[:, 0:256], Psb[:, 384:640], AF.Square, scale=1.0 / 16.0, accum=ss2[:, 1:2])
        rstd_rows(S, C, ss2[:, 0:2], rs2[:, 0:2], 2, 1.0)
        o_act(S, cs[:, 0:384], Psb[:, 0:384], AF.Copy, scale=rs2[:, 0:1])
        o_act(S, cs[:, 384:640], Psb[:, 384:640], AF.Copy, scale=rs2[:, 1:2])
        pT = C.next_pT()
        for k in range(5):
            o_tr(S, pT[:, k * 128:(k + 1) * 128], cs[:, k * 128:(k + 1) * 128], C.ident[:, :])
        o_tt(S, 'dve', V(cT, 0, (128, 5), (1, 128)), V(pT, 0, (128, 5), (1, 128)), V(lg, 0, (1, 5), (0, 128)), ALU.mult)
        proj_tm2(S, C, pP, np_, [cT[:, k, :] for k in range(3)], lambda k, c0, w: Wq[:, k, c0:c0 + w], 1536, Qsb)
        proj_tm2(S, C, pP, np_, [cT[:, 3 + k, :] for k in range(2)], lambda k, c0, w: Wkv[:, k, c0:c0 + w], 2048, KVsb)
        head_norm(S, C, (Qsb, 0), 8, 192, gq, (qn, 0), scr)
        o_copy(S, 'act', qb[:, :], qn[:, :])
        if t >= 2:
            rope_gen(S, (qn, 128), (qb, 128), 8, 192, 192, 32, ct[:, t - 2, :], sn[:, t - 2, :], scr)
        knope = V(KVsb, 0, (256, 8), (1, 128))
        ksqv = V(ksq, 0, (128, 8), (1, 128))
        o_tt(S, 'pool', ksqv, knope, knope, ALU.mult)
        o_red(S, 'dve', ssk[:, 0:8], ksqv)
        o_act(S, C.junk[:, 0:64], Psb[:, 640:704], AF.Square, accum=ssr[:, 0:1])
        o_ts(S, 'dve', ssk[:, 0:8], ssk[:, 0:8], ssr[:, 0:1], None, ALU.add)
        rstd_rows(S, C, ssk[:, 0:8], rsk[:, 0:8], 8, 1.0 / 192.0)
        o_tt(S, 'dve', ksqv, knope, V(rsk, 0, (1, 8), (0, 128)), ALU.mult)
        o_tt(S, 'pool', V(kb, 0, (192, 8), (1, 128)), ksqv, V(gk, 0, (0, 8), (1, 128)), ALU.mult)
        krv = V(kr, 0, (64, 8), (1, 64))
        o_tt(S, 'dve', krv, V(Psb, 640, (0, 8), (1, 64)), V(rsk, 0, (1, 8), (0, 64)), ALU.mult)
        o_tt(S, 'pool', krv, krv, V(gk, 128, (0, 8), (1, 64)), ALU.mult)
        if t >= 2:
            rope_gen(S, (kr, 0), (kb, 128), 8, 64, 192, 32, ct[:, t - 2, :], sn[:, t - 2, :], scr)
        else:
            o_copy(S, 'dve', V(kb, 128, (192, 8), (1, 64)), krv)
        vbt = vb[t % 2]
        o_copy(S, 'act', V(vbt, 0, (128, 8), (1, 128)), V(KVsb, 128, (256, 8), (1, 128)))
        S.dma('sp', D['Vt'][t * 128:(t + 1) * 128, :], vbt[:, :], is_output=True)
        sg = stg[t % 2]
        items = []
        for hh in range(8):
            items.append((qb[:, hh * 192:hh * 192 + 128], sg['qn'][:, hh, :]))
        for hh in range(8):
            items.append((qb[:, hh * 192 + 128:hh * 192 + 192], sg['qr'][0:64, hh, :]))
        for hh in range(8):
            items.append((kb[:, hh, 0:128], sg['kn'][:, hh, :]))
        for hh in range(8):
            items.append((kb[:, hh, 128:192], sg['kr'][0:64, hh, :]))
        transposes_gen(S, C, items)
        S.dma('sp', D['QTn'][:, :, t * 128:(t + 1) * 128].rearrange('h p n -> p h n'), sg['qn'][:, :, :], is_output=True)
        S.dma('sp', D['QTr'][:, :, t * 128:(t + 1) * 128].rearrange('h p n -> p h n'), sg['qr'][0:64, :, :], is_output=True)
        S.dma('sp', D['KTn'][:, :, t * 128:(t + 1) * 128].rearrange('h p n -> p h n'), sg['kn'][:, :, :], is_output=True)
        S.dma('sp', D['KTr'][:, :, t * 128:(t + 1) * 128].rearrange('h p n -> p h n'), sg['kr'][0:64, :, :], is_output=True)


def build_B_mla(S, D, layer, ntiles=18, nkeys=8448):
    C = Ctx(S, D)
    A = AttnBufs(S)
    gate = load_gate_rep(S, D, layer, 'g1rep', 0)
    ntok = ntiles * 128
    nch = nkeys // 128
    nqb = (ntok - 256) // 512
    QTn = S.sb('QTn', [128, 8, ntok], BF16)
    QTr = S.sb('QTr', [128, 8, ntok], BF16)
    KTn = S.sb('KTn', [128, nkeys], BF16)
    KTr = S.sb('KTr', [128, nkeys], BF16)
    Vh = S.sb('Vh', [128, nch, 128], BF16)
    wo = S.sb('wo', [128, 8, 1024], BF16)
    S.dma('sp', QTn[:, :, :], D['QTn'].rearrange('h p n -> p h n'))
    S.dma('sp', QTr[0:64, :, :], D['QTr'].rearrange('h p n -> p h n'))
    S.dma('pool', wo[:, :, :], D['wo'].rearrange('(k p) m -> p k m', p=128))
    for h in range(8):
        S.dma('sp', KTn[:, :], D['KTnf'][h])
        S.dma('sp', KTr[0:64, :], D['KTrf'][h])
        S.dma('sp', Vh[:, :, :], D['Vf'][:, h * 128:(h + 1) * 128].rearrange('(c p) m -> p c m', p=128))
        chunks = [([(KTn[:, c * 128:(c + 1) * 128], QTn[:, h, 0:256]), (KTr[0:64, c * 128:(c + 1) * 128], QTr[0:64, h, 0:256])],
                   Vh[:, c, :], 128) for c in range(2)]
        attn_head_block(S, C, A, 256, chunks, QTn[:, h, 0:256], 128, 1.0)
        for qb in range(nqb):
            q0 = 256 + qb * 512
            chunks = [([(KTn[:, c * 128:(c + 1) * 128], QTn[:, h, q0:q0 + 512]), (KTr[0:64, c * 128:(c + 1) * 128], QTr[0:64, h, q0:q0 + 512])],
                       Vh[:, c, :], 128) for c in range(nch)]
            attn_head_block(S, C, A, 512, chunks, QTn[:, h, q0:q0 + 512], 128, 1.0)
    out_proj_residual(S, C, A, D, QTn, 8, wo, gate, ntiles, D['hin'], D['hmid'])


def build_A_ret(S, D, layer, ntiles=18):
    C = Ctx(S, D)
    md = load_mod(S, D, layer, 'md1', 0, D['norm1'])
    W = S.sb('wqkvg', [128, 8, 6144], BF16)
    for k in range(8):
        S.dma('pool', W[:, k, :], D['wqkvg'][k * 128:(k + 1) * 128, :])
    ld = S.sb('ld', [128, 2, 4], F32)
    S.dma('sp', ld[:, :, :], D['ldrep'])
    E = S.sb('E', [128, 4], F32)
    Ec = S.sb('Ec', [128, 2, 2], F32)
    S.dma('sp', E[:, :], D['E'])
    S.dma('sp', Ec[:, :, :], D['Ec'])
    dec = S.sb('dec', [128, 4, 4], F32)
    decc = S.sb('decc', [128, 2, 2, 4], F32)
    for kind in range(4):
        o_ts(S, 'dve', dec[:, kind, :], ld[:, kind % 2, :], E[:, kind:kind + 1], None, ALU.mult)
    for tt_ in range(2):
        for dr in range(2):
            o_ts(S, 'dve', decc[:, tt_, dr, :], ld[:, dr, :], Ec[:, tt_, dr:dr + 1], None, ALU.mult)
    o_act(S, dec[:, :, :], dec[:, :, :], AF.Exp)
    o_act(S, decc[:, :, :, :], decc[:, :, :, :], AF.Exp)
    o_ts(S, 'dve', dec[:, 2:4, :], dec[:, 2:4, :], 1.0 / 16.0, None, ALU.mult)
    o_ts(S, 'dve', decc[:, :, :, :], decc[:, :, :, :], 1.0 / 16.0, None, ALU.mult)
    scr = dict(ss=S.sb('ss', [128, 1], F32), rs=S.sb('rs', [128, 1], F32), xs=S.sb('xs', [128, 1024], BF16),
               xtmp=S.sb('xtmp', [128, 1024], F32), ra=S.sb('ra', [128, 1024], F32), rb=S.sb('rb', [128, 1024], F32),
               rc=S.sb('rc', [128, 1024], F32), rd=S.sb('rd', [128, 1024], F32))
    ht = [S.sb('ht%d' % i, [128, 1024], F32) for i in range(2)]
    cst = [S.sb('cst%d' % i, [128, 2, 128], F32) for i in range(2)]
    xTt = S.sb('xTt', [128, 8, 128], BF16)
    Psb = S.sb('Psb', [128, 2048], F32)
    qkr = S.sb('qkr', [128, 2048], F32)
    scb = [S.sb('scb%d' % i, [128, 1024], BF16) for i in range(4)]
    vb = S.sb('vb', [128, 2048], BF16)
    gs = S.sb('gs', [128, 2048], BF16)
    stg = [S.sb('stg%d' % i, [128, 8, 128], BF16) for i in range(4)]
    pP = [S.ps('pP%d' % i, [128, 512], F32) for i in range(4)]
    np_ = [0]
    names = ['QfT', 'QbT', 'KfT', 'KbT']
    for t in range(ntiles):
        s = 1 if t < 2 else 0
        h = ht[t % 2]
        S.dma('sp', h[:, :], D['hin'][t * 128:(t + 1) * 128, :])
        norm_T(S, C, h[:, :], V(xTt, 0, (128, 8), (1, 128)), md[:, s, 0, :], md[:, s, 1, :], scr)
        lhs = [xTt[:, k, :] for k in range(8)]
        if t >= 2:
            n = t - 2
            cs_ = cst[t % 2]
            S.dma('sp', cs_[:, 0, :], D['cos'][:, n, :])
            S.dma('sp', cs_[:, 1, :], D['sin'][:, n, :])
            proj_tm2(S, C, pP, np_, lhs, lambda k, c0, w: W[:, k, c0:c0 + w], 2048, Psb)
            rope_gen(S, (Psb, 0), (qkr, 0), 8, 256, 256, 128, cs_[:, 0, :], cs_[:, 1, :], scr)
            for kind in range(4):
                src_off = 0 if kind < 2 else 1024
                o_tt(S, 'dve' if kind % 2 == 0 else 'pool', V(scb[kind], 0, (256, 4), (1, 256)), V(qkr, src_off, (256, 4), (1, 256)),
                     V(dec, kind * 4, (1, 4), (0, 256)), ALU.mult)
        else:
            proj_tm2(S, C, pP, np_, lhs, lambda k, c0, w: W[:, k, 1024 + c0:1024 + c0 + w], 1024, Psb, pcol0=1024)
            for dr in range(2):
                o_tt(S, 'dve' if dr == 0 else 'pool', V(scb[2 + dr], 0, (256, 4), (1, 256)), V(Psb, 1024, (256, 4), (1, 256)),
                     V(decc, (t * 2 + dr) * 4, (1, 4), (0, 256)), ALU.mult)
        S.dma('sp', D['Kf'][t * 128:(t + 1) * 128, :], scb[2][:, :], is_output=True)
        S.dma('sp', D['Kb'][t * 128:(t + 1) * 128, :], scb[3][:, :], is_output=True)
        if t >= 2:
            n = t - 2
            items = []
            for kind in range(4):
                for hh in range(4):
                    for hf in range(2):
                        items.append((scb[kind][:, hh * 256 + hf * 128:hh * 256 + (hf + 1) * 128], stg[kind][:, hh * 2 + hf, :]))
            transposes_gen(S, C, items)
            for kind in range(4):
                S.dma('sp', D[names[kind]][:, :, :, n * 128:(n + 1) * 128].rearrange('h f p n -> p (h f) n'), stg[kind][:, :, :], is_output=True)
        for c in range(4):
            p = pP[np_[0] % 4]
            np_[0] += 1
            for k in range(8):
                o_mm(S, p[:, :], lhs[k], W[:, k, 2048 + c * 512:2048 + (c + 1) * 512], k == 0, k == 7)
            o_copy(S, C.evac_eng(), vb[:, c * 512:(c + 1) * 512], p[:, :])
        S.dma('sp', D['Vt'][t * 128:(t + 1) * 128, :], vb[:, :], is_output=True)
        if t >= 2:
            n = t - 2
            for c in range(4):
                p = pP[np_[0] % 4]
                np_[0] += 1
                for k in range(8):
                    o_mm(S, p[:, :], lhs[k], W[:, k, 4096 + c * 512:4096 + (c + 1) * 512], k == 0, k == 7)
                o_act(S, gs[:, c * 512:(c + 1) * 512], p[:, :], AF.Silu)
            S.dma('sp', D['Gs'][n * 128:(n + 1) * 128, :], gs[:, :], is_output=True)


def build_A2_ret(S, D, layer, ntiles=18):
    C = Ctx(S, D)
    ld = S.sb('ld', [128, 2, 4], F32)
    S.dma('sp', ld[:, :, :], D['ldrep'])
    wm = S.sb('wm', [128, 2, ntiles], F32)
    S.dma('sp', wm[:, :, :], D['wmul'])
    wt = S.sb('wt', [128, 2, ntiles, 4], F32)
    for dr in range(2):
        for t in range(ntiles):
            o_ts(S, 'dve', wt[:, dr, t, :], ld[:, dr, :], wm[:, dr, t:t + 1], None, ALU.mult)
    o_act(S, wt[:, :, :, :], wt[:, :, :, :], AF.Exp)
    U = [[S.sb('U%d_%d' % (a, dr), [128, 4, 2, 512], F32) for dr in range(2)] for a in range(2)]
    for a in range(2):
        for dr in range(2):
            o_memset(S, 'pool', U[a][dr][:, :, :, :], 0.0)
    Kt = [[S.sb('Kt%d_%d' % (i, dr), [128, 1024], BF16) for dr in range(2)] for i in range(2)]
    Vt = [S.sb('Vt%d' % i, [128, 2048], BF16) for i in range(2)]
    pD = [S.ps('pD%d' % i, [128, 512], F32) for i in range(4)]
    nd = 0
    for t in range(ntiles):
        a = 0 if t < 2 else 1
        S.dma('sp', Kt[t % 2][0][:, :], D['Kf'][t * 128:(t + 1) * 128, :])
        S.dma('sp', Kt[t % 2][1][:, :], D['Kb'][t * 128:(t + 1) * 128, :])
        S.dma('sp', Vt[t % 2][:, :], D['Vt'][t * 128:(t + 1) * 128, :])
        for dr in range(2):
            for hh in range(4):
                for hf in range(2):
                    p = pD[nd % 4]
                    nd += 1
                    o_mm(S, p[:, :], Kt[t % 2][dr][:, hh * 256 + hf * 128:hh * 256 + (hf + 1) * 128], Vt[t % 2][:, hh * 512:(hh + 1) * 512], True, True)
                    u = U[a][dr][:, hh, hf, :]
                    o_stt(S, 'dve', u, p[:, :], wt[:, dr, t, hh:hh + 1], u, ALU.mult, ALU.add)
    if 'Lfb' in D:
        for dr in range(2):
            S.dma('sp', D['Sc'][dr].rearrange('h f p n -> p h f n'), U[0][dr][:, :, :, :], is_output=True)
            S.dma('sp', D['Lfb'][dr].rearrange('h f p n -> p h f n'), U[1][dr][:, :, :, :], is_output=True)
        return
    outs = [['Scf', 'Scb'], ['Lf', 'Lb']]
    for a in range(2):
        for dr in range(2):
            S.dma('sp', D[outs[a][dr]].rearrange('h f p n -> p h f n'), U[a][dr][:, :, :, :], is_output=True)


def build_B_ret(S, D, layer, nchunks=16, nsrc=5):
    C = Ctx(S, D)
    gate = S.sb('g1rep', [128, 1024], F32)
    S.dma('sp', gate[:, :], D['modv'][layer, 0, 2048:3072].partition_broadcast(128))
    ld = S.sb('ld', [128, 2, 4], F32)
    S.dma('sp', ld[:, :, :], D['ldrep'])
    cm = S.sb('cm', [128, 2, nsrc, 2], F32)
    S.dma('sp', cm[:, :, :, :], D['coefm'])
    coef = S.sb('coef', [128, 2, nsrc, 4], F32)
    for dr in range(2):
        for sidx in range(nsrc):
            o_ts(S, 'dve', coef[:, dr, sidx, :], ld[:, dr, :], cm[:, dr, sidx, 0:1], None, ALU.mult)
    o_act(S, coef[:, :, :, :], coef[:, :, :, :], AF.Exp)
    for dr in range(2):
        for sidx in range(nsrc):
            o_ts(S, 'dve', coef[:, dr, sidx, :], coef[:, dr, sidx, :], cm[:, dr, sidx, 1:2], None, ALU.mult)
    d128 = S.sb('d128', [128, 2, 4], F32)
    o_act(S, d128[:, :, :], ld[:, :, :], AF.Exp, scale=128.0)
    St = [S.sb('St%d' % dr, [128, 4, 2, 512], F32) for dr in range(2)]
    Sb = [S.sb('Sb%d' % dr, [128, 4, 2, 512], BF16) for dr in range(2)]
    Lp = [S.sb('Lp%d' % i, [128, 2, 512], F32) for i in range(2)]
    nl = 0
    for dr in range(2):
        o_memset(S, 'pool', St[dr][:, :, :, :], 0.0)
        for sidx in range(nsrc):
            for hh in range(4):
                lp = Lp[nl % 2]
                nl += 1
                if 'G_L' in D:
                    src = D['G_L'][sidx, dr, hh] if sidx < nsrc - 1 else D['Sc'][dr, hh]
                else:
                    src = D['Lall'][sidx, dr, hh]
                S.dma('sp', lp[:, :, :], src.rearrange('f p n -> p f n'))
                u = St[dr][:, hh, :, :]
                o_stt(S, 'dve', u, lp[:, :, :], coef[:, dr, sidx, hh:hh + 1], u, ALU.mult, ALU.add)
        o_copy(S, 'act', Sb[dr][:, :, :, :], St[dr][:, :, :, :])
    mD = S.sb('mD', [128, 2, 128], BF16)
    S.dma('sp', mD[:, :, :], D['maskD'].rearrange('d p n -> p d n'))
    onr = S.sb('onr', [128, 2048], F32)
    S.dma('sp', onr[:, :], D['onorm'])
    wo = S.sb('wo', [128, 16, 1024], BF16)
    S.dma('pool', wo[:, :, :], D['wo'].rearrange('(k p) m -> p k m', p=128))
    QTc = [S.sb('QTc%d' % i, [128, 8, 128], BF16) for i in range(2)]
    KTc = [S.sb('KTc%d' % i, [128, 8, 128], BF16) for i in range(2)]
    Kc = [S.sb('Kc%d' % i, [128, 1024], BF16) for i in range(2)]
    Vc = [S.sb('Vc%d' % i, [128, 2048], BF16) for i in range(2)]
    AT = [S.sb('AT%d' % i, [128, 128], BF16) for i in range(2)]
    yf = S.sb('yf', [128, 2048], F32)
    y = S.sb('y', [128, 2048], F32)
    sq = S.sb('sq', [128, 2048], F32)
    yb = S.sb('yb', [128, 2048], BF16)
    gsb = S.sb('gsb', [128, 2048], BF16)
    YT = S.sb('YT', [128, 16, 128], BF16)
    st4 = S.sb('st4', [128, 4], F32)
    mean = S.sb('mean', [128, 4], F32)
    rs4 = S.sb('rs4', [128, 4], F32)
    tmpS2 = [S.sb('tmpS%d' % i, [128, 512], F32) for i in range(2)]
    htl = [S.sb('htl%d' % i, [128, 1024], F32) for i in range(2)]
    hol = [S.sb('hol%d' % i, [128, 1024], F32) for i in range(2)]
    oot = [S.sb('oot%d' % i, [128, 512], F32) for i in range(2)]
    pS = [S.ps('pS%d' % i, [128, 512], F32) for i in range(2)]
    pY = [S.ps('pY%d' % i, [128, 512], F32) for i in range(2)]
    pD = [S.ps('pD%d' % i, [128, 512], F32) for i in range(2)]
    nm = ['Q%sT', 'K%sT', 'K%s']
    cnt = dict(s=0, y=0, d=0, o=0, it=0)
    for dr in range(2):
        sfx = 'f' if dr == 0 else 'b'
        order = list(range(nchunks)) if dr == 0 else list(range(nchunks - 1, -1, -1))
        for n in order:
            it = cnt['it']
            cnt['it'] += 1
            qt = QTc[it % 2]
            kt = KTc[it % 2]
            kc = Kc[it % 2]
            vc = Vc[it % 2]
            S.dma('sp', qt[:, :, :], D['Q%sT' % sfx][:, :, :, n * 128:(n + 1) * 128].rearrange('h f p n -> p (h f) n'))
            S.dma('sp', kt[:, :, :], D['K%sT' % sfx][:, :, :, n * 128:(n + 1) * 128].rearrange('h f p n -> p (h f) n'))
            S.dma('sp', kc[:, :], D['K%s' % sfx][256 + n * 128:256 + (n + 1) * 128, :])
            S.dma('sp', vc[:, :], D['Vt'][256 + n * 128:256 + (n + 1) * 128, :])
            if dr == 1:
                S.dma('sp', yf[:, :], D['yf'][n * 128:(n + 1) * 128, :])
                S.dma('sp', gsb[:, :], D['Gs'][n * 128:(n + 1) * 128, :])
            for hh in range(4):
                ps_ = pS[cnt['s'] % 2]
                at = AT[cnt['s'] % 2]
                cnt['s'] += 1
                for hf in range(2):
                    o_mm(S, ps_[:, 0:128], kt[:, hh * 2 + hf, :], qt[:, hh * 2 + hf, :], hf == 0, hf == 1)
                o_tt(S, 'dve', at[:, :], ps_[:, 0:128], mD[:, dr, :], ALU.mult)
                py = pY[cnt['y'] % 2]
                cnt['y'] += 1
                o_mm(S, py[:, :], at[:, :], vc[:, hh * 512:(hh + 1) * 512], True, False)
                for hf in range(2):
                    o_mm(S, py[:, :], qt[:, hh * 2 + hf, :], Sb[dr][:, hh, hf, :], False, hf == 1)
                if dr == 0:
                    o_copy(S, 'act', yf[:, hh * 512:(hh + 1) * 512], py[:, :])
                else:
                    o_tt(S, 'dve', y[:, hh * 512:(hh + 1) * 512], py[:, :], yf[:, hh * 512:(hh + 1) * 512], ALU.add)
                for hf in range(2):
                    pd = pD[cnt['d'] % 2]
                    cnt['d'] += 1
                    o_mm(S, pd[:, :], kc[:, hh * 256 + hf * 128:hh * 256 + (hf + 1) * 128], vc[:, hh * 512:(hh + 1) * 512], True, True)
                    u = St[dr][:, hh, hf, :]
                    ts_ = tmpS2[cnt['d'] % 2]
                    o_tt(S, 'dve', ts_[:, :], pd[:, :], u, ALU.add)
                    o_act(S, u, ts_[:, :], AF.Copy, scale=d128[:, dr, hh:hh + 1])
                    o_act(S, Sb[dr][:, hh, hf, :], ts_[:, :], AF.Copy, scale=d128[:, dr, hh:hh + 1])
            if dr == 0:
                S.dma('sp', D['yf'][n * 128:(n + 1) * 128, :], yf[:, :], is_output=True)
            else:
                yv = V(y, 0, (512, 4), (1, 512))
                sv = V(sq, 0, (512, 4), (1, 512))
                o_red(S, 'dve', st4[:, 0:4], yv)
                o_ts(S, 'dve', mean[:, 0:4], st4[:, 0:4], 1.0 / 512.0, None, ALU.mult)
                o_tt(S, 'dve', yv, yv, V(mean, 0, (1, 4), (0, 512)), ALU.subtract)
                o_tt(S, 'pool', sv, yv, yv, ALU.mult)
                o_red(S, 'dve', st4[:, 0:4], sv)
                rstd_rows(S, C, st4[:, 0:4], rs4[:, 0:4], 4, 1.0 / 512.0)
                o_tt(S, 'dve', yv, yv, V(rs4, 0, (1, 4), (0, 512)), ALU.mult)
                o_tt(S, 'pool', sq[:, :], y[:, :], onr[:, :], ALU.mult)
                o_tt(S, 'dve', yb[:, :], sq[:, :], gsb[:, :], ALU.mult)
                items = [(yb[:, k * 128:(k + 1) * 128], YT[:, k, :]) for k in range(16)]
                transposes_gen(S, C, items)
                hti = htl[cnt['o'] % 2]
                hoi = hol[cnt['o'] % 2]
                cnt['o'] += 1
                S.dma('sp', hti[:, :], D['hin'][256 + n * 128:256 + (n + 1) * 128, :])
                for hf in range(2):
                    p = pS[cnt['s'] % 2]
                    cnt['s'] += 1
                    o_ = oot[hf]
                    for k in range(16):
                        o_mm(S, p[:, :], YT[:, k, :], wo[:, k, hf * 512:(hf + 1) * 512], k == 0, k == 15)
                    o_tt(S, 'dve', o_[:, :], p[:, :], gate[:, hf * 512:(hf + 1) * 512], ALU.mult)
                    o_tt(S, 'pool', hoi[:, hf * 512:(hf + 1) * 512], o_[:, :], hti[:, hf * 512:(hf + 1) * 512], ALU.add)
                S.dma('sp', D['hmid'][n * 128:(n + 1) * 128, :], hoi[:, :], is_output=True)


def ld_KT(S, dst, G, hsel, ntok, R):
    nlat = ntok - 256
    S.dma('sp', dst[:, :, 0:256], G[0, hsel, :, 0:256].rearrange('h p n -> p h n'))
    for r in range(R):
        S.dma('sp', dst[:, :, 256 + r * nlat:256 + (r + 1) * nlat], G[r, hsel, :, 256:ntok].rearrange('h p n -> p h n'))


def ld_KT1(S, dst, G, h, ntok, R):
    nlat = ntok - 256
    S.dma('sp', dst[:, 0:256], G[0, h, :, 0:256])
    for r in range(R):
        S.dma('sp', dst[:, 256 + r * nlat:256 + (r + 1) * nlat], G[r, h, :, 256:ntok])


def ld_V(S, dst, G, c0, c1, ntok, R):
    nlt = (ntok - 256) // 128
    S.dma('sp', dst[:, 0:2, :], G[0, 0:256, c0:c1].rearrange('(c p) m -> p c m', p=128))
    for r in range(R):
        S.dma('sp', dst[:, 2 + r * nlt:2 + (r + 1) * nlt, :], G[r, 256:ntok, c0:c1].rearrange('(c p) m -> p c m', p=128))


def build_B_gqa_f(S, D, layer, ntiles, R):
    C = Ctx(S, D, npT=0)
    A = AttnBufs(S, nS=4)
    gate = load_gate_rep(S, D, layer, 'g1rep', 0)
    ntok = ntiles * 128
    nkeys = 256 + R * (ntok - 256)
    nch = nkeys // 128
    QT = S.sb('QT', [128, 8, ntok], BF16)
    KT = S.sb('KT', [128, 2, nkeys], BF16)
    Vf = S.sb('Vf', [128, nch, 256], BF16)
    wo = S.sb('wo', [128, 8, 1024], BF16)
    S.dma('sp', QT[:, :, :], D['QT'].rearrange('h p n -> p h n'))
    ld_KT(S, KT, D['G_KT'], slice(0, 2), ntok, R)
    ld_V(S, Vf, D['G_V'], 0, 256, ntok, R)
    S.dma('pool', wo[:, :, :], D['wo'].rearrange('(k p) m -> p k m', p=128))
    scale = 128.0 ** -0.5
    nqb = (ntok - 256) // 512
    for h in range(8):
        kv = h // 4
        chunks = [([(KT[:, kv, c * 128:(c + 1) * 128], QT[:, h, 0:256])], Vf[:, c, kv * 128:(kv + 1) * 128], 128) for c in range(2)]
        attn_head_block(S, C, A, 256, chunks, QT[:, h, 0:256], 128, scale)
        for qb in range(nqb):
            q0 = 256 + qb * 512
            chunks = [([(KT[:, kv, c * 128:(c + 1) * 128], QT[:, h, q0:q0 + 512])], Vf[:, c, kv * 128:(kv + 1) * 128], 128) for c in range(nch)]
            attn_head_block(S, C, A, 512, chunks, QT[:, h, q0:q0 + 512], 128, scale)
    out_proj_residual(S, C, A, D, QT, 8, wo, gate, ntiles, D['hin'], D['hmid'])


def build_B_mla_f(S, D, layer, ntiles, R):
    C = Ctx(S, D, npT=0)
    A = AttnBufs(S, nS=4)
    gate = load_gate_rep(S, D, layer, 'g1rep', 0)
    ntok = ntiles * 128
    nkeys = 256 + R * (ntok - 256)
    nch = nkeys // 128
    nqb = (ntok - 256) // 512
    QTn = S.sb('QTn', [128, 8, ntok], BF16)
    QTr = S.sb('QTr', [128, 8, ntok], BF16)
    KTn = S.sb('KTn', [128, nkeys], BF16)
    KTr = S.sb('KTr', [128, nkeys], BF16)
    Vh = S.sb('Vh', [128, nch, 128], BF16)
    wo = S.sb('wo', [128, 8, 1024], BF16)
    S.dma('sp', QTn[:, :, :], D['QTn'].rearrange('h p n -> p h n'))
    S.dma('sp', QTr[0:64, :, :], D['QTr'].rearrange('h p n -> p h n'))
    S.dma('pool', wo[:, :, :], D['wo'].rearrange('(k p) m -> p k m', p=128))
    for h in range(8):
        ld_KT1(S, KTn, D['G_KTn'], h, ntok, R)
        ld_KT1(S, KTr[0:64, :], D['G_KTr'], h, ntok, R)
        ld_V(S, Vh, D['G_V'], h * 128, (h + 1) * 128, ntok, R)
        chunks = [([(KTn[:, c * 128:(c + 1) * 128], QTn[:, h, 0:256]), (KTr[0:64, c * 128:(c + 1) * 128], QTr[0:64, h, 0:256])],
                   Vh[:, c, :], 128) for c in range(2)]
        attn_head_block(S, C, A, 256, chunks, QTn[:, h, 0:256], 128, 1.0)
        for qb in range(nqb):
            q0 = 256 + qb * 512
            chunks = [([(KTn[:, c * 128:(c + 1) * 128], QTn[:, h, q0:q0 + 512]), (KTr[0:64, c * 128:(c + 1) * 128], QTr[0:64, h, q0:q0 + 512])],
                       Vh[:, c, :], 128) for c in range(nch)]
            attn_head_block(S, C, A, 512, chunks, QTn[:, h, q0:q0 + 512], 128, 1.0)
    out_proj_residual(S, C, A, D, QTn, 8, wo, gate, ntiles, D['hin'], D['hmid'])


def build_B_na_f(S, D, layer, ntiles, R):
    C = Ctx(S, D, npT=0)
    A = AttnBufs(S, nS=4)
    gate = load_gate_rep(S, D, layer, 'g1rep', 0)
    ntok = ntiles * 128
    nlat = ntok - 256
    nlt = nlat // 128
    nqb = nlat // 512
    nloc = nlat + 512
    QT = S.sb('QT', [128, 8, ntok], BF16)
    KTp = [S.sb('KTp%d' % i, [128, 256 + nloc], BF16) for i in range(2)]
    Vp = [S.sb('Vp%d' % i, [128, 2 + nloc // 128, 128], BF16) for i in range(2)]
    wo = S.sb('wo', [128, 8, 1024], BF16)
    TT = [S.sb('TT%d' % i, [128, 2, 1408], BF16) for i in range(2)]
    mK = S.sb('mK', [2, 128], BF16)
    mQs = [S.sb('mQ%d' % i, [2, 8 * 512], BF16) for i in range(2)]
    hw = S.sb('hw', [128, 2, R], F32)
    slabK = [S.sb('slabK%d' % i, [128, R, 256], BF16) for i in range(2)]
    slabV = [S.sb('slabV%d' % i, [128, R, 2, 128], BF16) for i in range(2)]
    S.dma('sp', hw[:, :, :], D['hw'])
    S.dma('sp', QT[:, :, :], D['QT'].rearrange('h p n -> p h n'))
    S.dma('pool', wo[:, :, :], D['wo'].rearrange('(k p) m -> p k m', p=128))
    S.dma('sp', mK[:, :], D['maskK'])
    GK = D['G_KT']
    GV = D['G_V']
    nm = 0
    for pair in range(8):
        tt = TT[pair % 2]
        KT = KTp[pair % 2]
        Vf = Vp[pair % 2]
        pc0, pc1 = pair * 128, (pair + 1) * 128
        S.dma('sp', tt[:, :, :], D['TT'][2 * pair:2 * pair + 2].rearrange('h p n -> p h n'))
        S.dma('sp', KT[:, 0:256], D['KT'][pair, :, 0:256])
        S.dma('sp', KT[:, 512:512 + nlat], D['KT'][pair, :, 256:ntok])
        S.dma('sp', Vf[:, 0:2, :], D['Vt'][0:256, pc0:pc1].rearrange('(c p) m -> p c m', p=128))
        S.dma('sp', Vf[:, 4:4 + nlt, :], D['Vt'][256:ntok, pc0:pc1].rearrange('(c p) m -> p c m', p=128))
        for a in range(2):
            sk = slabK[a]
            sv = slabV[a]
            if a == 0:
                S.dma('sp', sk[:, :, :], GK[:, pair, :, 256:512].rearrange('r p n -> p r n'))
                for r in range(R):
                    S.dma('sp', sv[:, r, :, :], GV[r, 256:512, pc0:pc1].rearrange('(c p) m -> p c m', p=128))
                kd = KT[:, 256:512]
                vd = Vf[:, 2:4, :]
            else:
                S.dma('sp', sk[:, :, :], GK[:, pair, :, 0:256].rearrange('r p n -> p r n'))
                for r in range(R):
                    S.dma('sp', sv[:, r, :, :], GV[r, 0:256, pc0:pc1].rearrange('(c p) m -> p c m', p=128))
                kd = KT[:, 512 + nlat:768 + nlat]
                vd = Vf[:, 4 + nlt:6 + nlt, :]
            for r in range(R):
                w = hw[:, a, r:r + 1]
                if r == 0:
                    o_ts(S, 'dve', kd, sk[:, 0, :], w, None, ALU.mult)
                    o_ts(S, 'pool', vd, sv[:, 0, :, :], w, None, ALU.mult)
                else:
                    o_stt(S, 'dve', kd, sk[:, r, :], w, kd, ALU.mult, ALU.add)
                    o_stt(S, 'dve', vd, sv[:, r, :, :], w, vd, ALU.mult, ALU.add)
        for hh in range(2):
            pb = hh * 64
            chunks = [([(KT[pb:pb + 64, c * 128:(c + 1) * 128], QT[pb:pb + 64, pair, 0:256])],
                       Vf[:, c, :], 128) for c in range(2)]
            attn_head_block(S, C, A, 256, chunks, QT[pb:pb + 64, pair, 0:256], 128, 1.0, orow=(pb, pb + 64))
            for qb in range(nqb):
                q0 = 256 + qb * 512
                mQ = mQs[nm % 2]
                nm += 1
                S.dma('sp', mQ[:, :], D['maskQ'][:, qb * 4096:(qb + 1) * 4096])
                chunks = []
                for c in range(2):
                    chunks.append(([(KT[pb:pb + 64, c * 128:(c + 1) * 128], QT[pb:pb + 64, pair, q0:q0 + 512])],
                                   Vf[:, c, :], 128))
                for c in range(8):
                    m = 4 * qb + c
                    i0 = 14 - 2 * c
                    pieces = [(KT[pb:pb + 64, 256 + m * 128:256 + (m + 1) * 128], QT[pb:pb + 64, pair, q0:q0 + 512]),
                              (C.ident[:, :], tt[:, hh, i0 * 64:i0 * 64 + 512]),
                              (mK[0:2, :], mQ[0:2, c * 512:(c + 1) * 512])]
                    chunks.append((pieces, Vf[:, 2 + m, :], 128))
                attn_head_block(S, C, A, 512, chunks, QT[pb:pb + 64, pair, q0:q0 + 512], 128, 1.0, orow=(pb, pb + 64))
    out_proj_residual(S, C, A, D, QT, 8, wo, gate, ntiles, D['hin'], D['hmid'])


GROUPS = [[0, 1, 2, 3], [4, 5, 6, 7]]


def build_fused(S, X, nc, ntiles=18, R=4, groups=None):
    groups = groups or GROUPS
    bf = BF16
    ntok = ntiles * 128
    nlat = ntok - 256
    N = ntiles - 2

    def dram(name, shape, dt):
        return nc.dram_tensor(name, list(shape), dt).ap()

    def gather(name, src, rows, cols, dt):
        g = dram(name, [R * rows, cols], dt)
        z = mybir.dt.size(dt)
        maxrows = max(16, ((1 << 20) // (cols * z)) // 16 * 16)
        if rows <= maxrows:
            S.cc('AllGather', groups, src.opt(), g.opt())
            return g
        p0 = 0
        while p0 < rows:
            rp = min(maxrows, rows - p0)
            tmp = dram('%s_t%d' % (name, p0), [R * rp, cols], dt)
            S.cc('AllGather', groups, src[p0:p0 + rp, :].opt(), tmp.opt())
            for r in range(R):
                S.dma('sp', g[r * rows + p0:r * rows + p0 + rp, :], tmp[r * rp:(r + 1) * rp, :])
            p0 += rp
        return g

    S.final_names = {'out'}
    base = dict(ident=X['ident'])
    mpart = dram('mpart', [8, 1536], F32)
    Dm = dict(base, cin=X['cin'], mod_w=X['mod_w'], mod_b=X['mod_b'], modv=mpart.rearrange('(s i) c -> s i c', s=2))
    build_M(S, Dm, ncol=1536, nrow=2)
    gm = gather('G_mod', mpart, 8, 1536, F32)
    modv = dram('modv', [4, 2, 6144], F32)
    for r in range(R):
        for s_ in range(2):
            S.dma('sp', modv[:, s_, r * 1536:(r + 1) * 1536], gm[r * 8 + s_ * 4:r * 8 + s_ * 4 + 4, :])
    base['modv'] = modv
    hcur = X['hin']
    S.new_phase()
    QTn = dram('QTn0', [8, 128, ntok], bf)
    QTr = dram('QTr0', [8, 64, ntok], bf)
    KTn = dram('KTn0', [8, 128, ntok], bf)
    KTr = dram('KTr0', [8, 64, ntok], bf)
    Vt = dram('Vt0', [ntok, 1024], bf)
    build_A_mla(S, dict(base, norm1=X['norm1_0'], wdown=X['wdown'], wuq=X['wuq'], wukv=X['wukv'], lg=X['lg'], gq=X['gq0'], gk=X['gk0'],
                        cos=X['cos64'], sin=X['sin64'], hin=hcur, QTn=QTn, QTr=QTr, KTn=KTn, KTr=KTr, Vt=Vt), 0, ntiles=ntiles)
    gkn = gather('G_KTn0', KTn.rearrange('h p n -> (h p) n'), 8 * 128, ntok, bf).rearrange('(r h p) n -> r h p n', r=R, h=8)
    gkr = gather('G_KTr0', KTr.rearrange('h p n -> (h p) n'), 8 * 64, ntok, bf).rearrange('(r h p) n -> r h p n', r=R, h=8)
    gv = gather('G_V0', Vt, ntok, 1024, bf).rearrange('(r t) m -> r t m', r=R)
    S.new_phase()
    hmid = dram('hmid0', [ntok, 1024], F32)
    build_B_mla_f(S, dict(base, QTn=QTn, QTr=QTr, G_KTn=gkn, G_KTr=gkr, G_V=gv, wo=X['wo0'], hin=hcur, hmid=hmid), 0, ntiles, R)
    S.new_phase()
    h1 = dram('h1', [ntok, 1024], F32)
    build_C(S, dict(base, norm2=X['norm2_0'], w13=X['w13_0'], w2=X['w2_0'], hin=hmid, hout=h1), 0, ntiles=ntiles)
    hcur = h1
    S.new_phase()
    QT = dram('QT1', [8, 128, ntok], bf)
    KT = dram('KT1', [2, 128, ntok], bf)
    Vt = dram('Vt1', [ntok, 256], bf)
    build_A_gqa(S, dict(base, norm1=X['norm1_1'], wqkv=X['wqkv1'], gq=X['gq1'], cos=X['cos128'], sin=X['sin128'], hin=hcur, QT=QT, KT=KT, Vt=Vt),
                1, ntiles=ntiles)
    gk = gather('G_KT1', KT.rearrange('h p n -> (h p) n'), 2 * 128, ntok, bf).rearrange('(r h p) n -> r h p n', r=R, h=2)
    gv = gather('G_V1', Vt, ntok, 256, bf).rearrange('(r t) m -> r t m', r=R)
    S.new_phase()
    hmid = dram('hmid1', [ntok, 1024], F32)
    build_B_gqa_f(S, dict(base, QT=QT, G_KT=gk, G_V=gv, wo=X['wo1'], hin=hcur, hmid=hmid), 1, ntiles, R)
    S.new_phase()
    h2 = dram('h2', [ntok, 1024], F32)
    build_C(S, dict(base, norm2=X['norm2_1'], w13=X['w13_1'], w2=X['w2_1'], hin=hmid, hout=h2), 1, ntiles=ntiles)
    hcur = h2
    S.new_phase()
    QT = dram('QT2', [8, 128, ntok], bf)
    KT = dram('KT2', [8, 128, ntok], bf)
    Vt = dram('Vt2', [ntok, 1024], bf)
    KTh = dram('KTh2', [8, 128, 512], bf)
    Vh = dram('Vh2', [512, 1024], bf)
    build_A_na(S, dict(base, norm1=X['norm1_2'], wqkv=X['wqkv2'], gq=X['gq2'], hin=hcur, QT=QT, KT=KT, Vt=Vt, KTh=KTh, Vh=Vh), 2, ntiles=ntiles)
    gk = gather('G_KT2', KTh.rearrange('h p n -> (h p) n'), 8 * 128, 512, bf).rearrange('(r h p) n -> r h p n', r=R, h=8)
    gv = gather('G_V2', Vh, 512, 1024, bf).rearrange('(r t) m -> r t m', r=R)
    S.new_phase()
    hmid = dram('hmid2', [ntok, 1024], F32)
    build_B_na_f(S, dict(base, QT=QT, KT=KT, Vt=Vt, G_KT=gk, G_V=gv, TT=X['TT'], maskK=X['maskK'], maskQ=X['maskQ'], hw=X['hw'],
                         wo=X['wo2'], hin=hcur, hmid=hmid), 2, ntiles, R)
    S.new_phase()
    h3 = dram('h3', [ntok, 1024], F32)
    build_C(S, dict(base, norm2=X['norm2_2'], w13=X['w13_2'], w2=X['w2_2'], hin=hmid, hout=h3), 2, ntiles=ntiles)
    hcur = h3
    S.new_phase()
    xt = [4, 2, 128, nlat]
    T = dict(QfT=dram('QfT', xt, bf), QbT=dram('QbT', xt, bf), KfT=dram('KfT', xt, bf), KbT=dram('KbT', xt, bf),
             Kf=dram('Kf', [ntok, 1024], bf), Kb=dram('Kb', [ntok, 1024], bf), Vt=dram('Vt3', [ntok, 2048], bf), Gs=dram('Gs', [nlat, 2048], bf))
    build_A_ret(S, dict(base, norm1=X['norm1_3'], wqkvg=X['wqkvg'], ldrep=X['ldrep'], E=X['E'], Ec=X['Ec'], cos=X['cos256'], sin=X['sin256'],
                        hin=hcur, **T), 3, ntiles=ntiles)
    S.new_phase()
    Lfb = dram('Lfb', [2, 4, 2, 128, 512], F32)
    Sc = dram('Sc', [2, 4, 2, 128, 512], F32)
    build_A2_ret(S, dict(base, ldrep=X['ldrep'], wmul=X['wmul'], Kf=T['Kf'], Kb=T['Kb'], Vt=T['Vt'], Lfb=Lfb, Sc=Sc), 3, ntiles=ntiles)
    gl = gather('G_L', Lfb.rearrange('d h f p n -> (d h f p) n'), 2048, 512, F32).rearrange('(r d h f p) n -> r d h f p n', r=R, d=2, h=4, f=2)
    S.new_phase()
    hmid = dram('hmid3', [nlat, 1024], F32)
    yf = dram('yf', [nlat, 2048], F32)
    build_B_ret(S, dict(base, ldrep=X['ldrep'], coefm=X['coefm'], G_L=gl, Sc=Sc, maskD=X['maskD'], onorm=X['onorm'], wo=X['wo3'], hin=hcur,
                        hmid=hmid, yf=yf, **T), 3, nchunks=N, nsrc=R + 1)
    S.new_phase()
    build_C(S, dict(base, norm2=X['norm2_3'], w13=X['w13_3'], w2=X['w2_3'], hin=hmid, hout=X['out']), 3, ntiles=N, nctx=0)


NPDT = {np.dtype(np.float32): F32, np.dtype(ml_dtypes.bfloat16): BF16}


def launch(build_fn, in_maps, outs, **kw):
    nc = bass.Bass("TRN2", target_bir_lowering=False)
    S = Sched(nc)
    D = {}
    for name, arr in in_maps[0].items():
        D[name] = nc.dram_tensor(name, list(arr.shape), NPDT[arr.dtype], kind="ExternalInput").ap()
    for name, (shape, dt) in outs.items():
        D[name] = nc.dram_tensor(name, list(shape), NPDT[np.dtype(dt)], kind="ExternalOutput").ap()
    build_fn(S, D, **kw)
    S.emit()
    res = run_bass_kernel_spmd(nc, in_maps, core_ids=list(range(len(in_maps))))
    return [{k: np.asarray(r[k]) for k in outs} for r in res.results]


def fm(vec):
    return np.ascontiguousarray(np.asarray(vec, np.float32).reshape(-1, 128).T)


IDENT = np.eye(128, dtype=np.float32)


GRID_W = 64


def rope_tables_np(dim, j):
    t = np.arange(j * 2048, (j + 1) * 2048)
    row = (t // GRID_W).astype(np.float32)
    col = (t % GRID_W).astype(np.float32)
    quarter = dim // 4
    inv_freq = (10000.0 ** (-np.arange(quarter, dtype=np.float32) / quarter)).astype(np.float32)
    ang = np.concatenate([row[:, None] * inv_freq, col[:, None] * inv_freq], axis=-1).astype(np.float32)
    c = np.cos(ang).astype(np.float32).reshape(16, 128, dim // 2).transpose(1, 0, 2)
    s = np.sin(ang).astype(np.float32).reshape(16, 128, dim // 2).transpose(1, 0, 2)
    return np.ascontiguousarray(c), np.ascontiguousarray(s)


def rep128(v):
    return np.ascontiguousarray(np.broadcast_to(np.asarray(v, np.float32).reshape(1, -1), (128, v.size)))


def run_layer_gqa(LA, d, L, hcat, modv, NC=8):
    j0 = L // 4
    in_maps = []
    for c in range(NC):
        b, j = c // 4, c % 4
        cs, sn = rope_tables_np(128, j)
        gq = np.concatenate([np.tile(d['gqa_q_norm'][j0], 8), np.tile(d['gqa_k_norm'][j0], 2)])
        in_maps.append(dict(ident=IDENT, modv=modv[c], norm1=fm(d['norm1'][L]), wqkv=d['gqa_w_qkv'][j0], gq=rep128(gq),
                            cos=cs, sin=sn, hin=hcat[c]))
    bf = ml_dtypes.bfloat16
    resA = LA(build_A_gqa, in_maps, dict(QT=((8, 128, 2304), bf), KT=((2, 128, 2304), bf), Vt=((2304, 256), bf)), layer=L)
    in_maps = []
    for c in range(NC):
        b, j = c // 4, c % 4
        grp = [resA[b * 4 + jj] for jj in range(4)]
        KTf = np.concatenate([resA[c]['KT'][:, :, 0:256]] + [g['KT'][:, :, 256:] for g in grp], axis=2)
        Vf = np.concatenate([resA[c]['Vt'][0:256]] + [g['Vt'][256:] for g in grp], axis=0)
        in_maps.append(dict(ident=IDENT, modv=modv[c], QT=resA[c]['QT'], KTf=np.ascontiguousarray(KTf), Vf=np.ascontiguousarray(Vf),
                            wo=d['gqa_w_o'][j0], hin=hcat[c]))
    resB = LA(build_B_gqa, in_maps, dict(hmid=((2304, 1024), np.float32)), layer=L)
    return [x['hmid'] for x in resB]


def run_ffn(LA, d, L, hmid, modv, NC=8, nctx=2):
    in_maps = []
    nrow = hmid[0].shape[0]
    for c in range(NC):
        in_maps.append(dict(ident=IDENT, modv=modv[c], norm2=fm(d['norm2'][L]), w13=d['ffn_w13'][L], w2=d['ffn_w2'][L], hin=hmid[c]))
    res = LA(build_C, in_maps, dict(hout=((nrow, 1024), np.float32)), layer=L, ntiles=nrow // 128, nctx=nctx)
    return [x['hout'] for x in res]


def run_mod(LA, d, NC=8):
    cin = np.ascontiguousarray(np.stack([fm(d['c'][0]), fm(d['c'][1]), fm(d['c_ctx'])], axis=-1))
    ncol = 6144 // NC
    in_maps = []
    for c in range(NC):
        in_maps.append(dict(ident=IDENT, cin=cin, mod_w=np.ascontiguousarray(d['mod_w'][:, :, c * ncol:(c + 1) * ncol]),
                            mod_b=np.ascontiguousarray(d['mod_b'][:, c * ncol:(c + 1) * ncol])))
    res = LA(build_M, in_maps, dict(modv=((3, 4, ncol), np.float32)), ncol=ncol)
    full = np.concatenate([r_['modv'] for r_ in res], axis=2)
    out = []
    for c in range(NC):
        b = c // 4
        out.append(np.ascontiguousarray(np.stack([full[b], full[2]], axis=1)))
    return out


def kernel_unfused(**inputs):
    d = {k: np.asarray(v) for k, v in inputs.items()}
    NC = 8
    modv = run_mod(launch, d, NC)
    hcat = []
    for c in range(NC):
        b, j = c // 4, c % 4
        hcat.append(np.ascontiguousarray(np.concatenate([d['ctx'][b], d['x'][b, j * 2048:(j + 1) * 2048]], 0).astype(np.float32)))
    hmid = run_layer_mla(launch, d, 0, hcat, modv)
    hcat = run_ffn(launch, d, 0, hmid, modv)
    hmid = run_layer_gqa(launch, d, 1, hcat, modv)
    hcat = run_ffn(launch, d, 1, hmid, modv)
    hmid = run_layer_na(launch, d, 2, hcat, modv)
    hcat = run_ffn(launch, d, 2, hmid, modv)
    hmid = run_layer_ret(launch, d, 3, hcat, modv)
    hfin = run_ffn(launch, d, 3, hmid, modv, nctx=0)
    out = np.zeros((2, 8192, 1024), np.float32)
    for c in range(NC):
        b, j = c // 4, c % 4
        out[b, j * 2048:(j + 1) * 2048] = hfin[c]
    return out


def na_tables(rpb):
    H = rpb.shape[0]
    kc = np.arange(64)[:, None]
    qc = np.arange(64)[None, :]
    cs = np.clip(qc - 8, 0, 48)
    colvalid = (kc >= cs) & (kc < cs + 16)
    crel = np.clip(kc - qc + 15, 0, 30)
    TT = np.zeros((H, 2, 64, 22, 64), np.float32)
    for a in range(2):
        for i2 in range(22):
            i = i2 - 3 - a
            if 0 <= i <= 14:
                blk = rpb[:, 14 - i][:, crel]
                blk = np.where(colvalid[None], blk, np.float32(-30000.0))
            else:
                blk = np.broadcast_to(np.where(colvalid, np.float32(0), np.float32(-30000.0))[None], (H, 64, 64))
            TT[:, a, :, i2, :] = blk
    return np.ascontiguousarray(TT.reshape(H, 128, 22 * 64)).astype(ml_dtypes.bfloat16)


def na_masks(row_base, rows_total, nqb=4):
    mQ = np.zeros((2, nqb, 8, 8, 64), np.float32)
    for qb in range(nqb):
        R = row_base + 8 * qb
        for c in range(8):
            kr0 = R - 4 + 2 * c
            for a in range(2):
                kr = kr0 + a
                for t in range(8):
                    qr = R + t
                    r0 = min(max(qr - 4, 0), rows_total - 8)
                    ok = (0 <= kr < rows_total) and (r0 <= kr < r0 + 8)
                    mQ[a, qb, c, t, :] = 0.0 if ok else -30000.0
    mK = np.zeros((2, 2, 64), np.float32)
    mK[0, 0] = 1.0
    mK[1, 1] = 1.0
    bf = ml_dtypes.bfloat16
    return np.ascontiguousarray(mQ.reshape(2, -1)).astype(bf), np.ascontiguousarray(mK.reshape(2, 128)).astype(bf)


def run_layer_na(LA, d, L, hcat, modv, NC=8, ntiles=18, rows_total=128, cores_per_batch=4):
    j0 = L // 4
    bf = ml_dtypes.bfloat16
    ntok = ntiles * 128
    nlat = ntok - 256
    rows_core = nlat // 64
    in_maps = []
    gq = np.concatenate([np.tile(d['na_q_norm'][j0], 16), np.tile(d['na_k_norm'][j0], 16)])
    for c in range(NC):
        in_maps.append(dict(ident=IDENT, modv=modv[c], norm1=fm(d['norm1'][L]), wqkv=d['na_w_qkv'][j0], gq=rep128(gq), hin=hcat[c]))
    resA = LA(build_A_na, in_maps, dict(QT=((8, 128, ntok), bf), KT=((8, 128, ntok), bf), Vt=((ntok, 1024), bf)), layer=L, ntiles=ntiles)
    TT = na_tables(np.asarray(d['na_rpb'][j0]))
    in_maps = []
    for c in range(NC):
        b, j = c // cores_per_batch, c % cores_per_batch
        grp = [resA[b * cores_per_batch + jj] for jj in range(cores_per_batch)]
        Kall = np.concatenate([g['KT'][:, :, 256:] for g in grp], axis=2)
        Vall = np.concatenate([g['Vt'][256:] for g in grp], axis=0)
        row_base = j * rows_core
        KTl = np.zeros((8, 128, 2560), bf)
        Vl = np.zeros((2560, 1024), bf)
        lo = row_base - 4
        for lr in range(40):
            gr = lo + lr
            if 0 <= gr < rows_total:
                KTl[:, :, lr * 64:(lr + 1) * 64] = Kall[:, :, gr * 64:(gr + 1) * 64]
                Vl[lr * 64:(lr + 1) * 64] = Vall[gr * 64:(gr + 1) * 64]
        mQ, mK = na_masks(row_base, rows_total, nqb=4)
        in_maps.append(dict(ident=IDENT, modv=modv[c], QT=resA[c]['QT'], KTc=np.ascontiguousarray(resA[c]['KT'][:, :, 0:256]), KTl=KTl,
                            Vc=np.ascontiguousarray(resA[c]['Vt'][0:256]), Vl=Vl, TT=TT, maskK=mK, maskQ=mQ,
                            wo=d['na_w_o'][j0], hin=hcat[c]))
    resB = LA(build_B_na, in_maps, dict(hmid=((ntok, 1024), np.float32)), layer=L, ntiles=ntiles)
    return [x['hmid'] for x in resB]


def run_layer_mla(LA, d, L, hcat, modv, NC=8, ntiles=18, cores_per_batch=4):
    j0 = L // 4
    bf = ml_dtypes.bfloat16
    ntok = ntiles * 128
    lg = np.concatenate([d['mla_q_lora_norm'][j0], d['mla_kv_lora_norm'][j0]])
    in_maps = []
    for c in range(NC):
        b, j = c // cores_per_batch, c % cores_per_batch
        cs, sn = rope_tables_np(64, j)
        in_maps.append(dict(ident=IDENT, modv=modv[c], norm1=fm(d['norm1'][L]), wdown=d['mla_w_down'][j0], wuq=d['mla_w_uq'][j0],
                            wukv=d['mla_w_ukv'][j0], lg=fm(lg), gq=rep128(np.tile(d['mla_q_norm'][j0], 8)), gk=rep128(d['mla_k_norm'][j0]),
                            cos=cs, sin=sn, hin=hcat[c]))
    resA = LA(build_A_mla, in_maps, dict(QTn=((8, 128, ntok), bf), QTr=((8, 64, ntok), bf), KTn=((8, 128, ntok), bf),
                                         KTr=((8, 64, ntok), bf), Vt=((ntok, 1024), bf)), layer=L, ntiles=ntiles)
    in_maps = []
    for c in range(NC):
        b, j = c // cores_per_batch, c % cores_per_batch
        grp = [resA[b * cores_per_batch + jj] for jj in range(cores_per_batch)]
        KTnf = np.concatenate([resA[c]['KTn'][:, :, 0:256]] + [g['KTn'][:, :, 256:] for g in grp], axis=2)
        KTrf = np.concatenate([resA[c]['KTr'][:, :, 0:256]] + [g['KTr'][:, :, 256:] for g in grp], axis=2)
        Vf = np.concatenate([resA[c]['Vt'][0:256]] + [g['Vt'][256:] for g in grp], axis=0)
        in_maps.append(dict(ident=IDENT, modv=modv[c], QTn=resA[c]['QTn'], QTr=resA[c]['QTr'], KTnf=np.ascontiguousarray(KTnf),
                            KTrf=np.ascontiguousarray(KTrf), Vf=np.ascontiguousarray(Vf), wo=d['mla_w_o'][j0], hin=hcat[c]))
    nkeys = in_maps[0]['Vf'].shape[0]
    resB = LA(build_B_mla, in_maps, dict(hmid=((ntok, 1024), np.float32)), layer=L, ntiles=ntiles, nkeys=nkeys)
    return [x['hmid'] for x in resB]


def run_layer_ret(LA, d, L, hcat, modv, NC=8, ntiles=18, cores_per_batch=4):
    j0 = L // 4
    bf = ml_dtypes.bfloat16
    ntok = ntiles * 128
    N = ntiles - 2
    nlat = N * 128
    J = cores_per_batch
    p = np.arange(128, dtype=np.float32)
    E = np.stack([p, 127 - p, -p, -(127 - p)], axis=1).astype(np.float32)
    Ec = np.zeros((128, 2, 2), np.float32)
    for tt_ in range(2):
        m = tt_ * 128 + p
        Ec[:, tt_, 0] = 256 - m
        Ec[:, tt_, 1] = m + 1
    wmul = np.zeros((128, 2, ntiles), np.float32)
    for n in range(N):
        wmul[:, 0, 2 + n] = 128 * (N - n)
        wmul[:, 1, 2 + n] = 128 * (n + 1)
    ldrep = np.ascontiguousarray(np.broadcast_to(np.stack([d['ret_log_decay_fwd'][j0], d['ret_log_decay_bwd'][j0]])[None], (128, 2, 4))).astype(np.float32)
    tq = np.arange(128)
    maskD = np.stack([(tq[:, None] <= tq[None, :]), (tq[:, None] >= tq[None, :])]).astype(np.float32).astype(bf)
    in_maps = []
    for c in range(NC):
        b, j = c // J, c % J
        cs, sn = rope_tables_np(256, j)
        in_maps.append(dict(ident=IDENT, modv=modv[c], norm1=fm(d['norm1'][L]), wqkvg=d['ret_w_qkvg'][j0], ldrep=ldrep, E=E, Ec=Ec,
                            cos=cs[:, :N], sin=sn[:, :N], hin=hcat[c]))
        in_maps[-1]['cos'] = np.ascontiguousarray(in_maps[-1]['cos'])
        in_maps[-1]['sin'] = np.ascontiguousarray(in_maps[-1]['sin'])
    xt = (4, 2, 128, nlat)
    resA = LA(build_A_ret, in_maps, dict(QfT=(xt, bf), QbT=(xt, bf), KfT=(xt, bf), KbT=(xt, bf), Kf=((ntok, 1024), bf), Kb=((ntok, 1024), bf),
                                         Vt=((ntok, 2048), bf), Gs=((nlat, 2048), bf)), layer=L, ntiles=ntiles)
    in_maps = [dict(ident=IDENT, ldrep=ldrep, wmul=wmul, Kf=resA[c]['Kf'], Kb=resA[c]['Kb'], Vt=resA[c]['Vt']) for c in range(NC)]
    st = (4, 2, 128, 512)
    resA2 = LA(build_A2_ret, in_maps, dict(Scf=(st, np.float32), Scb=(st, np.float32), Lf=(st, np.float32), Lb=(st, np.float32)), layer=L, ntiles=ntiles)
    in_maps = []
    for c in range(NC):
        b, j = c // J, c % J
        grp = [resA2[b * J + jj] for jj in range(J)]
        Lall = np.zeros((J + 1, 2, 4, 2, 128, 512), np.float32)
        for i in range(J):
            Lall[i, 0] = grp[i]['Lf']
            Lall[i, 1] = grp[i]['Lb']
        Lall[J, 0] = resA2[c]['Scf']
        Lall[J, 1] = resA2[c]['Scb']
        cm = np.zeros((128, 2, J + 1, 2), np.float32)
        for i in range(J):
            if i < j:
                cm[:, 0, i] = (128 * N * (j - 1 - i), 1.0)
            if i > j:
                cm[:, 1, i] = (128 * N * (i - 1 - j), 1.0)
        cm[:, 0, J] = (128 * N * j, 1.0)
        cm[:, 1, J] = (128 * N * (J - 1 - j), 1.0)
        a = resA[c]
        in_maps.append(dict(ident=IDENT, modv=modv[c], ldrep=ldrep, coefm=cm, Lall=Lall, maskD=maskD, onorm=rep128(d['ret_out_norm'][j0]),
                            wo=d['ret_w_o'][j0], QfT=a['QfT'], QbT=a['QbT'], KfT=a['KfT'], KbT=a['KbT'], Kf=a['Kf'], Kb=a['Kb'], Vt=a['Vt'],
                            Gs=a['Gs'], hin=hcat[c]))
    resB = LA(build_B_ret, in_maps, dict(hmid=((nlat, 1024), np.float32), yf=((nlat, 2048), np.float32)), layer=L, nchunks=N, nsrc=J + 1)
    return [x['hmid'] for x in resB]


def rope_tables_gen(dim, j, nlat):
    t = np.arange(j * nlat, (j + 1) * nlat)
    row = (t // GRID_W).astype(np.float32)
    col = (t % GRID_W).astype(np.float32)
    quarter = dim // 4
    inv_freq = (10000.0 ** (-np.arange(quarter, dtype=np.float32) / quarter)).astype(np.float32)
    ang = np.concatenate([row[:, None] * inv_freq, col[:, None] * inv_freq], axis=-1).astype(np.float32)
    nt = nlat // 128
    c = np.cos(ang).astype(np.float32).reshape(nt, 128, dim // 2).transpose(1, 0, 2)
    s = np.sin(ang).astype(np.float32).reshape(nt, 128, dim // 2).transpose(1, 0, 2)
    return np.ascontiguousarray(c), np.ascontiguousarray(s)


def fused_inputs(d, NC=8, ntiles=18, R=4):
    bf = ml_dtypes.bfloat16
    ntok = ntiles * 128
    nlat = ntok - 256
    N = ntiles - 2
    rows_core = nlat // 64
    rows_total = rows_core * R
    f32 = lambda a: np.ascontiguousarray(np.asarray(a, np.float32))
    common = dict(ident=IDENT)
    for L in range(4):
        common['norm1_%d' % L] = fm(d['norm1'][L])
        common['norm2_%d' % L] = fm(d['norm2'][L])
        common['w13_%d' % L] = f32(d['ffn_w13'][L])
        common['w2_%d' % L] = f32(d['ffn_w2'][L])
    lg = np.concatenate([d['mla_q_lora_norm'][0], d['mla_kv_lora_norm'][0]])
    common.update(wdown=f32(d['mla_w_down'][0]), wuq=f32(d['mla_w_uq'][0]), wukv=f32(d['mla_w_ukv'][0]), lg=fm(lg),
                  gq0=rep128(np.tile(d['mla_q_norm'][0], 8)), gk0=rep128(d['mla_k_norm'][0]), wo0=f32(d['mla_w_o'][0]))
    common.update(wqkv1=f32(d['gqa_w_qkv'][0]), gq1=rep128(np.concatenate([np.tile(d['gqa_q_norm'][0], 8), np.tile(d['gqa_k_norm'][0], 2)])),
                  wo1=f32(d['gqa_w_o'][0]))
    common.update(wqkv2=f32(d['na_w_qkv'][0]), gq2=rep128(np.concatenate([np.tile(d['na_q_norm'][0], 16), np.tile(d['na_k_norm'][0], 16)])),
                  TT=na_tables(np.asarray(d['na_rpb'][0])), wo2=f32(d['na_w_o'][0]))
    p = np.arange(128, dtype=np.float32)
    E = np.stack([p, 127 - p, -p, -(127 - p)], axis=1).astype(np.float32)
    Ec = np.zeros((128, 2, 2), np.float32)
    for tt_ in range(2):
        m = tt_ * 128 + p
        Ec[:, tt_, 0] = 256 - m
        Ec[:, tt_, 1] = m + 1
    wmul = np.zeros((128, 2, ntiles), np.float32)
    for n in range(N):
        wmul[:, 0, 2 + n] = 128 * (N - n)
        wmul[:, 1, 2 + n] = 128 * (n + 1)
    ldrep = np.ascontiguousarray(np.broadcast_to(np.stack([d['ret_log_decay_fwd'][0], d['ret_log_decay_bwd'][0]])[None], (128, 2, 4))).astype(np.float32)
    tq = np.arange(128)
    maskD = np.stack([(tq[:, None] <= tq[None, :]), (tq[:, None] >= tq[None, :])]).astype(np.float32).astype(bf)
    common.update(wqkvg=f32(d['ret_w_qkvg'][0]), ldrep=ldrep, E=E, Ec=Ec, wmul=wmul, maskD=maskD, onorm=rep128(d['ret_out_norm'][0]),
                  wo3=f32(d['ret_w_o'][0]))
    in_maps = []
    for c in range(NC):
        b, j = c // R, c % R
        m = dict(common)
        m['hin'] = np.ascontiguousarray(np.concatenate([d['ctx'][b], d['x'][b, j * nlat:(j + 1) * nlat]], 0).astype(np.float32))
        m['cin'] = np.ascontiguousarray(np.stack([fm(d['c'][b]), fm(d['c_ctx'])], axis=-1))
        m['mod_w'] = np.ascontiguousarray(d['mod_w'][:, :, j * 1536:(j + 1) * 1536])
        m['mod_b'] = np.ascontiguousarray(d['mod_b'][:, j * 1536:(j + 1) * 1536])
        for dim in (64, 128, 256):
            cs, sn = rope_tables_gen(dim, j, nlat)
            m['cos%d' % dim] = cs
            m['sin%d' % dim] = sn
        mQ, mK = na_masks(j * rows_core, rows_total, nqb=nlat // 512)
        m['maskQ'] = mQ
        m['maskK'] = mK
        hw = np.zeros((128, 2, R), np.float32)
        if j - 1 >= 0:
            hw[:, 0, j - 1] = 1.0
        if j + 1 < R:
            hw[:, 1, j + 1] = 1.0
        m['hw'] = hw
        cm = np.zeros((128, 2, R + 1, 2), np.float32)
        for i in range(R):
            if i < j:
                cm[:, 0, i] = (128 * N * (j - 1 - i), 1.0)
            if i > j:
                cm[:, 1, i] = (128 * N * (i - 1 - j), 1.0)
        cm[:, 0, R] = (128 * N * j, 1.0)
        cm[:, 1, R] = (128 * N * (R - 1 - j), 1.0)
        m['coefm'] = cm
        in_maps.append(m)
    return in_maps


ARENA_BYTES = 200 * 1024


def build_program(in_map0, ntiles, R, groups):
    nc = bass.Bass("TRN2", target_bir_lowering=False)
    S = Sched(nc)
    S.use_arena(ARENA_BYTES)
    X = {}
    for name, arr in in_map0.items():
        X[name] = nc.dram_tensor(name, list(arr.shape), NPDT[arr.dtype], kind="ExternalInput").ap()
    X['out'] = nc.dram_tensor('out', [(ntiles - 2) * 128, 1024], F32, kind="ExternalOutput").ap()
    build_fused(S, X, nc, ntiles=ntiles, R=R, groups=groups)
    S.emit()
    return nc, S


def kernel(**inputs):
    d = {k: np.asarray(v) for k, v in inputs.items()}
    NC, R, ntiles = 8, 4, 18
    in_maps = fused_inputs(d, NC, ntiles, R)
    nc, S = build_program(in_maps[0], ntiles, R, GROUPS)
    res = run_bass_kernel_spmd(nc, in_maps, core_ids=list(range(NC)))
    nlat = (ntiles - 2) * 128
    out = np.zeros((2, R * nlat, 1024), np.float32)
    for c in range(NC):
        b, j = c // R, c % R
        out[b, j * nlat:(j + 1) * nlat] = np.asarray(res.results[c]['out'])
    return out
```

```python
import bisect
import ml_dtypes


import numpy as np
import concourse.bass as bass
import concourse.mybir as mybir
from concourse.bass_utils import run_bass_kernel_spmd

F32 = mybir.dt.float32
BF16 = mybir.dt.bfloat16
AF = mybir.ActivationFunctionType
ALU = mybir.AluOpType
AX = mybir.AxisListType

SEM_CHUNK = 30000
N_DMA_SEMS = 12


DTSIZE = {}


def _box(ap):
    t = ap.tensor
    dims = list(ap.ap)
    off = ap.offset
    shape = list(t.shape)
    cls = type(t).__name__
    if cls.startswith('DRam'):
        lo = off
        hi = off
        for st, cnt in dims:
            if st >= 0:
                hi += st * (cnt - 1)
            else:
                lo += st * (cnt - 1)
        return (0, 1, lo, hi + 1)
    row = 1
    for s in shape[1:]:
        row *= s
    if cls.startswith('PSum'):
        return (0, 128, 0, 1 << 20)
    pst, pcnt = dims[0]
    if pst == 0:
        pst = row
    p0 = off // row
    f0 = off % row
    assert pst % row == 0 or pcnt == 1, (pst, row, dims)
    p1 = p0 + (pst // row) * (pcnt - 1) + 1
    f1 = f0
    for st, cnt in dims[1:]:
        assert st >= 0
        f1 += st * (cnt - 1)
    if t.name == 'arena':
        z = mybir.dt.size(ap.dtype)
        return (p0, p1, f0 * z, (f1 + 1) * z)
    return (p0, p1, f0, f1 + 1)


def _ovl(a, b):
    return a[0] < b[1] and b[0] < a[1] and a[2] < b[3] and b[2] < a[3]


def _cov(a, b):
    return a[0] <= b[0] and a[1] >= b[1] and a[2] <= b[2] and a[3] >= b[3]


class Op:
    __slots__ = ('eng', 'idx', 'fn', 'deps', 'is_dma', 'signal', 'dsem', 'dval', 'dprev', 'sigcount')

    def __init__(self, eng, idx, fn, is_dma):
        self.eng = eng
        self.idx = idx
        self.fn = fn
        self.deps = []
        self.is_dma = is_dma
        self.signal = False
        self.dsem = None
        self.dval = 0
        self.dprev = None
        self.sigcount = 0


class Sched:
    ENGS = ('pe', 'act', 'dve', 'pool', 'sp')

    def __init__(self, nc, same_engine_sync=True):
        self.nc = nc
        self.ops = {e: [] for e in self.ENGS}
        self.rec = {}
        self.same = same_engine_sync
        self.wm = {e: {} for e in self.ENGS}
        self.dma_rr = {e: 0 for e in self.ENGS}
        self.dma_last = {}
        self.dma_cnt = {}
        self.out_dmas = []
        self.nalloc = 0
        self.final_names = None
        self.arena_bytes = 0
        self.arena = None
        self.arena_views = {}
        self.bump = 0
        self.phase = 0
        self.allocs = []
        self.banks = None
        self.nbank = 0
        self.ccs = []

    def use_arena(self, nbytes):
        self.arena_bytes = nbytes
        self.arena = self.nc.alloc_sbuf_tensor('arena', [128, nbytes // 2], BF16)
        self.arena_views = {BF16: self.arena, F32: self.arena.bitcast(F32)}
        self.banks = [self.nc.alloc_psum_tensor('bank%d' % i, [128, 512], F32) for i in range(8)]

    def new_phase(self):
        self.barrier()
        self.phase += 1
        self.bump = 0
        self.allocs = []
        self.nbank = 0

    def barrier(self):
        lasts = {}
        for e in self.ENGS:
            for o in reversed(self.ops[e]):
                if (not o.is_dma) and o.fn is not None and o.fn != 'final':
                    lasts[e] = o
                    break
        for e in self.ENGS:
            b = Op(e, len(self.ops[e]), None, False)
            for e2, o in lasts.items():
                if e2 != e:
                    self._add_dep(b, o)
            for key, o in self.dma_last.items():
                self._add_dep(b, o)
            for o in self.ccs:
                self._add_dep(b, o)
            self.ops[e].append(b)

    def sb(self, name, shape, dt):
        self.nalloc += 1
        if self.arena is None:
            return self.nc.alloc_sbuf_tensor('sb_' + name, list(shape), dt)
        z = mybir.dt.size(dt)
        n = 1
        for x in shape[1:]:
            n *= x
        nbytes = (n * z + 63) // 64 * 64
        off = self.bump
        self.bump += nbytes
        assert self.bump <= self.arena_bytes, ('arena overflow', name, self.bump)
        self.allocs.append((off, off + nbytes))
        h = self.arena_views[dt]
        row = self.arena_bytes // z
        dims = [[row, shape[0]]]
        st = n
        for x in shape[1:]:
            st //= x
            dims.append([st, x])
        return bass.AP(h, off // z, dims)

    def ps(self, name, shape, dt=F32):
        if self.banks is None:
            return self.nc.alloc_psum_tensor('ps_' + name, list(shape), dt)
        b = self.banks[self.nbank]
        self.nbank += 1
        assert self.nbank <= 8
        if dt == BF16:
            h = b.bitcast(BF16)
            return bass.AP(h, 0, [[1024, shape[0]], [1, shape[1]]])
        return bass.AP(b, 0, [[512, shape[0]], [1, shape[1]]])

    def _key(self, ap, bx):
        nm = ap.tensor.name
        if nm != 'arena':
            return nm
        i = bisect.bisect_right(self.allocs, (bx[2], 1 << 60)) - 1
        assert i >= 0 and self.allocs[i][0] <= bx[2] and bx[3] <= self.allocs[i][1], (bx, self.allocs[i] if i >= 0 else None)
        return 'a%d_%d' % (self.phase, i)

    def cc(self, kind, groups, in_ap, out_ap):
        o = Op('pool', len(self.ops['pool']), None, True)
        o.dsem = 'cc%d' % len(self.ccs)
        o.dval = 1
        o.fn = ('cc', kind, groups, in_ap, out_ap)
        self.dma_cnt[('pool', o.dsem)] = 1
        self._track(o, [in_ap], [out_ap])
        self.ops['pool'].append(o)
        self.ccs.append(o)
        return o

    def _add_dep(self, op, dep):
        if dep is op:
            return
        if dep.is_dma:
            key = ('d', dep.eng, dep.dsem)
            val = dep.dval
        else:
            if dep.eng == op.eng and not op.is_dma:
                if not self.same:
                    return
                if op.eng == 'pe':
                    return
            key = ('e', dep.eng)
            val = dep.idx
        w = self.wm[op.eng]
        if w.get(key, -1) >= val:
            return
        w[key] = val
        op.deps.append(dep)
        dep.signal = True

    def _track(self, op, aps_r, aps_w):
        for ap in aps_r:
            bx = _box(ap)
            nm = self._key(ap, bx)
            lst = self.rec.setdefault(nm, [])
            isps = type(ap.tensor).__name__.startswith('PSum')
            for r in lst:
                if r[2] and _ovl(r[0], bx):
                    self._add_dep(op, r[1])
                elif isps and (not r[2]) and r[1].eng != op.eng:
                    self._add_dep(op, r[1])
        for ap in aps_w:
            bx = _box(ap)
            nm = self._key(ap, bx)
            lst = self.rec.setdefault(nm, [])
            for r in lst:
                if _ovl(r[0], bx):
                    self._add_dep(op, r[1])
        for ap in aps_r:
            bx = _box(ap)
            nm = self._key(ap, bx)
            lst = self.rec[nm]
            new = []
            for r in lst:
                if (not r[2]) and r[1].eng == op.eng and (not r[1].is_dma) and (not op.is_dma) and _cov(bx, r[0]):
                    continue
                new.append(r)
            new.append([bx, op, False])
            self.rec[nm] = new
        for ap in aps_w:
            bx = _box(ap)
            nm = self._key(ap, bx)
            lst = self.rec[nm]
            new = [r for r in lst if not _cov(bx, r[0])]
            new.append([bx, op, True])
            self.rec[nm] = new

    def op(self, eng, fn, reads=(), writes=()):
        o = Op(eng, len(self.ops[eng]), fn, False)
        self._track(o, list(reads), list(writes))
        self.ops[eng].append(o)
        return o

    def dma(self, eng, out, in_, is_output=False):
        o = Op(eng, len(self.ops[eng]), None, True)
        slot = self.dma_rr[eng] % N_DMA_SEMS
        self.dma_rr[eng] += 1
        o.dsem = slot
        prev = self.dma_last.get((eng, slot))
        o.dprev = prev
        cnt = self.dma_cnt.get((eng, slot), 0) + 1
        self.dma_cnt[(eng, slot)] = cnt
        o.dval = cnt
        self.dma_last[(eng, slot)] = o
        o.fn = (out, in_)
        if prev is not None:
            w = self.wm[eng]
            key = ('d', eng, slot)
            if w.get(key, -1) < prev.dval:
                w[key] = prev.dval
                o.deps.append(prev)
        self._track(o, [in_], [out])
        self.ops[eng].append(o)
        if is_output and (self.final_names is None or out.tensor.name in self.final_names):
            self.out_dmas.append(o)
        return o

    def emit(self):
        nc = self.nc
        engobj = {'pe': nc.tensor, 'act': nc.scalar, 'dve': nc.vector, 'pool': nc.gpsimd, 'sp': nc.sync}
        fin = Op('sp', len(self.ops['sp']), 'final', False)
        for d in self.out_dmas:
            fin.deps.append(d)
        self.ops['sp'].append(fin)
        sems = {}
        nsig = {}
        for e in self.ENGS:
            k = 0
            for o in self.ops[e]:
                if (not o.is_dma) and o.signal:
                    k += 1
                    o.sigcount = k
            nsig[e] = k
            sems[e] = [nc.alloc_semaphore(f'sem_{e}_{i}') for i in range((k + SEM_CHUNK - 1) // SEM_CHUNK)]
        dsems = {}
        for (e, slot) in self.dma_cnt:
            dsems[(e, slot)] = nc.alloc_semaphore(f'dsem_{e}_{slot}')
        self.stats = {e: (len(self.ops[e]), nsig[e]) for e in self.ENGS}

        def emit_engine(e, eng):
            for o in self.ops[e]:
                for d in o.deps:
                    if d.is_dma:
                        mult = 1 if isinstance(d.dsem, str) else 16
                        eng.wait_ge(dsems[(d.eng, d.dsem)], mult * d.dval)
                    else:
                        k = d.sigcount - 1
                        eng.wait_ge(sems[d.eng][k // SEM_CHUNK], k % SEM_CHUNK + 1)
                if o.fn == 'final' or o.fn is None:
                    continue
                if o.is_dma and isinstance(o.fn[0], str):
                    _, kind, groups, in_ap, out_ap = o.fn
                    eng.collective_compute(kind, ALU.bypass, groups, [in_ap], [out_ap]).then_inc(dsems[(e, o.dsem)])
                    continue
                if o.is_dma:
                    out, in_ = o.fn
                    eng.dma_start(out=out, in_=in_, allow_slow_non_contiguous=True).then_inc(dsems[(e, o.dsem)], 16)
                else:
                    ins = o.fn(eng)
                    if o.signal:
                        k = o.sigcount - 1
                        ins.then_inc(sems[e][k // SEM_CHUNK], 1)

        with nc.Block() as block:
            @block.tensor
            def _(eng):
                emit_engine('pe', eng)

            @block.scalar
            def _(eng):
                emit_engine('act', eng)

            @block.vector
            def _(eng):
                emit_engine('dve', eng)

            @block.gpsimd
            def _(eng):
                emit_engine('pool', eng)

            @block.sync
            def _(eng):
                emit_engine('sp', eng)


def V(t, off, *dims, p0=0, npart=128):
    if isinstance(t, bass.AP):
        row = t.ap[0][0]
        return bass.AP(t.tensor, t.offset + p0 * row + off, [[row, npart]] + [list(d) for d in dims])
    row = 1
    for s in list(t.shape)[1:]:
        row *= s
    return bass.AP(t, p0 * row + off, [[row, npart]] + [list(d) for d in dims])


def o_tt(S, eng, out, in0, in1, op):
    return S.op(eng, lambda e: e.tensor_tensor(out=out, in0=in0, in1=in1, op=op), [in0, in1], [out])


def o_ts(S, eng, out, in0, s1, s2, op0, op1=None):
    rd = [in0] + [x for x in (s1, s2) if isinstance(x, bass.AP)]
    if op1 is None:
        return S.op(eng, lambda e: e.tensor_scalar(out=out, in0=in0, scalar1=s1, scalar2=None, op0=op0), rd, [out])
    return S.op(eng, lambda e: e.tensor_scalar(out=out, in0=in0, scalar1=s1, scalar2=s2, op0=op0, op1=op1), rd, [out])


def o_stt(S, eng, out, in0, scalar, in1, op0, op1):
    rd = [in0, in1] + ([scalar] if isinstance(scalar, bass.AP) else [])
    return S.op(eng, lambda e: e.scalar_tensor_tensor(out=out, in0=in0, scalar=scalar, in1=in1, op0=op0, op1=op1), rd, [out])


def o_act(S, out, in_, func, bias=None, scale=1.0, accum=None):
    rd = [in_] + [x for x in (bias, scale) if isinstance(x, bass.AP)]
    wr = [out] + ([accum] if accum is not None else [])
    kw = {}
    if bias is not None:
        kw['bias'] = bias
    if accum is not None:
        kw['accum_out'] = accum
    return S.op('act', lambda e: e.activation(out=out, in_=in_, func=func, scale=scale, **kw), rd, wr)


def o_copy(S, eng, out, in_):
    if eng == 'act':
        return S.op('act', lambda e: e.copy(out=out, in_=in_), [in_], [out])
    return S.op(eng, lambda e: e.tensor_copy(out=out, in_=in_), [in_], [out])


def o_red(S, eng, out, in_, op=None):
    op = op or ALU.add
    return S.op(eng, lambda e: e.tensor_reduce(out=out, in_=in_, axis=AX.X, op=op), [in_], [out])


def o_recip(S, out, in_):
    return S.op('dve', lambda e: e.reciprocal(out=out, in_=in_), [in_], [out])


def o_memset(S, eng, out, val):
    return S.op(eng, lambda e: e.memset(out, val), [], [out])


def o_mm(S, out, lhsT, rhs, start, stop):
    return S.op('pe', lambda e: e.matmul(out, lhsT=lhsT, rhs=rhs, start=start, stop=stop), [lhsT, rhs], [out])


def o_tr(S, out, in_, ident):
    return S.op('pe', lambda e: e.transpose(out=out, in_=in_, identity=ident), [in_, ident], [out])


class Ctx:
    def __init__(self, S, D, npT=2):
        self.S = S
        self.identf = S.sb('identf', [128, 128], F32)
        self.ident = S.sb('identb', [128, 128], BF16)
        self.eps = S.sb('eps', [128, 1], F32)
        self.ones = S.sb('ones', [128, 128], BF16)
        self.junk = S.sb('junk', [128, 1024], F32)
        S.dma('sp', self.identf[:, :], D['ident'])
        o_copy(S, 'dve', self.ident[:, :], self.identf[:, :])
        o_memset(S, 'dve', self.eps[:, :], 1e-6)
        o_memset(S, 'dve', self.ones[:, :], 1.0)
        self.pT = [S.ps('pT%d' % i, [128, 1024], BF16) for i in range(npT)]
        self.nT = 0
        self.evac_rr = 0

    def next_pT(self):
        t = self.pT[self.nT % 2]
        self.nT += 1
        return t

    def evac_eng(self):
        self.evac_rr += 1
        return 'act' if self.evac_rr % 2 else 'dve'


def rstd_rows(S, C, ss, out, n, inv_d):
    o_act(S, out, ss, AF.Sqrt, bias=C.eps[:, 0:1], scale=inv_d)
    o_recip(S, out, out)


def norm_T(S, C, hsrc, xT_dst, Gp, Sh, scr):
    ss = scr['ss']
    rs = scr['rs']
    xs = scr['xs']
    o_act(S, C.junk[:, :], hsrc, AF.Square, scale=1.0 / 32.0, accum=ss[:, 0:1])
    rstd_rows(S, C, ss[:, 0:1], rs[:, 0:1], 1, 1.0)
    o_act(S, xs[:, :], hsrc, AF.Copy, scale=rs[:, 0:1])
    pT = C.next_pT()
    for k in range(8):
        o_tr(S, pT[:, k * 128:(k + 1) * 128], xs[:, k * 128:(k + 1) * 128], C.ident[:, :])
    tmp = scr['xtmp']
    gb = bass.AP(Gp.tensor, Gp.offset, [list(Gp.ap[0]), [1, 8], [0, 128]])
    sb_ = bass.AP(Sh.tensor, Sh.offset, [list(Sh.ap[0]), [1, 8], [0, 128]])
    pv = V(pT, 0, (128, 8), (1, 128))
    tv = V(tmp, 0, (128, 8), (1, 128))
    o_tt(S, 'dve', tv, pv, gb, ALU.mult)
    o_tt(S, 'pool', xT_dst, tv, sb_, ALU.add)


def load_mod(S, D, layer, scr_name, which, norm_ap):
    t = S.sb(scr_name, [128, 2, 2, 8], F32)
    raw = S.sb(scr_name + '_raw', [128, 2, 2, 8], F32)
    nrm = S.sb(scr_name + '_n', [128, 8], F32)
    modv = D['modv']
    S.dma('sp', nrm[:, :], norm_ap)
    base = which * 3 * 1024
    for s in range(2):
        S.dma('sp', raw[:, s, 1, :], modv[layer, s, base:base + 1024].rearrange('(k p) -> p k', p=128))
        S.dma('sp', raw[:, s, 0, :], modv[layer, s, base + 1024:base + 2048].rearrange('(k p) -> p k', p=128))
        o_stt(S, 'dve', t[:, s, 0, :], raw[:, s, 0, :], 1.0, nrm[:, :], ALU.add, ALU.mult)
        o_copy(S, 'dve', t[:, s, 1, :], raw[:, s, 1, :])
    return t


def load_gate_rep(S, D, layer, name, which):
    t = S.sb(name, [128, 2, 1024], F32)
    modv = D['modv']
    base = (which * 3 + 2) * 1024
    for s in range(2):
        S.dma('sp', t[:, s, :], modv[layer, s, base:base + 1024].partition_broadcast(128))
    return t


def build_M(S, D, ncol=768, nrow=3):
    C = Ctx(S, D)
    cin = S.sb('cin', [128, 8, nrow], F32)
    sc = S.sb('sc', [128, 8, nrow], BF16)
    S.dma('sp', cin[:, :, :], D['cin'])
    scf = S.sb('scf', [128, 8, nrow], F32)
    o_act(S, scf[:, :, :], cin[:, :, :], AF.Silu)
    o_copy(S, 'dve', sc[:, :, :], scf[:, :, :])
    bias = S.sb('mbias', [nrow, 4, ncol], F32)
    res = S.sb('mres', [nrow, 4, ncol], F32)
    for s in range(nrow):
        S.dma('sp', bias[s:s + 1, :, :], D['mod_b'].partition_broadcast(1))
    wt = [S.sb('mw%d' % i, [128, 8, ncol], BF16) for i in range(2)]
    pM = [S.ps('pM%d' % i, [128, 512], F32) for i in range(2)]
    n = 0
    for i in range(4):
        w = wt[i % 2]
        S.dma('pool', w[:, :, :], D['mod_w'][i].rearrange('(k p) m -> p k m', p=128))
        c0 = 0
        while c0 < ncol:
            wd = min(512, ncol - c0)
            p = pM[n % 2]
            n += 1
            for k in range(8):
                o_mm(S, p[0:nrow, 0:wd], sc[:, k, :], w[:, k, c0:c0 + wd], k == 0, k == 7)
            o_tt(S, 'dve', res[0:nrow, i, c0:c0 + wd], p[0:nrow, 0:wd], bias[0:nrow, i, c0:c0 + wd], ALU.add)
            c0 += wd
    S.dma('sp', D['modv'], res[0:nrow, :, :], is_output=True)


def norm_a(S, C, hsrc, xs, ss, rs):
    o_act(S, C.junk[:, :], hsrc, AF.Square, scale=1.0 / 32.0, accum=ss)
    rstd_rows(S, C, ss, rs, 1, 1.0)
    o_act(S, xs, hsrc, AF.Copy, scale=rs)


def norm_b(S, C, xs, xT_dst, Gp, Sh, tmp):
    pT = C.next_pT()
    for k in range(8):
        o_tr(S, pT[:, k * 128:(k + 1) * 128], xs[:, k * 128:(k + 1) * 128], C.ident[:, :])
    gb = bass.AP(Gp.tensor, Gp.offset, [list(Gp.ap[0]), [1, 8], [0, 128]])
    sb_ = bass.AP(Sh.tensor, Sh.offset, [list(Sh.ap[0]), [1, 8], [0, 128]])
    pv = V(pT, 0, (128, 8), (1, 128))
    tv = V(tmp, 0, (128, 8), (1, 128))
    o_tt(S, 'dve', tv, pv, gb, ALU.mult)
    o_tt(S, 'pool', xT_dst, tv, sb_, ALU.add)


def build_C(S, D, layer, ntiles=18, nctx=2):
    C = Ctx(S, D)
    md = load_mod(S, D, layer, 'md2', 1, D['norm2'])
    gate = load_gate_rep(S, D, layer, 'g2rep', 1)
    w13 = S.sb('w13', [128, 8, 2, 1408], BF16)
    w2 = S.sb('w2', [128, 11, 1024], BF16)
    ntok_all = ntiles * 128
    xTall = S.sb('xTall', [128, 8, ntok_all], BF16)
    ss = [S.sb('ss%d' % i, [128, 1], F32) for i in range(2)]
    rs = [S.sb('rs%d' % i, [128, 1], F32) for i in range(2)]
    xs = [S.sb('xs%d' % i, [128, 1024], BF16) for i in range(2)]
    xtmp = S.sb('xtmp', [128, 1024], F32)
    ht = [S.sb('ht%d' % i, [128, 1024], F32) for i in range(4)]
    hacc = [S.sb('hacc%d' % i, [128, 1024], F32) for i in range(2)]
    gT = S.sb('gT', [128, 11, 512], BF16)
    s1 = [S.sb('s1_%d' % i, [128, 512], F32) for i in range(2)]
    otmp = [S.sb('otmp%d' % i, [128, 512], F32) for i in range(2)]
    ho = [S.sb('ho%d' % i, [128, 1024], F32) for i in range(2)]
    pA = [S.ps('pA%d' % i, [128, 512], F32) for i in range(4)]
    pO = [S.ps('pO%d' % i, [128, 512], F32) for i in range(2)]
    hin = D['hin']
    hout = D['hout']
    nblk = (ntiles + 3) // 4
    na = 0
    no = 0

    def nA(t):
        S.dma('sp', ht[t % 4][:, :], hin[t * 128:(t + 1) * 128, :])
        norm_a(S, C, ht[t % 4][:, :], xs[t % 2][:, :], ss[t % 2][:, 0:1], rs[t % 2][:, 0:1])

    def nB(t):
        s = 1 if t < nctx else 0
        norm_b(S, C, xs[t % 2], V(xTall, t * 128, (ntok_all, 8), (1, 128)), md[:, s, 0, :], md[:, s, 1, :], xtmp)

    def blk_tiles(b):
        return list(range(b * 4, min(ntiles, b * 4 + 4)))

    for ps_ in range(2):
        for k in range(8):
            for u in range(2):
                S.dma('pool', w13[:, k, u, :], D['w13'][k * 128:(k + 1) * 128, u * 2816 + ps_ * 1408:u * 2816 + (ps_ + 1) * 1408])
        S.dma('pool', w2[:, :, :], D['w2'][ps_ * 1408:(ps_ + 1) * 1408, :].rearrange('(f p) m -> p f m', p=128))
        if ps_ == 0:
            for t in blk_tiles(0):
                nA(t)
                nB(t)
        for b in range(nblk):
            tl = blk_tiles(b)
            t0 = tl[0]
            nt = len(tl)
            ntok = nt * 128
            c0 = t0 * 128
            nxt = blk_tiles(b + 1) if (ps_ == 0 and b + 1 < nblk) else []
            for f in range(11):
                p1 = pA[na % 4]
                p3 = pA[(na + 1) % 4]
                na += 2
                for k in range(8):
                    o_mm(S, p1[:, 0:ntok], w13[:, k, 0, f * 128:(f + 1) * 128], xTall[:, k, c0:c0 + ntok], k == 0, k == 7)
                for k in range(8):
                    o_mm(S, p3[:, 0:ntok], w13[:, k, 1, f * 128:(f + 1) * 128], xTall[:, k, c0:c0 + ntok], k == 0, k == 7)
                if f % 2 == 1 and f // 2 < len(nxt):
                    nA(nxt[f // 2])
                if f % 2 == 0 and f >= 2 and f // 2 - 1 < len(nxt):
                    nB(nxt[f // 2 - 1])
                st = s1[f % 2]
                o_act(S, st[:, 0:ntok], p1[:, 0:ntok], AF.Silu)
                o_tt(S, 'dve', gT[:, f, 0:ntok], p3[:, 0:ntok], st[:, 0:ntok], ALU.mult)
            for j in range(nt):
                s = 1 if (t0 + j) < nctx else 0
                ha = hacc[no % 2]
                hob = ho[no % 2]
                src = hin if ps_ == 0 else hout
                S.dma('sp', ha[:, :], src[(t0 + j) * 128:(t0 + j + 1) * 128, :])
                for hf in range(2):
                    p = pO[(2 * no + hf) % 2]
                    ot = otmp[(2 * no + hf) % 2]
                    for f in range(11):
                        o_mm(S, p[:, :], gT[:, f, j * 128:(j + 1) * 128], w2[:, f, hf * 512:(hf + 1) * 512], f == 0, f == 10)
                    o_tt(S, 'dve', ot[:, :], p[:, :], gate[:, s, hf * 512:(hf + 1) * 512], ALU.mult)
                    o_tt(S, 'pool', hob[:, hf * 512:(hf + 1) * 512], ot[:, :], ha[:, hf * 512:(hf + 1) * 512], ALU.add)
                no += 1
                S.dma('sp', hout[(t0 + j) * 128:(t0 + j + 1) * 128, :], hob[:, :], is_output=True)


def proj_tm(S, C, pP, np_, xTt, W, ncols, Psb):
    c0 = 0
    while c0 < ncols:
        w = min(512, ncols - c0)
        p = pP[np_[0] % len(pP)]
        np_[0] += 1
        for k in range(8):
            o_mm(S, p[:, 0:w], xTt[:, k, :], W[:, k, c0:c0 + w], k == 0, k == 7)
        o_copy(S, C.evac_eng(), Psb[:, c0:c0 + w], p[:, 0:w])
        c0 += w


def head_norm(S, C, src, nh, d, gain_rep, dst, scr):
    (st, so) = src
    (dt_, do) = dst
    sq = scr['sq']
    ss = scr['ssh']
    rs = scr['rsh']
    sv = V(st, so, (d, nh), (1, d))
    qv = V(sq, 0, (d, nh), (1, d))
    o_tt(S, 'pool', qv, sv, sv, ALU.mult)
    o_red(S, 'dve', ss[:, 0:nh], qv)
    rstd_rows(S, C, ss[:, 0:nh], rs[:, 0:nh], nh, 1.0 / d)
    rb = V(rs, 0, (1, nh), (0, d))
    o_tt(S, 'dve', qv, sv, rb, ALU.mult)
    gv = V(gain_rep, 0, (d, nh), (1, d))
    dv_ = V(dt_, do, (d, nh), (1, d))
    o_tt(S, 'pool', dv_, qv, gv, ALU.mult)


def rope_tm(S, src, nh, d, cos, sin, dst, scr):
    (st, so) = src
    (dt_, do) = dst
    hd = d // 2
    x1 = V(st, so, (d, nh), (1, hd))
    x2 = V(st, so + hd, (d, nh), (1, hd))
    o1 = V(dt_, do, (d, nh), (1, hd))
    o2 = V(dt_, do + hd, (d, nh), (1, hd))
    cb = bass.AP(cos.tensor, cos.offset, [list(cos.ap[0]), [0, nh], [1, hd]])
    sb_ = bass.AP(sin.tensor, sin.offset, [list(sin.ap[0]), [0, nh], [1, hd]])
    ta = V(scr['ra'], 0, (hd, nh), (1, hd))
    tb = V(scr['rb'], 0, (hd, nh), (1, hd))
    tc = V(scr['rc'], 0, (hd, nh), (1, hd))
    td = V(scr['rd'], 0, (hd, nh), (1, hd))
    o_tt(S, 'dve', ta, x1, cb, ALU.mult)
    o_tt(S, 'pool', tb, x2, sb_, ALU.mult)
    o_tt(S, 'dve', o1, ta, tb, ALU.subtract)
    o_tt(S, 'pool', tc, x1, sb_, ALU.mult)
    o_tt(S, 'dve', td, x2, cb, ALU.mult)
    o_tt(S, 'pool', o2, tc, td, ALU.add)


def transposes_to(S, C, src_bf, ncolblk, dsts):
    i = 0
    while i < ncolblk:
        n = min(8, ncolblk - i)
        pT = C.next_pT()
        for j in range(n):
            o_tr(S, pT[:, j * 128:(j + 1) * 128], src_bf[:, (i + j) * 128:(i + j + 1) * 128], C.ident[:, :])
        for j in range(n):
            o_copy(S, C.evac_eng(), dsts[i + j], pT[:, j * 128:(j + 1) * 128])
        i += n


def build_A_gqa(S, D, layer, ntiles=18):
    C = Ctx(S, D)
    md = load_mod(S, D, layer, 'md1', 0, D['norm1'])
    W = S.sb('wqkv', [128, 8, 1536], BF16)
    for k in range(8):
        S.dma('pool', W[:, k, :], D['wqkv'][k * 128:(k + 1) * 128, :])
    gq = S.sb('gq', [128, 1280], F32)
    S.dma('sp', gq[:, :], D['gq'])
    ct = S.sb('cost', [128, ntiles - 2, 64], F32)
    sn = S.sb('sint', [128, ntiles - 2, 64], F32)
    S.dma('sp', ct[:, :, :], D['cos'])
    S.dma('sp', sn[:, :, :], D['sin'])
    scr = dict(ss=S.sb('ss', [128, 1], F32), rs=S.sb('rs', [128, 1], F32), xs=S.sb('xs', [128, 1024], BF16),
               xtmp=S.sb('xtmp', [128, 1024], F32), sq=S.sb('sq', [128, 1280], F32), ssh=S.sb('ssh', [128, 16], F32),
               rsh=S.sb('rsh', [128, 16], F32), ra=S.sb('ra', [128, 640], F32), rb=S.sb('rb', [128, 640], F32),
               rc=S.sb('rc', [128, 640], F32), rd=S.sb('rd', [128, 640], F32))
    ht = [S.sb('ht%d' % i, [128, 1024], F32) for i in range(2)]
    xTt = S.sb('xTt', [128, 8, 128], BF16)
    Psb = S.sb('Psb', [128, 1536], F32)
    qn = S.sb('qn', [128, 1280], F32)
    qb = S.sb('qb', [128, 1280], BF16)
    QTs = S.sb('QTs', [128, 8, ntiles * 128], BF16)
    KTs = S.sb('KTs', [128, 2, ntiles * 128], BF16)
    Vs = S.sb('Vs', [128, ntiles, 256], BF16)
    pP = [S.ps('pP%d' % i, [128, 512], F32) for i in range(3)]
    np_ = [0]
    Psb2 = [Psb, S.sb('Psb_b', [128, 1536], F32)]

    def front(t):
        s = 1 if t < 2 else 0
        h = ht[t % 2]
        S.dma('sp', h[:, :], D['hin'][t * 128:(t + 1) * 128, :])
        norm_T(S, C, h[:, :], V(xTt, 0, (128, 8), (1, 128)), md[:, s, 0, :], md[:, s, 1, :], scr)
        proj_tm(S, C, pP, np_, xTt, W, 1536, Psb2[t % 2])

    def back(t):
        Psb = Psb2[t % 2]
        head_norm(S, C, (Psb, 0), 10, 128, gq, (qn, 0), scr)
        if t >= 2:
            rope_tm(S, (qn, 0), 10, 128, ct[:, t - 2, :], sn[:, t - 2, :], (qb, 0), scr)
        else:
            o_copy(S, 'dve', qb[:, :], qn[:, :])
        dsts = [QTs[:, hh, t * 128:(t + 1) * 128] for hh in range(8)] + [KTs[:, hh, t * 128:(t + 1) * 128] for hh in range(2)]
        transposes_to(S, C, qb, 10, dsts)
        o_copy(S, 'act', Vs[:, t, :], Psb[:, 1280:1536])

    front(0)
    for t in range(ntiles):
        if t + 1 < ntiles:
            front(t + 1)
        back(t)
    S.dma('sp', D['QT'].rearrange('h p n -> p h n'), QTs[:, :, :], is_output=True)
    S.dma('sp', D['KT'].rearrange('h p n -> p h n'), KTs[:, :, :], is_output=True)
    S.dma('sp', D['Vt'].rearrange('(t p) m -> p t m', p=128), Vs[:, :, :], is_output=True)


class AttnBufs:
    def __init__(self, S, nS=2):
        self.pS = [S.ps('pS%d' % i, [128, 512], F32) for i in range(nS)]
        self.pO = [S.ps('pO%d' % i, [128, 512], F32) for i in range(2)]
        self.pM = [S.ps('pSm%d' % i, [128, 512], F32) for i in range(2)]
        self.pt = [S.sb('pt%d' % i, [128, 512], BF16) for i in range(4)]
        self.rc = [S.sb('rcp%d' % i, [128, 512], F32) for i in range(2)]
        self.acc = [S.sb('sacc%d' % i, [128, 512], F32) for i in range(2)]
        self.hi = S.sb('shi', [128, 512], BF16)
        self.lo = S.sb('slo', [128, 512], BF16)
        self.ns = 0
        self.no = 0


def attn_head_block(S, C, A, nq, chunks, out_ap, dv, scale, orow=None):
    pO = A.pO[A.no % 2]
    pM = A.pM[A.no % 2]
    rc = A.rc[A.no % 2]
    A.no += 1
    n = len(chunks)
    nS = len(A.pS)
    PD = nS - 1
    base = A.ns
    A.ns += n

    def scores(ci):
        pieces, v_ap, nk = chunks[ci]
        pS = A.pS[(base + ci) % nS]
        for pi, (l, r) in enumerate(pieces):
            o_mm(S, pS[0:nk, 0:nq], l, r, pi == 0, pi == len(pieces) - 1)

    for ci in range(min(PD, n)):
        scores(ci)
    acc = A.acc[(A.no - 1) % 2]
    for ci, (pieces, v_ap, nk) in enumerate(chunks):
        if ci + PD < n:
            scores(ci + PD)
        pS = A.pS[(base + ci) % nS]
        pt = A.pt[(base + ci) % len(A.pt)]
        o_act(S, pt[0:nk, 0:nq], pS[0:nk, 0:nq], AF.Exp, scale=scale)
        o_mm(S, pO[0:dv, 0:nq], v_ap, pt[0:nk, 0:nq], ci == 0, ci == n - 1)
        if ci % 2 == 0:
            o_mm(S, pM[0:dv, 0:nq], C.ones[0:nk, 0:dv], pt[0:nk, 0:nq], ci == 0, False)
        elif ci == 1:
            o_copy(S, 'dve', acc[0:nk, 0:nq], pt[0:nk, 0:nq])
        else:
            o_tt(S, 'dve', acc[0:nk, 0:nq], acc[0:nk, 0:nq], pt[0:nk, 0:nq], ALU.add)
    nk0 = chunks[0][2]
    hi = A.hi
    lo = A.lo
    o_copy(S, 'dve', hi[0:nk0, 0:nq], acc[0:nk0, 0:nq])
    o_tt(S, 'dve', lo[0:nk0, 0:nq], acc[0:nk0, 0:nq], hi[0:nk0, 0:nq], ALU.subtract)
    o_mm(S, pM[0:dv, 0:nq], C.ones[0:nk0, 0:dv], hi[0:nk0, 0:nq], False, False)
    o_mm(S, pM[0:dv, 0:nq], C.ones[0:nk0, 0:dv], lo[0:nk0, 0:nq], False, True)
    r0, r1 = orow if orow is not None else (0, dv)
    o_recip(S, rc[r0:r1, 0:nq], pM[r0:r1, 0:nq])
    o_tt(S, 'dve', out_ap, pO[r0:r1, 0:nq], rc[r0:r1, 0:nq], ALU.mult)


def out_proj_residual(S, C, A, D, OT, nk, wo, gate, ntiles, hin, hout, kslice=None):
    ht = [S.sb('oht%d' % i, [128, 1024], F32) for i in range(2)]
    ho = [S.sb('oho%d' % i, [128, 1024], F32) for i in range(2)]
    ot = [S.sb('oot%d' % i, [128, 512], F32) for i in range(2)]
    n = 0
    for t in range(ntiles):
        s = 1 if t < 2 else 0
        h = ht[t % 2]
        hb = ho[t % 2]
        S.dma('sp', h[:, :], hin[t * 128:(t + 1) * 128, :])
        for hf in range(2):
            p = A.pS[n % 2]
            o_ = ot[n % 2]
            n += 1
            for k in range(nk):
                lhs = OT[:, k, t * 128:(t + 1) * 128] if kslice is None else kslice(k, t)
                o_mm(S, p[:, :], lhs, wo[:, k, hf * 512:(hf + 1) * 512], k == 0, k == nk - 1)
            o_tt(S, 'dve', o_[:, :], p[:, :], gate[:, s, hf * 512:(hf + 1) * 512], ALU.mult)
            o_tt(S, 'pool', hb[:, hf * 512:(hf + 1) * 512], o_[:, :], h[:, hf * 512:(hf + 1) * 512], ALU.add)
        S.dma('sp', hout[t * 128:(t + 1) * 128, :], hb[:, :], is_output=True)


def build_B_gqa(S, D, layer, ntiles=18, nkeys=8448):
    C = Ctx(S, D)
    A = AttnBufs(S)
    gate = load_gate_rep(S, D, layer, 'g1rep', 0)
    ntok = ntiles * 128
    nch = nkeys // 128
    QT = S.sb('QT', [128, 8, ntok], BF16)
    KT = S.sb('KT', [128, 2, nkeys], BF16)
    Vf = S.sb('Vf', [128, nch, 256], BF16)
    OT = S.sb('OT', [128, 8, ntok], BF16)
    wo = S.sb('wo', [128, 8, 1024], BF16)
    S.dma('sp', QT[:, :, :], D['QT'].rearrange('h p n -> p h n'))
    S.dma('sp', KT[:, :, :], D['KTf'].rearrange('h p n -> p h n'))
    S.dma('sp', Vf[:, :, :], D['Vf'].rearrange('(c p) m -> p c m', p=128))
    S.dma('pool', wo[:, :, :], D['wo'].rearrange('(k p) m -> p k m', p=128))
    scale = 128.0 ** -0.5
    for h in range(8):
        kv = h // 4
        chunks = [([(KT[:, kv, c * 128:(c + 1) * 128], QT[:, h, 0:256])], Vf[:, c, kv * 128:(kv + 1) * 128], 128) for c in range(2)]
        attn_head_block(S, C, A, 256, chunks, OT[:, h, 0:256], 128, scale)
        nqb = (ntok - 256) // 512
        for qb in range(nqb):
            q0 = 256 + qb * 512
            chunks = [([(KT[:, kv, c * 128:(c + 1) * 128], QT[:, h, q0:q0 + 512])], Vf[:, c, kv * 128:(kv + 1) * 128], 128) for c in range(nch)]
            attn_head_block(S, C, A, 512, chunks, OT[:, h, q0:q0 + 512], 128, scale)
    out_proj_residual(S, C, A, D, OT, 8, wo, gate, ntiles, D['hin'], D['hmid'])


def build_A_na(S, D, layer, ntiles=18):
    C = Ctx(S, D)
    md = load_mod(S, D, layer, 'md1', 0, D['norm1'])
    W = S.sb('wqkv', [128, 8, 3072], BF16)
    for k in range(8):
        S.dma('pool', W[:, k, :], D['wqkv'][k * 128:(k + 1) * 128, :])
    gq = S.sb('gq', [128, 2048], F32)
    S.dma('sp', gq[:, :], D['gq'])
    o_ts(S, 'dve', gq[:, 0:1024], gq[:, 0:1024], 0.125, None, ALU.mult)
    scr = dict(ss=S.sb('ss', [128, 1], F32), rs=S.sb('rs', [128, 1], F32), xs=S.sb('xs', [128, 1024], BF16),
               xtmp=S.sb('xtmp', [128, 1024], F32), sq=S.sb('sq', [128, 2048], F32), ssh=S.sb('ssh', [128, 32], F32),
               rsh=S.sb('rsh', [128, 32], F32))
    ht = [S.sb('ht%d' % i, [128, 1024], F32) for i in range(2)]
    xTt = S.sb('xTt', [128, 8, 128], BF16)
    Psb = S.sb('Psb', [128, 3072], F32)
    qn = S.sb('qn', [128, 2048], F32)
    qb = S.sb('qb', [128, 2048], BF16)
    QTs = S.sb('QTs', [128, 8, ntiles * 128], BF16)
    KTs = S.sb('KTs', [128, 8, ntiles * 128], BF16)
    Vs = S.sb('Vs', [128, 2, 1024], BF16)
    pP = [S.ps('pP%d' % i, [128, 512], F32) for i in range(3)]
    np_ = [0]
    Psb2 = [Psb, S.sb('Psb_b', [128, 3072], F32)]

    def front(t):
        s = 1 if t < 2 else 0
        h = ht[t % 2]
        S.dma('sp', h[:, :], D['hin'][t * 128:(t + 1) * 128, :])
        norm_T(S, C, h[:, :], V(xTt, 0, (128, 8), (1, 128)), md[:, s, 0, :], md[:, s, 1, :], scr)
        proj_tm(S, C, pP, np_, xTt, W, 3072, Psb2[t % 2])

    def back(t):
        Psb = Psb2[t % 2]
        head_norm(S, C, (Psb, 0), 32, 64, gq, (qn, 0), scr)
        o_copy(S, 'dve', qb[:, :], qn[:, :])
        dsts = [QTs[:, hh, t * 128:(t + 1) * 128] for hh in range(8)] + [KTs[:, hh, t * 128:(t + 1) * 128] for hh in range(8)]
        transposes_to(S, C, qb, 16, dsts)
        o_copy(S, 'act', Vs[:, t % 2, :], Psb[:, 2048:3072])
        S.dma('sp', D['Vt'][t * 128:(t + 1) * 128, :], Vs[:, t % 2, :], is_output=True)
        if 'Vh' in D:
            if t in (2, 3):
                S.dma('sp', D['Vh'][(t - 2) * 128:(t - 1) * 128, :], Vs[:, t % 2, :])
            if t in (ntiles - 2, ntiles - 1):
                S.dma('sp', D['Vh'][256 + (t - ntiles + 2) * 128:256 + (t - ntiles + 3) * 128, :], Vs[:, t % 2, :])

    front(0)
    for t in range(ntiles):
        if t + 1 < ntiles:
            front(t + 1)
        back(t)
    S.dma('sp', D['QT'].rearrange('h p n -> p h n'), QTs[:, :, :], is_output=True)
    S.dma('sp', D['KT'].rearrange('h p n -> p h n'), KTs[:, :, :], is_output=True)
    if 'KTh' in D:
        S.dma('sp', D['KTh'][:, :, 0:256].rearrange('h p n -> p h n'), KTs[:, :, 256:512])
        S.dma('sp', D['KTh'][:, :, 256:512].rearrange('h p n -> p h n'), KTs[:, :, ntiles * 128 - 256:ntiles * 128])


def build_B_na(S, D, layer, ntiles=18):
    C = Ctx(S, D)
    A = AttnBufs(S)
    gate = load_gate_rep(S, D, layer, 'g1rep', 0)
    ntok = ntiles * 128
    nqb = (ntok - 256) // 512
    QT = S.sb('QT', [128, 8, ntok], BF16)
    KTp = [S.sb('KTp%d' % i, [128, 2816], BF16) for i in range(2)]
    Vp = [S.sb('Vp%d' % i, [128, 22, 128], BF16) for i in range(2)]
    wo = S.sb('wo', [128, 8, 1024], BF16)
    TT = [S.sb('TT%d' % i, [128, 2, 1408], BF16) for i in range(2)]
    mK = S.sb('mK', [2, 128], BF16)
    mQs = [S.sb('mQ%d' % i, [2, 8 * 512], BF16) for i in range(2)]
    S.dma('sp', QT[:, :, :], D['QT'].rearrange('h p n -> p h n'))
    S.dma('pool', wo[:, :, :], D['wo'].rearrange('(k p) m -> p k m', p=128))
    S.dma('sp', mK[:, :], D['maskK'])
    nm = 0
    for pair in range(8):
        tt = TT[pair % 2]
        KT = KTp[pair % 2]
        Vf = Vp[pair % 2]
        S.dma('sp', tt[:, :, :], D['TT'][2 * pair:2 * pair + 2].rearrange('h p n -> p h n'))
        S.dma('sp', KT[:, 0:256], D['KTc'][pair])
        S.dma('sp', KT[:, 256:2816], D['KTl'][pair])
        S.dma('sp', Vf[:, 0:2, :], D['Vc'][:, pair * 128:(pair + 1) * 128].rearrange('(c p) m -> p c m', p=128))
        S.dma('sp', Vf[:, 2:22, :], D['Vl'][:, pair * 128:(pair + 1) * 128].rearrange('(c p) m -> p c m', p=128))
        for hh in range(2):
            pb = hh * 64
            chunks = [([(KT[pb:pb + 64, c * 128:(c + 1) * 128], QT[pb:pb + 64, pair, 0:256])],
                       Vf[:, c, :], 128) for c in range(2)]
            attn_head_block(S, C, A, 256, chunks, QT[pb:pb + 64, pair, 0:256], 128, 1.0, orow=(pb, pb + 64))
            for qb in range(nqb):
                q0 = 256 + qb * 512
                mQ = mQs[nm % 2]
                nm += 1
                S.dma('sp', mQ[:, :], D['maskQ'][:, qb * 4096:(qb + 1) * 4096])
                chunks = []
                for c in range(2):
                    chunks.append(([(KT[pb:pb + 64, c * 128:(c + 1) * 128], QT[pb:pb + 64, pair, q0:q0 + 512])],
                                   Vf[:, c, :], 128))
                for c in range(8):
                    m = 4 * qb + c
                    i0 = 14 - 2 * c
                    pieces = [(KT[pb:pb + 64, 256 + m * 128:256 + (m + 1) * 128], QT[pb:pb + 64, pair, q0:q0 + 512]),
                              (C.ident[:, :], tt[:, hh, i0 * 64:i0 * 64 + 512]),
                              (mK[0:2, :], mQ[0:2, c * 512:(c + 1) * 512])]
                    chunks.append((pieces, Vf[:, 2 + m, :], 128))
                attn_head_block(S, C, A, 512, chunks, QT[pb:pb + 64, pair, q0:q0 + 512], 128, 1.0, orow=(pb, pb + 64))
    out_proj_residual(S, C, A, D, QT, 8, wo, gate, ntiles, D['hin'], D['hmid'])


def proj_tm2(S, C, pP, np_, lhs_list, Wfn, ncols, Psb, pcol0=0):
    c0 = 0
    nk = len(lhs_list)
    while c0 < ncols:
        w = min(512, ncols - c0)
        p = pP[np_[0] % len(pP)]
        np_[0] += 1
        for k in range(nk):
            o_mm(S, p[:, 0:w], lhs_list[k], Wfn(k, c0, w), k == 0, k == nk - 1)
        o_copy(S, C.evac_eng(), Psb[:, pcol0 + c0:pcol0 + c0 + w], p[:, 0:w])
        c0 += w


def rope_gen(S, src, dst, nh, hstride_s, hstride_d, hd, cos, sin, scr):
    (st, so) = src
    (dt_, do) = dst
    x1 = V(st, so, (hstride_s, nh), (1, hd))
    x2 = V(st, so + hd, (hstride_s, nh), (1, hd))
    o1 = V(dt_, do, (hstride_d, nh), (1, hd))
    o2 = V(dt_, do + hd, (hstride_d, nh), (1, hd))
    cb = bass.AP(cos.tensor, cos.offset, [list(cos.ap[0]), [0, nh], [1, hd]])
    sb_ = bass.AP(sin.tensor, sin.offset, [list(sin.ap[0]), [0, nh], [1, hd]])
    ta = V(scr['ra'], 0, (hd, nh), (1, hd))
    tb = V(scr['rb'], 0, (hd, nh), (1, hd))
    tc = V(scr['rc'], 0, (hd, nh), (1, hd))
    td = V(scr['rd'], 0, (hd, nh), (1, hd))
    o_tt(S, 'dve', ta, x1, cb, ALU.mult)
    o_tt(S, 'pool', tb, x2, sb_, ALU.mult)
    o_tt(S, 'dve', o1, ta, tb, ALU.subtract)
    o_tt(S, 'pool', tc, x1, sb_, ALU.mult)
    o_tt(S, 'dve', td, x2, cb, ALU.mult)
    o_tt(S, 'pool', o2, tc, td, ALU.add)


def transposes_gen(S, C, items):
    i = 0
    n_it = len(items)
    while i < n_it:
        n = min(8, n_it - i)
        pT = C.next_pT()
        for j in range(n):
            src, dst = items[i + j]
            w = src.shape[-1]
            o_tr(S, pT[0:w, j * 128:(j + 1) * 128], src, C.ident[:, :])
        for j in range(n):
            src, dst = items[i + j]
            w = src.shape[-1]
            o_copy(S, C.evac_eng(), dst, pT[0:w, j * 128:(j + 1) * 128])
        i += n


def build_A_mla(S, D, layer, ntiles=18):
    C = Ctx(S, D)
    md = load_mod(S, D, layer, 'md1', 0, D['norm1'])
    Wd = S.sb('wdown', [128, 8, 704], BF16)
    Wq = S.sb('wuq', [128, 3, 1536], BF16)
    Wkv = S.sb('wukv', [128, 2, 2048], BF16)
    S.dma('pool', Wd[:, :, :], D['wdown'].rearrange('(k p) m -> p k m', p=128))
    S.dma('pool', Wq[:, :, :], D['wuq'].rearrange('(k p) m -> p k m', p=128))
    S.dma('pool', Wkv[:, :, :], D['wukv'].rearrange('(k p) m -> p k m', p=128))
    lg = S.sb('lg', [128, 5], F32)
    S.dma('sp', lg[:, :], D['lg'])
    gq = S.sb('gq', [128, 1536], F32)
    gk = S.sb('gk', [128, 192], F32)
    S.dma('sp', gq[:, :], D['gq'])
    S.dma('sp', gk[:, :], D['gk'])
    o_ts(S, 'dve', gq[:, :], gq[:, :], 192.0 ** -0.5, None, ALU.mult)
    ct = S.sb('cost', [128, ntiles - 2, 32], F32)
    sn = S.sb('sint', [128, ntiles - 2, 32], F32)
    S.dma('sp', ct[:, :, :], D['cos'])
    S.dma('sp', sn[:, :, :], D['sin'])
    scr = dict(ss=S.sb('ss', [128, 1], F32), rs=S.sb('rs', [128, 1], F32), xs=S.sb('xs', [128, 1024], BF16),
               xtmp=S.sb('xtmp', [128, 1024], F32), sq=S.sb('sq', [128, 1536], F32), ssh=S.sb('ssh', [128, 16], F32),
               rsh=S.sb('rsh', [128, 16], F32), ra=S.sb('ra', [128, 256], F32), rb=S.sb('rb', [128, 256], F32),
               rc=S.sb('rc', [128, 256], F32), rd=S.sb('rd', [128, 256], F32))
    ht = [S.sb('ht%d' % i, [128, 1024], F32) for i in range(2)]
    xTt = S.sb('xTt', [128, 8, 128], BF16)
    Psb = S.sb('Psb', [128, 704], F32)
    ss2 = S.sb('ss2', [128, 2], F32)
    rs2 = S.sb('rs2', [128, 2], F32)
    cs = S.sb('cs', [128, 640], BF16)
    cT = S.sb('cT', [128, 5, 128], BF16)
    Qsb = S.sb('Qsb', [128, 1536], F32)
    KVsb = S.sb('KVsb', [128, 2048], F32)
    qn = S.sb('qn', [128, 1536], F32)
    qb = S.sb('qb', [128, 1536], BF16)
    ksq = S.sb('ksq', [128, 1024], F32)
    ssk = S.sb('ssk', [128, 8], F32)
    ssr = S.sb('ssr', [128, 1], F32)
    rsk = S.sb('rsk', [128, 8], F32)
    kr = S.sb('kr', [128, 8, 64], F32)
    kb = S.sb('kb', [128, 8, 192], BF16)
    vb = [S.sb('vb%d' % i, [128, 1024], BF16) for i in range(2)]
    stg = [dict(qn=S.sb('sqn%d' % i, [128, 8, 128], BF16), qr=S.sb('sqr%d' % i, [128, 8, 128], BF16),
                kn=S.sb('skn%d' % i, [128, 8, 128], BF16), kr=S.sb('skr%d' % i, [128, 8, 128], BF16)) for i in range(2)]
    pP = [S.ps('pP%d' % i, [128, 512], F32) for i in range(3)]
    np_ = [0]
    for t in range(ntiles):
        s = 1 if t < 2 else 0
        h = ht[t % 2]
        S.dma('sp', h[:, :], D['hin'][t * 128:(t + 1) * 128, :])
        norm_T(S, C, h[:, :], V(xTt, 0, (128, 8), (1, 128)), md[:, s, 0, :], md[:, s, 1, :], scr)
        proj_tm2(S, C, pP, np_, [xTt[:, k, :] for k in range(8)], lambda k, c0, w: Wd[:, k, c0:c0 + w], 704, Psb)
        o_act(S, C.junk[:, 0:384], Psb[:, 0:384], AF.Square, scale=384.0 ** -0.5, accum=ss2[:, 0:1])
        o_act(S, C.junk[:, 0:256], Psb[:, 384:640], AF.Square, scale=1.0 / 16.0, accum=ss2[:, 1:2])
        rstd_rows(S, C, ss2[:, 0:2], rs2[:, 0:2], 2, 1.0)
        o_act(S, cs[:, 0:384], Psb[:, 0:384], AF.Copy, scale=rs2[:, 0:1])
        o_act(S, cs[:, 384:640], Psb[:, 384:640], AF.Copy, scale=rs2[:, 1:2])
        pT = C.next_pT()
        for k in range(5):
            o_tr(S, pT[:, k * 128:(k + 1) * 128], cs[:, k * 128:(k + 1) * 128], C.ident[:, :])
        o_tt(S, 'dve', V(cT, 0, (128, 5), (1, 128)), V(pT, 0, (128, 5), (1, 128)), V(lg, 0, (1, 5), (0, 128)), ALU.mult)
        proj_tm2(S, C, pP, np_, [cT[:, k, :] for k in range(3)], lambda k, c0, w: Wq[:, k, c0:c0 + w], 1536, Qsb)
        proj_tm2(S, C, pP, np_, [cT[:, 3 + k, :] for k in range(2)], lambda k, c0, w: Wkv[:, k, c0:c0 + w], 2048, KVsb)
        head_norm(S, C, (Qsb, 0), 8, 192, gq, (qn, 0), scr)
        o_copy(S, 'act', qb[:, :], qn[:, :])
        if t >= 2:
            rope_gen(S, (qn, 128), (qb, 128), 8, 192, 192, 32, ct[:, t - 2, :], sn[:, t - 2, :], scr)
        knope = V(KVsb, 0, (256, 8), (1, 128))
        ksqv = V(ksq, 0, (128, 8), (1, 128))
        o_tt(S, 'pool', ksqv, knope, knope, ALU.mult)
        o_red(S, 'dve', ssk[:, 0:8], ksqv)
        o_act(S, C.junk[:, 0:64], Psb[:, 640:704], AF.Square, accum=ssr[:, 0:1])
        o_ts(S, 'dve', ssk[:, 0:8], ssk[:, 0:8], ssr[:, 0:1], None, ALU.add)
        rstd_rows(S, C, ssk[:, 0:8], rsk[:, 0:8], 8, 1.0 / 192.0)
        o_tt(S, 'dve', ksqv, knope, V(rsk, 0, (1, 8), (0, 128)), ALU.mult)
        o_tt(S, 'pool', V(kb, 0, (192, 8), (1, 128)), ksqv, V(gk, 0, (0, 8), (1, 128)), ALU.mult)
        krv = V(kr, 0, (64, 8), (1, 64))
        o_tt(S, 'dve', krv, V(Psb, 640, (0, 8), (1, 64)), V(rsk, 0, (1, 8), (0, 64)), ALU.mult)
        o_tt(S, 'pool', krv, krv, V(gk, 128, (0, 8), (1, 64)), ALU.mult)
        if t >= 2:
            rope_gen(S, (kr, 0), (kb, 128), 8, 64, 192, 32, ct[:, t - 2, :], sn[:, t - 2, :], scr)
        else:
            o_copy(S, 'dve', V(kb, 128, (192, 8), (1, 64)), krv)
        vbt = vb[t % 2]
        o_copy(S, 'act', V(vbt, 0, (128, 8), (1, 128)), V(KVsb, 128, (256, 8), (1, 128)))
        S.dma('sp', D['Vt'][t * 128:(t + 1) * 128, :], vbt[:, :], is_output=True)
        sg = stg[t % 2]
        items = []
        for hh in range(8):
            items.append((qb[:, hh * 192:hh * 192 + 128], sg['qn'][:, hh, :]))
        for hh in range(8):
            items.append((qb[:, hh * 192 + 128:hh * 192 + 192], sg['qr'][0:64, hh, :]))
        for hh in range(8):
            items.append((kb[:, hh, 0:128], sg['kn'][:, hh, :]))
        for hh in range(8):
            items.append((kb[:, hh, 128:192], sg['kr'][0:64, hh, :]))
        transposes_gen(S, C, items)
        S.dma('sp', D['QTn'][:, :, t * 128:(t + 1) * 128].rearrange('h p n -> p h n'), sg['qn'][:, :, :], is_output=True)
        S.dma('sp', D['QTr'][:, :, t * 128:(t + 1) * 128].rearrange('h p n -> p h n'), sg['qr'][0:64, :, :], is_output=True)
        S.dma('sp', D['KTn'][:, :, t * 128:(t + 1) * 128].rearrange('h p n -> p h n'), sg['kn'][:, :, :], is_output=True)
        S.dma('sp', D['KTr'][:, :, t * 128:(t + 1) * 128].rearrange('h p n -> p h n'), sg['kr'][0:64, :, :], is_output=True)


def build_B_mla(S, D, layer, ntiles=18, nkeys=8448):
    C = Ctx(S, D)
    A = AttnBufs(S)
    gate = load_gate_rep(S, D, layer, 'g1rep', 0)
    ntok = ntiles * 128
    nch = nkeys // 128
    nqb = (ntok - 256) // 512
    QTn = S.sb('QTn', [128, 8, ntok], BF16)
    QTr = S.sb('QTr', [128, 8, ntok], BF16)
    KTn = S.sb('KTn', [128, nkeys], BF16)
    KTr = S.sb('KTr', [128, nkeys], BF16)
    Vh = S.sb('Vh', [128, nch, 128], BF16)
    wo = S.sb('wo', [128, 8, 1024], BF16)
    S.dma('sp', QTn[:, :, :], D['QTn'].rearrange('h p n -> p h n'))
    S.dma('sp', QTr[0:64, :, :], D['QTr'].rearrange('h p n -> p h n'))
    S.dma('pool', wo[:, :, :], D['wo'].rearrange('(k p) m -> p k m', p=128))
    for h in range(8):
        S.dma('sp', KTn[:, :], D['KTnf'][h])
        S.dma('sp', KTr[0:64, :], D['KTrf'][h])
        S.dma('sp', Vh[:, :, :], D['Vf'][:, h * 128:(h + 1) * 128].rearrange('(c p) m -> p c m', p=128))
        chunks = [([(KTn[:, c * 128:(c + 1) * 128], QTn[:, h, 0:256]), (KTr[0:64, c * 128:(c + 1) * 128], QTr[0:64, h, 0:256])],
                   Vh[:, c, :], 128) for c in range(2)]
        attn_head_block(S, C, A, 256, chunks, QTn[:, h, 0:256], 128, 1.0)
        for qb in range(nqb):
            q0 = 256 + qb * 512
            chunks = [([(KTn[:, c * 128:(c + 1) * 128], QTn[:, h, q0:q0 + 512]), (KTr[0:64, c * 128:(c + 1) * 128], QTr[0:64, h, q0:q0 + 512])],
                       Vh[:, c, :], 128) for c in range(nch)]
            attn_head_block(S, C, A, 512, chunks, QTn[:, h, q0:q0 + 512], 128, 1.0)
    out_proj_residual(S, C, A, D, QTn, 8, wo, gate, ntiles, D['hin'], D['hmid'])


def build_A_ret(S, D, layer, ntiles=18):
    C = Ctx(S, D)
    md = load_mod(S, D, layer, 'md1', 0, D['norm1'])
    W = S.sb('wqkvg', [128, 8, 6144], BF16)
    for k in range(8):
        S.dma('pool', W[:, k, :], D['wqkvg'][k * 128:(k + 1) * 128, :])
    ld = S.sb('ld', [128, 2, 4], F32)
    S.dma('sp', ld[:, :, :], D['ldrep'])
    E = S.sb('E', [128, 4], F32)
    Ec = S.sb('Ec', [128, 2, 2], F32)
    S.dma('sp', E[:, :], D['E'])
    S.dma('sp', Ec[:, :, :], D['Ec'])
    dec = S.sb('dec', [128, 4, 4], F32)
    decc = S.sb('decc', [128, 2, 2, 4], F32)
    for kind in range(4):
        o_ts(S, 'dve', dec[:, kind, :], ld[:, kind % 2, :], E[:, kind:kind + 1], None, ALU.mult)
    for tt_ in range(2):
        for dr in range(2):
            o_ts(S, 'dve', decc[:, tt_, dr, :], ld[:, dr, :], Ec[:, tt_, dr:dr + 1], None, ALU.mult)
    o_act(S, dec[:, :, :], dec[:, :, :], AF.Exp)
    o_act(S, decc[:, :, :, :], decc[:, :, :, :], AF.Exp)
    o_ts(S, 'dve', dec[:, 2:4, :], dec[:, 2:4, :], 1.0 / 16.0, None, ALU.mult)
    o_ts(S, 'dve', decc[:, :, :, :], decc[:, :, :, :], 1.0 / 16.0, None, ALU.mult)
    scr = dict(ss=S.sb('ss', [128, 1], F32), rs=S.sb('rs', [128, 1], F32), xs=S.sb('xs', [128, 1024], BF16),
               xtmp=S.sb('xtmp', [128, 1024], F32), ra=S.sb('ra', [128, 1024], F32), rb=S.sb('rb', [128, 1024], F32),
               rc=S.sb('rc', [128, 1024], F32), rd=S.sb('rd', [128, 1024], F32))
    ht = [S.sb('ht%d' % i, [128, 1024], F32) for i in range(2)]
    cst = [S.sb('cst%d' % i, [128, 2, 128], F32) for i in range(2)]
    xTt = S.sb('xTt', [128, 8, 128], BF16)
    Psb = S.sb('Psb', [128, 2048], F32)
    qkr = S.sb('qkr', [128, 2048], F32)
    scb = [S.sb('scb%d' % i, [128, 1024], BF16) for i in range(4)]
    vb = S.sb('vb', [128, 2048], BF16)
    gs = S.sb('gs', [128, 2048], BF16)
    stg = [S.sb('stg%d' % i, [128, 8, 128], BF16) for i in range(4)]
    pP = [S.ps('pP%d' % i, [128, 512], F32) for i in range(4)]
    np_ = [0]
    names = ['QfT', 'QbT', 'KfT', 'KbT']
    for t in range(ntiles):
        s = 1 if t < 2 else 0
        h = ht[t % 2]
        S.dma('sp', h[:, :], D['hin'][t * 128:(t + 1) * 128, :])
        norm_T(S, C, h[:, :], V(xTt, 0, (128, 8), (1, 128)), md[:, s, 0, :], md[:, s, 1, :], scr)
        lhs = [xTt[:, k, :] for k in range(8)]
        if t >= 2:
            n = t - 2
            cs_ = cst[t % 2]
            S.dma('sp', cs_[:, 0, :], D['cos'][:, n, :])
            S.dma('sp', cs_[:, 1, :], D['sin'][:, n, :])
            proj_tm2(S, C, pP, np_, lhs, lambda k, c0, w: W[:, k, c0:c0 + w], 2048, Psb)
            rope_gen(S, (Psb, 0), (qkr, 0), 8, 256, 256, 128, cs_[:, 0, :], cs_[:, 1, :], scr)
            for kind in range(4):
                src_off = 0 if kind < 2 else 1024
                o_tt(S, 'dve' if kind % 2 == 0 else 'pool', V(scb[kind], 0, (256, 4), (1, 256)), V(qkr, src_off, (256, 4), (1, 256)),
                     V(dec, kind * 4, (1, 4), (0, 256)), ALU.mult)
        else:
            proj_tm2(S, C, pP, np_, lhs, lambda k, c0, w: W[:, k, 1024 + c0:1024 + c0 + w], 1024, Psb, pcol0=1024)
            for dr in range(2):
                o_tt(S, 'dve' if dr == 0 else 'pool', V(scb[2 + dr], 0, (256, 4), (1, 256)), V(Psb, 1024, (256, 4), (1, 256)),
                     V(decc, (t * 2 + dr) * 4, (1, 4), (0, 256)), ALU.mult)
        S.dma('sp', D['Kf'][t * 128:(t + 1) * 128, :], scb[2][:, :], is_output=True)
        S.dma('sp', D['Kb'][t * 128:(t + 1) * 128, :], scb[3][:, :], is_output=True)
        if t >= 2:
            n = t - 2
            items = []
            for kind in range(4):
                for hh in range(4):
                    for hf in range(2):
                        items.append((scb[kind][:, hh * 256 + hf * 128:hh * 256 + (hf + 1) * 128], stg[kind][:, hh * 2 + hf, :]))
            transposes_gen(S, C, items)
            for kind in range(4):
                S.dma('sp', D[names[kind]][:, :, :, n * 128:(n + 1) * 128].rearrange('h f p n -> p (h f) n'), stg[kind][:, :, :], is_output=True)
        for c in range(4):
            p = pP[np_[0] % 4]
            np_[0] += 1
            for k in range(8):
                o_mm(S, p[:, :], lhs[k], W[:, k, 2048 + c * 512:2048 + (c + 1) * 512], k == 0, k == 7)
            o_copy(S, C.evac_eng(), vb[:, c * 512:(c + 1) * 512], p[:, :])
        S.dma('sp', D['Vt'][t * 128:(t + 1) * 128, :], vb[:, :], is_output=True)
        if t >= 2:
            n = t - 2
            for c in range(4):
                p = pP[np_[0] % 4]
                np_[0] += 1
                for k in range(8):
                    o_mm(S, p[:, :], lhs[k], W[:, k, 4096 + c * 512:4096 + (c + 1) * 512], k == 0, k == 7)
                o_act(S, gs[:, c * 512:(c + 1) * 512], p[:, :], AF.Silu)
            S.dma('sp', D['Gs'][n * 128:(n + 1) * 128, :], gs[:, :], is_output=True)


def build_A2_ret(S, D, layer, ntiles=18):
    C = Ctx(S, D)
    ld = S.sb('ld', [128, 2, 4], F32)
    S.dma('sp', ld[:, :, :], D['ldrep'])
    wm = S.sb('wm', [128, 2, ntiles], F32)
    S.dma('sp', wm[:, :, :], D['wmul'])
    wt = S.sb('wt', [128, 2, ntiles, 4], F32)
    for dr in range(2):
        for t in range(ntiles):
            o_ts(S, 'dve', wt[:, dr, t, :], ld[:, dr, :], wm[:, dr, t:t + 1], None, ALU.mult)
    o_act(S, wt[:, :, :, :], wt[:, :, :, :], AF.Exp)
    U = [[S.sb('U%d_%d' % (a, dr), [128, 4, 2, 512], F32) for dr in range(2)] for a in range(2)]
    for a in range(2):
        for dr in range(2):
            o_memset(S, 'pool', U[a][dr][:, :, :, :], 0.0)
    Kt = [[S.sb('Kt%d_%d' % (i, dr), [128, 1024], BF16) for dr in range(2)] for i in range(2)]
    Vt = [S.sb('Vt%d' % i, [128, 2048], BF16) for i in range(2)]
    pD = [S.ps('pD%d' % i, [128, 512], F32) for i in range(4)]
    nd = 0
    for t in range(ntiles):
        a = 0 if t < 2 else 1
        S.dma('sp', Kt[t % 2][0][:, :], D['Kf'][t * 128:(t + 1) * 128, :])
        S.dma('sp', Kt[t % 2][1][:, :], D['Kb'][t * 128:(t + 1) * 128, :])
        S.dma('sp', Vt[t % 2][:, :], D['Vt'][t * 128:(t + 1) * 128, :])
        for dr in range(2):
            for hh in range(4):
                for hf in range(2):
                    p = pD[nd % 4]
                    nd += 1
                    o_mm(S, p[:, :], Kt[t % 2][dr][:, hh * 256 + hf * 128:hh * 256 + (hf + 1) * 128], Vt[t % 2][:, hh * 512:(hh + 1) * 512], True, True)
                    u = U[a][dr][:, hh, hf, :]
                    o_stt(S, 'dve', u, p[:, :], wt[:, dr, t, hh:hh + 1], u, ALU.mult, ALU.add)
    if 'Lfb' in D:
        for dr in range(2):
            S.dma('sp', D['Sc'][dr].rearrange('h f p n -> p h f n'), U[0][dr][:, :, :, :], is_output=True)
            S.dma('sp', D['Lfb'][dr].rearrange('h f p n -> p h f n'), U[1][dr][:, :, :, :], is_output=True)
        return
    outs = [['Scf', 'Scb'], ['Lf', 'Lb']]
    for a in range(2):
        for dr in range(2):
            S.dma('sp', D[outs[a][dr]].rearrange('h f p n -> p h f n'), U[a][dr][:, :, :, :], is_output=True)


def build_B_ret(S, D, layer, nchunks=16, nsrc=5):
    C = Ctx(S, D)
    gate = S.sb('g1rep', [128, 1024], F32)
    S.dma('sp', gate[:, :], D['modv'][layer, 0, 2048:3072].partition_broadcast(128))
    ld = S.sb('ld', [128, 2, 4], F32)
    S.dma('sp', ld[:, :, :], D['ldrep'])
    cm = S.sb('cm', [128, 2, nsrc, 2], F32)
    S.dma('sp', cm[:, :, :, :], D['coefm'])
    coef = S.sb('coef', [128, 2, nsrc, 4], F32)
    for dr in range(2):
        for sidx in range(nsrc):
            o_ts(S, 'dve', coef[:, dr, sidx, :], ld[:, dr, :], cm[:, dr, sidx, 0:1], None, ALU.mult)
    o_act(S, coef[:, :, :, :], coef[:, :, :, :], AF.Exp)
    for dr in range(2):
        for sidx in range(nsrc):
            o_ts(S, 'dve', coef[:, dr, sidx, :], coef[:, dr, sidx, :], cm[:, dr, sidx, 1:2], None, ALU.mult)
    d128 = S.sb('d128', [128, 2, 4], F32)
    o_act(S, d128[:, :, :], ld[:, :, :], AF.Exp, scale=128.0)
    St = [S.sb('St%d' % dr, [128, 4, 2, 512], F32) for dr in range(2)]
    Sb = [S.sb('Sb%d' % dr, [128, 4, 2, 512], BF16) for dr in range(2)]
    Lp = [S.sb('Lp%d' % i, [128, 2, 512], F32) for i in range(2)]
    nl = 0
    for dr in range(2):
        o_memset(S, 'pool', St[dr][:, :, :, :], 0.0)
        for sidx in range(nsrc):
            for hh in range(4):
                lp = Lp[nl % 2]
                nl += 1
                if 'G_L' in D:
                    src = D['G_L'][sidx, dr, hh] if sidx < nsrc - 1 else D['Sc'][dr, hh]
                else:
                    src = D['Lall'][sidx, dr, hh]
                S.dma('sp', lp[:, :, :], src.rearrange('f p n -> p f n'))
                u = St[dr][:, hh, :, :]
                o_stt(S, 'dve', u, lp[:, :, :], coef[:, dr, sidx, hh:hh + 1], u, ALU.mult, ALU.add)
        o_copy(S, 'act', Sb[dr][:, :, :, :], St[dr][:, :, :, :])
    mD = S.sb('mD', [128, 2, 128], BF16)
    S.dma('sp', mD[:, :, :], D['maskD'].rearrange('d p n -> p d n'))
    onr = S.sb('onr', [128, 2048], F32)
    S.dma('sp', onr[:, :], D['onorm'])
    wo = S.sb('wo', [128, 16, 1024], BF16)
    S.dma('pool', wo[:, :, :], D['wo'].rearrange('(k p) m -> p k m', p=128))
    QTc = [S.sb('QTc%d' % i, [128, 8, 128], BF16) for i in range(2)]
    KTc = [S.sb('KTc%d' % i, [128, 8, 128], BF16) for i in range(2)]
    Kc = [S.sb('Kc%d' % i, [128, 1024], BF16) for i in range(2)]
    Vc = [S.sb('Vc%d' % i, [128, 2048], BF16) for i in range(2)]
    AT = [S.sb('AT%d' % i, [128, 128], BF16) for i in range(2)]
    yf = S.sb('yf', [128, 2048], F32)
    y = S.sb('y', [128, 2048], F32)
    sq = S.sb('sq', [128, 2048], F32)
    yb = S.sb('yb', [128, 2048], BF16)
    gsb = S.sb('gsb', [128, 2048], BF16)
    YT = S.sb('YT', [128, 16, 128], BF16)
    st4 = S.sb('st4', [128, 4], F32)
    mean = S.sb('mean', [128, 4], F32)
    rs4 = S.sb('rs4', [128, 4], F32)
    tmpS2 = [S.sb('tmpS%d' % i, [128, 512], F32) for i in range(2)]
    htl = [S.sb('htl%d' % i, [128, 1024], F32) for i in range(2)]
    hol = [S.sb('hol%d' % i, [128, 1024], F32) for i in range(2)]
    oot = [S.sb('oot%d' % i, [128, 512], F32) for i in range(2)]
    pS = [S.ps('pS%d' % i, [128, 512], F32) for i in range(2)]
    pY = [S.ps('pY%d' % i, [128, 512], F32) for i in range(2)]
    pD = [S.ps('pD%d' % i, [128, 512], F32) for i in range(2)]
    nm = ['Q%sT', 'K%sT', 'K%s']
    cnt = dict(s=0, y=0, d=0, o=0, it=0)
    for dr in range(2):
        sfx = 'f' if dr == 0 else 'b'
        order = list(range(nchunks)) if dr == 0 else list(range(nchunks - 1, -1, -1))
        for n in order:
            it = cnt['it']
            cnt['it'] += 1
            qt = QTc[it % 2]
            kt = KTc[it % 2]
            kc = Kc[it % 2]
            vc = Vc[it % 2]
            S.dma('sp', qt[:, :, :], D['Q%sT' % sfx][:, :, :, n * 128:(n + 1) * 128].rearrange('h f p n -> p (h f) n'))
            S.dma('sp', kt[:, :, :], D['K%sT' % sfx][:, :, :, n * 128:(n + 1) * 128].rearrange('h f p n -> p (h f) n'))
            S.dma('sp', kc[:, :], D['K%s' % sfx][256 + n * 128:256 + (n + 1) * 128, :])
            S.dma('sp', vc[:, :], D['Vt'][256 + n * 128:256 + (n + 1) * 128, :])
            if dr == 1:
                S.dma('sp', yf[:, :], D['yf'][n * 128:(n + 1) * 128, :])
                S.dma('sp', gsb[:, :], D['Gs'][n * 128:(n + 1) * 128, :])
            for hh in range(4):
                ps_ = pS[cnt['s'] % 2]
                at = AT[cnt['s'] % 2]
                cnt['s'] += 1
                for hf in range(2):
                    o_mm(S, ps_[:, 0:128], kt[:, hh * 2 + hf, :], qt[:, hh * 2 + hf, :], hf == 0, hf == 1)
                o_tt(S, 'dve', at[:, :], ps_[:, 0:128], mD[:, dr, :], ALU.mult)
                py = pY[cnt['y'] % 2]
                cnt['y'] += 1
                o_mm(S, py[:, :], at[:, :], vc[:, hh * 512:(hh + 1) * 512], True, False)
                for hf in range(2):
                    o_mm(S, py[:, :], qt[:, hh * 2 + hf, :], Sb[dr][:, hh, hf, :], False, hf == 1)
                if dr == 0:
                    o_copy(S, 'act', yf[:, hh * 512:(hh + 1) * 512], py[:, :])
                else:
                    o_tt(S, 'dve', y[:, hh * 512:(hh + 1) * 512], py[:, :], yf[:, hh * 512:(hh + 1) * 512], ALU.add)
                for hf in range(2):
                    pd = pD[cnt['d'] % 2]
                    cnt['d'] += 1
                    o_mm(S, pd[:, :], kc[:, hh * 256 + hf * 128:hh * 256 + (hf + 1) * 128], vc[:, hh * 512:(hh + 1) * 512], True, True)
                    u = St[dr][:, hh, hf, :]
                    ts_ = tmpS2[cnt['d'] % 2]
                    o_tt(S, 'dve', ts_[:, :], pd[:, :], u, ALU.add)
                    o_act(S, u, ts_[:, :], AF.Copy, scale=d128[:, dr, hh:hh + 1])
                    o_act(S, Sb[dr][:, hh, hf, :], ts_[:, :], AF.Copy, scale=d128[:, dr, hh:hh + 1])
            if dr == 0:
                S.dma('sp', D['yf'][n * 128:(n + 1) * 128, :], yf[:, :], is_output=True)
            else:
                yv = V(y, 0, (512, 4), (1, 512))
                sv = V(sq, 0, (512, 4), (1, 512))
                o_red(S, 'dve', st4[:, 0:4], yv)
                o_ts(S, 'dve', mean[:, 0:4], st4[:, 0:4], 1.0 / 512.0, None, ALU.mult)
                o_tt(S, 'dve', yv, yv, V(mean, 0, (1, 4), (0, 512)), ALU.subtract)
                o_tt(S, 'pool', sv, yv, yv, ALU.mult)
                o_red(S, 'dve', st4[:, 0:4], sv)
                rstd_rows(S, C, st4[:, 0:4], rs4[:, 0:4], 4, 1.0 / 512.0)
                o_tt(S, 'dve', yv, yv, V(rs4, 0, (1, 4), (0, 512)), ALU.mult)
                o_tt(S, 'pool', sq[:, :], y[:, :], onr[:, :], ALU.mult)
                o_tt(S, 'dve', yb[:, :], sq[:, :], gsb[:, :], ALU.mult)
                items = [(yb[:, k * 128:(k + 1) * 128], YT[:, k, :]) for k in range(16)]
                transposes_gen(S, C, items)
                hti = htl[cnt['o'] % 2]
                hoi = hol[cnt['o'] % 2]
                cnt['o'] += 1
                S.dma('sp', hti[:, :], D['hin'][256 + n * 128:256 + (n + 1) * 128, :])
                for hf in range(2):
                    p = pS[cnt['s'] % 2]
                    cnt['s'] += 1
                    o_ = oot[hf]
                    for k in range(16):
                        o_mm(S, p[:, :], YT[:, k, :], wo[:, k, hf * 512:(hf + 1) * 512], k == 0, k == 15)
                    o_tt(S, 'dve', o_[:, :], p[:, :], gate[:, hf * 512:(hf + 1) * 512], ALU.mult)
                    o_tt(S, 'pool', hoi[:, hf * 512:(hf + 1) * 512], o_[:, :], hti[:, hf * 512:(hf + 1) * 512], ALU.add)
                S.dma('sp', D['hmid'][n * 128:(n + 1) * 128, :], hoi[:, :], is_output=True)


def ld_KT(S, dst, G, hsel, ntok, R):
    nlat = ntok - 256
    S.dma('sp', dst[:, :, 0:256], G[0, hsel, :, 0:256].rearrange('h p n -> p h n'))
    for r in range(R):
        S.dma('sp', dst[:, :, 256 + r * nlat:256 + (r + 1) * nlat], G[r, hsel, :, 256:ntok].rearrange('h p n -> p h n'))


def ld_KT1(S, dst, G, h, ntok, R):
    nlat = ntok - 256
    S.dma('sp', dst[:, 0:256], G[0, h, :, 0:256])
    for r in range(R):
        S.dma('sp', dst[:, 256 + r * nlat:256 + (r + 1) * nlat], G[r, h, :, 256:ntok])


def ld_V(S, dst, G, c0, c1, ntok, R):
    nlt = (ntok - 256) // 128
    S.dma('sp', dst[:, 0:2, :], G[0, 0:256, c0:c1].rearrange('(c p) m -> p c m', p=128))
    for r in range(R):
        S.dma('sp', dst[:, 2 + r * nlt:2 + (r + 1) * nlt, :], G[r, 256:ntok, c0:c1].rearrange('(c p) m -> p c m', p=128))


def build_B_gqa_f(S, D, layer, ntiles, R):
    C = Ctx(S, D, npT=0)
    A = AttnBufs(S, nS=4)
    gate = load_gate_rep(S, D, layer, 'g1rep', 0)
    ntok = ntiles * 128
    nkeys = 256 + R * (ntok - 256)
    nch = nkeys // 128
    QT = S.sb('QT', [128, 8, ntok], BF16)
    KT = S.sb('KT', [128, 2, nkeys], BF16)
    Vf = S.sb('Vf', [128, nch, 256], BF16)
    wo = S.sb('wo', [128, 8, 1024], BF16)
    S.dma('sp', QT[:, :, :], D['QT'].rearrange('h p n -> p h n'))
    ld_KT(S, KT, D['G_KT'], slice(0, 2), ntok, R)
    ld_V(S, Vf, D['G_V'], 0, 256, ntok, R)
    S.dma('pool', wo[:, :, :], D['wo'].rearrange('(k p) m -> p k m', p=128))
    scale = 128.0 ** -0.5
    nqb = (ntok - 256) // 512
    for h in range(8):
        kv = h // 4
        chunks = [([(KT[:, kv, c * 128:(c + 1) * 128], QT[:, h, 0:256])], Vf[:, c, kv * 128:(kv + 1) * 128], 128) for c in range(2)]
        attn_head_block(S, C, A, 256, chunks, QT[:, h, 0:256], 128, scale)
        for qb in range(nqb):
            q0 = 256 + qb * 512
            chunks = [([(KT[:, kv, c * 128:(c + 1) * 128], QT[:, h, q0:q0 + 512])], Vf[:, c, kv * 128:(kv + 1) * 128], 128) for c in range(nch)]
            attn_head_block(S, C, A, 512, chunks, QT[:, h, q0:q0 + 512], 128, scale)
    out_proj_residual(S, C, A, D, QT, 8, wo, gate, ntiles, D['hin'], D['hmid'])


def build_B_mla_f(S, D, layer, ntiles, R):
    C = Ctx(S, D, npT=0)
    A = AttnBufs(S, nS=4)
    gate = load_gate_rep(S, D, layer, 'g1rep', 0)
    ntok = ntiles * 128
    nkeys = 256 + R * (ntok - 256)
    nch = nkeys // 128
    nqb = (ntok - 256) // 512
    QTn = S.sb('QTn', [128, 8, ntok], BF16)
    QTr = S.sb('QTr', [128, 8, ntok], BF16)
    KTn = S.sb('KTn', [128, nkeys], BF16)
    KTr = S.sb('KTr', [128, nkeys], BF16)
    Vh = S.sb('Vh', [128, nch, 128], BF16)
    wo = S.sb('wo', [128, 8, 1024], BF16)
    S.dma('sp', QTn[:, :, :], D['QTn'].rearrange('h p n -> p h n'))
    S.dma('sp', QTr[0:64, :, :], D['QTr'].rearrange('h p n -> p h n'))
    S.dma('pool', wo[:, :, :], D['wo'].rearrange('(k p) m -> p k m', p=128))
    for h in range(8):
        ld_KT1(S, KTn, D['G_KTn'], h, ntok, R)
        ld_KT1(S, KTr[0:64, :], D['G_KTr'], h, ntok, R)
        ld_V(S, Vh, D['G_V'], h * 128, (h + 1) * 128, ntok, R)
        chunks = [([(KTn[:, c * 128:(c + 1) * 128], QTn[:, h, 0:256]), (KTr[0:64, c * 128:(c + 1) * 128], QTr[0:64, h, 0:256])],
                   Vh[:, c, :], 128) for c in range(2)]
        attn_head_block(S, C, A, 256, chunks, QTn[:, h, 0:256], 128, 1.0)
        for qb in range(nqb):
            q0 = 256 + qb * 512
            chunks = [([(KTn[:, c * 128:(c + 1) * 128], QTn[:, h, q0:q0 + 512]), (KTr[0:64, c * 128:(c + 1) * 128], QTr[0:64, h, q0:q0 + 512])],
                       Vh[:, c, :], 128) for c in range(nch)]
            attn_head_block(S, C, A, 512, chunks, QTn[:, h, q0:q0 + 512], 128, 1.0)
    out_proj_residual(S, C, A, D, QTn, 8, wo, gate, ntiles, D['hin'], D['hmid'])


def build_B_na_f(S, D, layer, ntiles, R):
    C = Ctx(S, D, npT=0)
    A = AttnBufs(S, nS=4)
    gate = load_gate_rep(S, D, layer, 'g1rep', 0)
    ntok = ntiles * 128
    nlat = ntok - 256
    nlt = nlat // 128
    nqb = nlat // 512
    nloc = nlat + 512
    QT = S.sb('QT', [128, 8, ntok], BF16)
    KTp = [S.sb('KTp%d' % i, [128, 256 + nloc], BF16) for i in range(2)]
    Vp = [S.sb('Vp%d' % i, [128, 2 + nloc // 128, 128], BF16) for i in range(2)]
    wo = S.sb('wo', [128, 8, 1024], BF16)
    TT = [S.sb('TT%d' % i, [128, 2, 1408], BF16) for i in range(2)]
    mK = S.sb('mK', [2, 128], BF16)
    mQs = [S.sb('mQ%d' % i, [2, 8 * 512], BF16) for i in range(2)]
    hw = S.sb('hw', [128, 2, R], F32)
    slabK = [S.sb('slabK%d' % i, [128, R, 256], BF16) for i in range(2)]
    slabV = [S.sb('slabV%d' % i, [128, R, 2, 128], BF16) for i in range(2)]
    S.dma('sp', hw[:, :, :], D['hw'])
    S.dma('sp', QT[:, :, :], D['QT'].rearrange('h p n -> p h n'))
    S.dma('pool', wo[:, :, :], D['wo'].rearrange('(k p) m -> p k m', p=128))
    S.dma('sp', mK[:, :], D['maskK'])
    GK = D['G_KT']
    GV = D['G_V']
    nm = 0
    for pair in range(8):
        tt = TT[pair % 2]
        KT = KTp[pair % 2]
        Vf = Vp[pair % 2]
        pc0, pc1 = pair * 128, (pair + 1) * 128
        S.dma('sp', tt[:, :, :], D['TT'][2 * pair:2 * pair + 2].rearrange('h p n -> p h n'))
        S.dma('sp', KT[:, 0:256], D['KT'][pair, :, 0:256])
        S.dma('sp', KT[:, 512:512 + nlat], D['KT'][pair, :, 256:ntok])
        S.dma('sp', Vf[:, 0:2, :], D['Vt'][0:256, pc0:pc1].rearrange('(c p) m -> p c m', p=128))
        S.dma('sp', Vf[:, 4:4 + nlt, :], D['Vt'][256:ntok, pc0:pc1].rearrange('(c p) m -> p c m', p=128))
        for a in range(2):
            sk = slabK[a]
            sv = slabV[a]
            if a == 0:
                S.dma('sp', sk[:, :, :], GK[:, pair, :, 256:512].rearrange('r p n -> p r n'))
                for r in range(R):
                    S.dma('sp', sv[:, r, :, :], GV[r, 256:512, pc0:pc1].rearrange('(c p) m -> p c m', p=128))
                kd = KT[:, 256:512]
                vd = Vf[:, 2:4, :]
            else:
                S.dma('sp', sk[:, :, :], GK[:, pair, :, 0:256].rearrange('r p n -> p r n'))
                for r in range(R):
                    S.dma('sp', sv[:, r, :, :], GV[r, 0:256, pc0:pc1].rearrange('(c p) m -> p c m', p=128))
                kd = KT[:, 512 + nlat:768 + nlat]
                vd = Vf[:, 4 + nlt:6 + nlt, :]
            for r in range(R):
                w = hw[:, a, r:r + 1]
                if r == 0:
                    o_ts(S, 'dve', kd, sk[:, 0, :], w, None, ALU.mult)
                    o_ts(S, 'pool', vd, sv[:, 0, :, :], w, None, ALU.mult)
                else:
                    o_stt(S, 'dve', kd, sk[:, r, :], w, kd, ALU.mult, ALU.add)
                    o_stt(S, 'dve', vd, sv[:, r, :, :], w, vd, ALU.mult, ALU.add)
        for hh in range(2):
            pb = hh * 64
            chunks = [([(KT[pb:pb + 64, c * 128:(c + 1) * 128], QT[pb:pb + 64, pair, 0:256])],
                       Vf[:, c, :], 128) for c in range(2)]
            attn_head_block(S, C, A, 256, chunks, QT[pb:pb + 64, pair, 0:256], 128, 1.0, orow=(pb, pb + 64))
            for qb in range(nqb):
                q0 = 256 + qb * 512
                mQ = mQs[nm % 2]
                nm += 1
                S.dma('sp', mQ[:, :], D['maskQ'][:, qb * 4096:(qb + 1) * 4096])
                chunks = []
                for c in range(2):
                    chunks.append(([(KT[pb:pb + 64, c * 128:(c + 1) * 128], QT[pb:pb + 64, pair, q0:q0 + 512])],
                                   Vf[:, c, :], 128))
                for c in range(8):
                    m = 4 * qb + c
                    i0 = 14 - 2 * c
                    pieces = [(KT[pb:pb + 64, 256 + m * 128:256 + (m + 1) * 128], QT[pb:pb + 64, pair, q0:q0 + 512]),
                              (C.ident[:, :], tt[:, hh, i0 * 64:i0 * 64 + 512]),
                              (mK[0:2, :], mQ[0:2, c * 512:(c + 1) * 512])]
                    chunks.append((pieces, Vf[:, 2 + m, :], 128))
                attn_head_block(S, C, A, 512, chunks, QT[pb:pb + 64, pair, q0:q0 + 512], 128, 1.0, orow=(pb, pb + 64))
    out_proj_residual(S, C, A, D, QT, 8, wo, gate, ntiles, D['hin'], D['hmid'])


GROUPS = [[0, 1, 2, 3], [4, 5, 6, 7]]


def build_fused(S, X, nc, ntiles=18, R=4, groups=None):
    groups = groups or GROUPS
    bf = BF16
    ntok = ntiles * 128
    nlat = ntok - 256
    N = ntiles - 2

    def dram(name, shape, dt):
        return nc.dram_tensor(name, list(shape), dt).ap()

    def gather(name, src, rows, cols, dt):
        g = dram(name, [R * rows, cols], dt)
        z = mybir.dt.size(dt)
        maxrows = max(16, ((1 << 20) // (cols * z)) // 16 * 16)
        if rows <= maxrows:
            S.cc('AllGather', groups, src.opt(), g.opt())
            return g
        p0 = 0
        while p0 < rows:
            rp = min(maxrows, rows - p0)
            tmp = dram('%s_t%d' % (name, p0), [R * rp, cols], dt)
            S.cc('AllGather', groups, src[p0:p0 + rp, :].opt(), tmp.opt())
            for r in range(R):
                S.dma('sp', g[r * rows + p0:r * rows + p0 + rp, :], tmp[r * rp:(r + 1) * rp, :])
            p0 += rp
        return g

    S.final_names = {'out'}
    base = dict(ident=X['ident'])
    mpart = dram('mpart', [8, 1536], F32)
    Dm = dict(base, cin=X['cin'], mod_w=X['mod_w'], mod_b=X['mod_b'], modv=mpart.rearrange('(s i) c -> s i c', s=2))
    build_M(S, Dm, ncol=1536, nrow=2)
    gm = gather('G_mod', mpart, 8, 1536, F32)
    modv = dram('modv', [4, 2, 6144], F32)
    for r in range(R):
        for s_ in range(2):
            S.dma('sp', modv[:, s_, r * 1536:(r + 1) * 1536], gm[r * 8 + s_ * 4:r * 8 + s_ * 4 + 4, :])
    base['modv'] = modv
    hcur = X['hin']
    S.new_phase()
    QTn = dram('QTn0', [8, 128, ntok], bf)
    QTr = dram('QTr0', [8, 64, ntok], bf)
    KTn = dram('KTn0', [8, 128, ntok], bf)
    KTr = dram('KTr0', [8, 64, ntok], bf)
    Vt = dram('Vt0', [ntok, 1024], bf)
    build_A_mla(S, dict(base, norm1=X['norm1_0'], wdown=X['wdown'], wuq=X['wuq'], wukv=X['wukv'], lg=X['lg'], gq=X['gq0'], gk=X['gk0'],
                        cos=X['cos64'], sin=X['sin64'], hin=hcur, QTn=QTn, QTr=QTr, KTn=KTn, KTr=KTr, Vt=Vt), 0, ntiles=ntiles)
    gkn = gather('G_KTn0', KTn.rearrange('h p n -> (h p) n'), 8 * 128, ntok, bf).rearrange('(r h p) n -> r h p n', r=R, h=8)
    gkr = gather('G_KTr0', KTr.rearrange('h p n -> (h p) n'), 8 * 64, ntok, bf).rearrange('(r h p) n -> r h p n', r=R, h=8)
    gv = gather('G_V0', Vt, ntok, 1024, bf).rearrange('(r t) m -> r t m', r=R)
    S.new_phase()
    hmid = dram('hmid0', [ntok, 1024], F32)
    build_B_mla_f(S, dict(base, QTn=QTn, QTr=QTr, G_KTn=gkn, G_KTr=gkr, G_V=gv, wo=X['wo0'], hin=hcur, hmid=hmid), 0, ntiles, R)
    S.new_phase()
    h1 = dram('h1', [ntok, 1024], F32)
    build_C(S, dict(base, norm2=X['norm2_0'], w13=X['w13_0'], w2=X['w2_0'], hin=hmid, hout=h1), 0, ntiles=ntiles)
    hcur = h1
    S.new_phase()
    QT = dram('QT1', [8, 128, ntok], bf)
    KT = dram('KT1', [2, 128, ntok], bf)
    Vt = dram('Vt1', [ntok, 256], bf)
    build_A_gqa(S, dict(base, norm1=X['norm1_1'], wqkv=X['wqkv1'], gq=X['gq1'], cos=X['cos128'], sin=X['sin128'], hin=hcur, QT=QT, KT=KT, Vt=Vt),
                1, ntiles=ntiles)
    gk = gather('G_KT1', KT.rearrange('h p n -> (h p) n'), 2 * 128, ntok, bf).rearrange('(r h p) n -> r h p n', r=R, h=2)
    gv = gather('G_V1', Vt, ntok, 256, bf).rearrange('(r t) m -> r t m', r=R)
    S.new_phase()
    hmid = dram('hmid1', [ntok, 1024], F32)
    build_B_gqa_f(S, dict(base, QT=QT, G_KT=gk, G_V=gv, wo=X['wo1'], hin=hcur, hmid=hmid), 1, ntiles, R)
    S.new_phase()
    h2 = dram('h2', [ntok, 1024], F32)
    build_C(S, dict(base, norm2=X['norm2_1'], w13=X['w13_1'], w2=X['w2_1'], hin=hmid, hout=h2), 1, ntiles=ntiles)
    hcur = h2
    S.new_phase()
    QT = dram('QT2', [8, 128, ntok], bf)
    KT = dram('KT2', [8, 128, ntok], bf)
    Vt = dram('Vt2', [ntok, 1024], bf)
    KTh = dram('KTh2', [8, 128, 512], bf)
    Vh = dram('Vh2', [512, 1024], bf)
    build_A_na(S, dict(base, norm1=X['norm1_2'], wqkv=X['wqkv2'], gq=X['gq2'], hin=hcur, QT=QT, KT=KT, Vt=Vt, KTh=KTh, Vh=Vh), 2, ntiles=ntiles)
    gk = gather('G_KT2', KTh.rearrange('h p n -> (h p) n'), 8 * 128, 512, bf).rearrange('(r h p) n -> r h p n', r=R, h=8)
    gv = gather('G_V2', Vh, 512, 1024, bf).rearrange('(r t) m -> r t m', r=R)
    S.new_phase()
    hmid = dram('hmid2', [ntok, 1024], F32)
    build_B_na_f(S, dict(base, QT=QT, KT=KT, Vt=Vt, G_KT=gk, G_V=gv, TT=X['TT'], maskK=X['maskK'], maskQ=X['maskQ'], hw=X['hw'],
                         wo=X['wo2'], hin=hcur, hmid=hmid), 2, ntiles, R)
    S.new_phase()
    h3 = dram('h3', [ntok, 1024], F32)
    build_C(S, dict(base, norm2=X['norm2_2'], w13=X['w13_2'], w2=X['w2_2'], hin=hmid, hout=h3), 2, ntiles=ntiles)
    hcur = h3
    S.new_phase()
    xt = [4, 2, 128, nlat]
    T = dict(QfT=dram('QfT', xt, bf), QbT=dram('QbT', xt, bf), KfT=dram('KfT', xt, bf), KbT=dram('KbT', xt, bf),
             Kf=dram('Kf', [ntok, 1024], bf), Kb=dram('Kb', [ntok, 1024], bf), Vt=dram('Vt3', [ntok, 2048], bf), Gs=dram('Gs', [nlat, 2048], bf))
    build_A_ret(S, dict(base, norm1=X['norm1_3'], wqkvg=X['wqkvg'], ldrep=X['ldrep'], E=X['E'], Ec=X['Ec'], cos=X['cos256'], sin=X['sin256'],
                        hin=hcur, **T), 3, ntiles=ntiles)
    S.new_phase()
    Lfb = dram('Lfb', [2, 4, 2, 128, 512], F32)
    Sc = dram('Sc', [2, 4, 2, 128, 512], F32)
    build_A2_ret(S, dict(base, ldrep=X['ldrep'], wmul=X['wmul'], Kf=T['Kf'], Kb=T['Kb'], Vt=T['Vt'], Lfb=Lfb, Sc=Sc), 3, ntiles=ntiles)
    gl = gather('G_L', Lfb.rearrange('d h f p n -> (d h f p) n'), 2048, 512, F32).rearrange('(r d h f p) n -> r d h f p n', r=R, d=2, h=4, f=2)
    S.new_phase()
    hmid = dram('hmid3', [nlat, 1024], F32)
    yf = dram('yf', [nlat, 2048], F32)
    build_B_ret(S, dict(base, ldrep=X['ldrep'], coefm=X['coefm'], G_L=gl, Sc=Sc, maskD=X['maskD'], onorm=X['onorm'], wo=X['wo3'], hin=hcur,
                        hmid=hmid, yf=yf, **T), 3, nchunks=N, nsrc=R + 1)
    S.new_phase()
    build_C(S, dict(base, norm2=X['norm2_3'], w13=X['w13_3'], w2=X['w2_3'], hin=hmid, hout=X['out']), 3, ntiles=N, nctx=0)


NPDT = {np.dtype(np.float32): F32, np.dtype(ml_dtypes.bfloat16): BF16}


def launch(build_fn, in_maps, outs, **kw):
    nc = bass.Bass("TRN2", target_bir_lowering=False)
    S = Sched(nc)
    D = {}
    for name, arr in in_maps[0].items():
        D[name] = nc.dram_tensor(name, list(arr.shape), NPDT[arr.dtype], kind="ExternalInput").ap()
    for name, (shape, dt) in outs.items():
        D[name] = nc.dram_tensor(name, list(shape), NPDT[np.dtype(dt)], kind="ExternalOutput").ap()
    build_fn(S, D, **kw)
    S.emit()
    res = run_bass_kernel_spmd(nc, in_maps, core_ids=list(range(len(in_maps))))
    return [{k: np.asarray(r[k]) for k in outs} for r in res.results]


def fm(vec):
    return np.ascontiguousarray(np.asarray(vec, np.float32).reshape(-1, 128).T)


IDENT = np.eye(128, dtype=np.float32)


GRID_W = 64


def rope_tables_np(dim, j):
    t = np.arange(j * 2048, (j + 1) * 2048)
    row = (t // GRID_W).astype(np.float32)
    col = (t % GRID_W).astype(np.float32)
    quarter = dim // 4
    inv_freq = (10000.0 ** (-np.arange(quarter, dtype=np.float32) / quarter)).astype(np.float32)
    ang = np.concatenate([row[:, None] * inv_freq, col[:, None] * inv_freq], axis=-1).astype(np.float32)
    c = np.cos(ang).astype(np.float32).reshape(16, 128, dim // 2).transpose(1, 0, 2)
    s = np.sin(ang).astype(np.float32).reshape(16, 128, dim // 2).transpose(1, 0, 2)
    return np.ascontiguousarray(c), np.ascontiguousarray(s)


def rep128(v):
    return np.ascontiguousarray(np.broadcast_to(np.asarray(v, np.float32).reshape(1, -1), (128, v.size)))


def run_layer_gqa(LA, d, L, hcat, modv, NC=8):
    j0 = L // 4
    in_maps = []
    for c in range(NC):
        b, j = c // 4, c % 4
        cs, sn = rope_tables_np(128, j)
        gq = np.concatenate([np.tile(d['gqa_q_norm'][j0], 8), np.tile(d['gqa_k_norm'][j0], 2)])
        in_maps.append(dict(ident=IDENT, modv=modv[c], norm1=fm(d['norm1'][L]), wqkv=d['gqa_w_qkv'][j0], gq=rep128(gq),
                            cos=cs, sin=sn, hin=hcat[c]))
    bf = ml_dtypes.bfloat16
    resA = LA(build_A_gqa, in_maps, dict(QT=((8, 128, 2304), bf), KT=((2, 128, 2304), bf), Vt=((2304, 256), bf)), layer=L)
    in_maps = []
    for c in range(NC):
        b, j = c // 4, c % 4
        grp = [resA[b * 4 + jj] for jj in range(4)]
        KTf = np.concatenate([resA[c]['KT'][:, :, 0:256]] + [g['KT'][:, :, 256:] for g in grp], axis=2)
        Vf = np.concatenate([resA[c]['Vt'][0:256]] + [g['Vt'][256:] for g in grp], axis=0)
        in_maps.append(dict(ident=IDENT, modv=modv[c], QT=resA[c]['QT'], KTf=np.ascontiguousarray(KTf), Vf=np.ascontiguousarray(Vf),
                            wo=d['gqa_w_o'][j0], hin=hcat[c]))
    resB = LA(build_B_gqa, in_maps, dict(hmid=((2304, 1024), np.float32)), layer=L)
    return [x['hmid'] for x in resB]


def run_ffn(LA, d, L, hmid, modv, NC=8, nctx=2):
    in_maps = []
    nrow = hmid[0].shape[0]
    for c in range(NC):
        in_maps.append(dict(ident=IDENT, modv=modv[c], norm2=fm(d['norm2'][L]), w13=d['ffn_w13'][L], w2=d['ffn_w2'][L], hin=hmid[c]))
    res = LA(build_C, in_maps, dict(hout=((nrow, 1024), np.float32)), layer=L, ntiles=nrow // 128, nctx=nctx)
    return [x['hout'] for x in res]


def run_mod(LA, d, NC=8):
    cin = np.ascontiguousarray(np.stack([fm(d['c'][0]), fm(d['c'][1]), fm(d['c_ctx'])], axis=-1))
    ncol = 6144 // NC
    in_maps = []
    for c in range(NC):
        in_maps.append(dict(ident=IDENT, cin=cin, mod_w=np.ascontiguousarray(d['mod_w'][:, :, c * ncol:(c + 1) * ncol]),
                            mod_b=np.ascontiguousarray(d['mod_b'][:, c * ncol:(c + 1) * ncol])))
    res = LA(build_M, in_maps, dict(modv=((3, 4, ncol), np.float32)), ncol=ncol)
    full = np.concatenate([r_['modv'] for r_ in res], axis=2)
    out = []
    for c in range(NC):
        b = c // 4
        out.append(np.ascontiguousarray(np.stack([full[b], full[2]], axis=1)))
    return out


def kernel_unfused(**inputs):
    d = {k: np.asarray(v) for k, v in inputs.items()}
    NC = 8
    modv = run_mod(launch, d, NC)
    hcat = []
    for c in range(NC):
        b, j = c // 4, c % 4
        hcat.append(np.ascontiguousarray(np.concatenate([d['ctx'][b], d['x'][b, j * 2048:(j + 1) * 2048]], 0).astype(np.float32)))
    hmid = run_layer_mla(launch, d, 0, hcat, modv)
    hcat = run_ffn(launch, d, 0, hmid, modv)
    hmid = run_layer_gqa(launch, d, 1, hcat, modv)
    hcat = run_ffn(launch, d, 1, hmid, modv)
    hmid = run_layer_na(launch, d, 2, hcat, modv)
    hcat = run_ffn(launch, d, 2, hmid, modv)
    hmid = run_layer_ret(launch, d, 3, hcat, modv)
    hfin = run_ffn(launch, d, 3, hmid, modv, nctx=0)
    out = np.zeros((2, 8192, 1024), np.float32)
    for c in range(NC):
        b, j = c // 4, c % 4
        out[b, j * 2048:(j + 1) * 2048] = hfin[c]
    return out


def na_tables(rpb):
    H = rpb.shape[0]
    kc = np.arange(64)[:, None]
    qc = np.arange(64)[None, :]
    cs = np.clip(qc - 8, 0, 48)
    colvalid = (kc >= cs) & (kc < cs + 16)
    crel = np.clip(kc - qc + 15, 0, 30)
    TT = np.zeros((H, 2, 64, 22, 64), np.float32)
    for a in range(2):
        for i2 in range(22):
            i = i2 - 3 - a
            if 0 <= i <= 14:
                blk = rpb[:, 14 - i][:, crel]
                blk = np.where(colvalid[None], blk, np.float32(-30000.0))
            else:
                blk = np.broadcast_to(np.where(colvalid, np.float32(0), np.float32(-30000.0))[None], (H, 64, 64))
            TT[:, a, :, i2, :] = blk
    return np.ascontiguousarray(TT.reshape(H, 128, 22 * 64)).astype(ml_dtypes.bfloat16)


def na_masks(row_base, rows_total, nqb=4):
    mQ = np.zeros((2, nqb, 8, 8, 64), np.float32)
    for qb in range(nqb):
        R = row_base + 8 * qb
        for c in range(8):
            kr0 = R - 4 + 2 * c
            for a in range(2):
                kr = kr0 + a
                for t in range(8):
                    qr = R + t
                    r0 = min(max(qr - 4, 0), rows_total - 8)
                    ok = (0 <= kr < rows_total) and (r0 <= kr < r0 + 8)
                    mQ[a, qb, c, t, :] = 0.0 if ok else -30000.0
    mK = np.zeros((2, 2, 64), np.float32)
    mK[0, 0] = 1.0
    mK[1, 1] = 1.0
    bf = ml_dtypes.bfloat16
    return np.ascontiguousarray(mQ.reshape(2, -1)).astype(bf), np.ascontiguousarray(mK.reshape(2, 128)).astype(bf)


def run_layer_na(LA, d, L, hcat, modv, NC=8, ntiles=18, rows_total=128, cores_per_batch=4):
    j0 = L // 4
    bf = ml_dtypes.bfloat16
    ntok = ntiles * 128
    nlat = ntok - 256
    rows_core = nlat // 64
    in_maps = []
    gq = np.concatenate([np.tile(d['na_q_norm'][j0], 16), np.tile(d['na_k_norm'][j0], 16)])
    for c in range(NC):
        in_maps.append(dict(ident=IDENT, modv=modv[c], norm1=fm(d['norm1'][L]), wqkv=d['na_w_qkv'][j0], gq=rep128(gq), hin=hcat[c]))
    resA = LA(build_A_na, in_maps, dict(QT=((8, 128, ntok), bf), KT=((8, 128, ntok), bf), Vt=((ntok, 1024), bf)), layer=L, ntiles=ntiles)
    TT = na_tables(np.asarray(d['na_rpb'][j0]))
    in_maps = []
    for c in range(NC):
        b, j = c // cores_per_batch, c % cores_per_batch
        grp = [resA[b * cores_per_batch + jj] for jj in range(cores_per_batch)]
        Kall = np.concatenate([g['KT'][:, :, 256:] for g in grp], axis=2)
        Vall = np.concatenate([g['Vt'][256:] for g in grp], axis=0)
        row_base = j * rows_core
        KTl = np.zeros((8, 128, 2560), bf)
        Vl = np.zeros((2560, 1024), bf)
        lo = row_base - 4
        for lr in range(40):
            gr = lo + lr
            if 0 <= gr < rows_total:
                KTl[:, :, lr * 64:(lr + 1) * 64] = Kall[:, :, gr * 64:(gr + 1) * 64]
                Vl[lr * 64:(lr + 1) * 64] = Vall[gr * 64:(gr + 1) * 64]
        mQ, mK = na_masks(row_base, rows_total, nqb=4)
        in_maps.append(dict(ident=IDENT, modv=modv[c], QT=resA[c]['QT'], KTc=np.ascontiguousarray(resA[c]['KT'][:, :, 0:256]), KTl=KTl,
                            Vc=np.ascontiguousarray(resA[c]['Vt'][0:256]), Vl=Vl, TT=TT, maskK=mK, maskQ=mQ,
                            wo=d['na_w_o'][j0], hin=hcat[c]))
    resB = LA(build_B_na, in_maps, dict(hmid=((ntok, 1024), np.float32)), layer=L, ntiles=ntiles)
    return [x['hmid'] for x in resB]


def run_layer_mla(LA, d, L, hcat, modv, NC=8, ntiles=18, cores_per_batch=4):
    j0 = L // 4
    bf = ml_dtypes.bfloat16
    ntok = ntiles * 128
    lg = np.concatenate([d['mla_q_lora_norm'][j0], d['mla_kv_lora_norm'][j0]])
    in_maps = []
    for c in range(NC):
        b, j = c // cores_per_batch, c % cores_per_batch
        cs, sn = rope_tables_np(64, j)
        in_maps.append(dict(ident=IDENT, modv=modv[c], norm1=fm(d['norm1'][L]), wdown=d['mla_w_down'][j0], wuq=d['mla_w_uq'][j0],
                            wukv=d['mla_w_ukv'][j0], lg=fm(lg), gq=rep128(np.tile(d['mla_q_norm'][j0], 8)), gk=rep128(d['mla_k_norm'][j0]),
                            cos=cs, sin=sn, hin=hcat[c]))
    resA = LA(build_A_mla, in_maps, dict(QTn=((8, 128, ntok), bf), QTr=((8, 64, ntok), bf), KTn=((8, 128, ntok), bf),
                                         KTr=((8, 64, ntok), bf), Vt=((ntok, 1024), bf)), layer=L, ntiles=ntiles)
    in_maps = []
    for c in range(NC):
        b, j = c // cores_per_batch, c % cores_per_batch
        grp = [resA[b * cores_per_batch + jj] for jj in range(cores_per_batch)]
        KTnf = np.concatenate([resA[c]['KTn'][:, :, 0:256]] + [g['KTn'][:, :, 256:] for g in grp], axis=2)
        KTrf = np.concatenate([resA[c]['KTr'][:, :, 0:256]] + [g['KTr'][:, :, 256:] for g in grp], axis=2)
        Vf = np.concatenate([resA[c]['Vt'][0:256]] + [g['Vt'][256:] for g in grp], axis=0)
        in_maps.append(dict(ident=IDENT, modv=modv[c], QTn=resA[c]['QTn'], QTr=resA[c]['QTr'], KTnf=np.ascontiguousarray(KTnf),
                            KTrf=np.ascontiguousarray(KTrf), Vf=np.ascontiguousarray(Vf), wo=d['mla_w_o'][j0], hin=hcat[c]))
    nkeys = in_maps[0]['Vf'].shape[0]
    resB = LA(build_B_mla, in_maps, dict(hmid=((ntok, 1024), np.float32)), layer=L, ntiles=ntiles, nkeys=nkeys)
    return [x['hmid'] for x in resB]


def run_layer_ret(LA, d, L, hcat, modv, NC=8, ntiles=18, cores_per_batch=4):
    j0 = L // 4
    bf = ml_dtypes.bfloat16
    ntok = ntiles * 128
    N = ntiles - 2
    nlat = N * 128
    J = cores_per_batch
    p = np.arange(128, dtype=np.float32)
    E = np.stack([p, 127 - p, -p, -(127 - p)], axis=1).astype(np.float32)
    Ec = np.zeros((128, 2, 2), np.float32)
    for tt_ in range(2):
        m = tt_ * 128 + p
        Ec[:, tt_, 0] = 256 - m
        Ec[:, tt_, 1] = m + 1
    wmul = np.zeros((128, 2, ntiles), np.float32)
    for n in range(N):
        wmul[:, 0, 2 + n] = 128 * (N - n)
        wmul[:, 1, 2 + n] = 128 * (n + 1)
    ldrep = np.ascontiguousarray(np.broadcast_to(np.stack([d['ret_log_decay_fwd'][j0], d['ret_log_decay_bwd'][j0]])[None], (128, 2, 4))).astype(np.float32)
    tq = np.arange(128)
    maskD = np.stack([(tq[:, None] <= tq[None, :]), (tq[:, None] >= tq[None, :])]).astype(np.float32).astype(bf)
    in_maps = []
    for c in range(NC):
        b, j = c // J, c % J
        cs, sn = rope_tables_np(256, j)
        in_maps.append(dict(ident=IDENT, modv=modv[c], norm1=fm(d['norm1'][L]), wqkvg=d['ret_w_qkvg'][j0], ldrep=ldrep, E=E, Ec=Ec,
                            cos=cs[:, :N], sin=sn[:, :N], hin=hcat[c]))
        in_maps[-1]['cos'] = np.ascontiguousarray(in_maps[-1]['cos'])
        in_maps[-1]['sin'] = np.ascontiguousarray(in_maps[-1]['sin'])
    xt = (4, 2, 128, nlat)
    resA = LA(build_A_ret, in_maps, dict(QfT=(xt, bf), QbT=(xt, bf), KfT=(xt, bf), KbT=(xt, bf), Kf=((ntok, 1024), bf), Kb=((ntok, 1024), bf),
                                         Vt=((ntok, 2048), bf), Gs=((nlat, 2048), bf)), layer=L, ntiles=ntiles)
    in_maps = [dict(ident=IDENT, ldrep=ldrep, wmul=wmul, Kf=resA[c]['Kf'], Kb=resA[c]['Kb'], Vt=resA[c]['Vt']) for c in range(NC)]
    st = (4, 2, 128, 512)
    resA2 = LA(build_A2_ret, in_maps, dict(Scf=(st, np.float32), Scb=(st, np.float32), Lf=(st, np.float32), Lb=(st, np.float32)), layer=L, ntiles=ntiles)
    in_maps = []
    for c in range(NC):
        b, j = c // J, c % J
        grp = [resA2[b * J + jj] for jj in range(J)]
        Lall = np.zeros((J + 1, 2, 4, 2, 128, 512), np.float32)
        for i in range(J):
            Lall[i, 0] = grp[i]['Lf']
            Lall[i, 1] = grp[i]['Lb']
        Lall[J, 0] = resA2[c]['Scf']
        Lall[J, 1] = resA2[c]['Scb']
        cm = np.zeros((128, 2, J + 1, 2), np.float32)
        for i in range(J):
            if i < j:
                cm[:, 0, i] = (128 * N * (j - 1 - i), 1.0)
            if i > j:
                cm[:, 1, i] = (128 * N * (i - 1 - j), 1.0)
        cm[:, 0, J] = (128 * N * j, 1.0)
        cm[:, 1, J] = (128 * N * (J - 1 - j), 1.0)
        a = resA[c]
        in_maps.append(dict(ident=IDENT, modv=modv[c], ldrep=ldrep, coefm=cm, Lall=Lall, maskD=maskD, onorm=rep128(d['ret_out_norm'][j0]),
                            wo=d['ret_w_o'][j0], QfT=a['QfT'], QbT=a['QbT'], KfT=a['KfT'], KbT=a['KbT'], Kf=a['Kf'], Kb=a['Kb'], Vt=a['Vt'],
                            Gs=a['Gs'], hin=hcat[c]))
    resB = LA(build_B_ret, in_maps, dict(hmid=((nlat, 1024), np.float32), yf=((nlat, 2048), np.float32)), layer=L, nchunks=N, nsrc=J + 1)
    return [x['hmid'] for x in resB]


def rope_tables_gen(dim, j, nlat):
    t = np.arange(j * nlat, (j + 1) * nlat)
    row = (t // GRID_W).astype(np.float32)
    col = (t % GRID_W).astype(np.float32)
    quarter = dim // 4
    inv_freq = (10000.0 ** (-np.arange(quarter, dtype=np.float32) / quarter)).astype(np.float32)
    ang = np.concatenate([row[:, None] * inv_freq, col[:, None] * inv_freq], axis=-1).astype(np.float32)
    nt = nlat // 128
    c = np.cos(ang).astype(np.float32).reshape(nt, 128, dim // 2).transpose(1, 0, 2)
    s = np.sin(ang).astype(np.float32).reshape(nt, 128, dim // 2).transpose(1, 0, 2)
    return np.ascontiguousarray(c), np.ascontiguousarray(s)


def fused_inputs(d, NC=8, ntiles=18, R=4):
    bf = ml_dtypes.bfloat16
    ntok = ntiles * 128
    nlat = ntok - 256
    N = ntiles - 2
    rows_core = nlat // 64
    rows_total = rows_core * R
    f32 = lambda a: np.ascontiguousarray(np.asarray(a, np.float32))
    common = dict(ident=IDENT)
    for L in range(4):
        common['norm1_%d' % L] = fm(d['norm1'][L])
        common['norm2_%d' % L] = fm(d['norm2'][L])
        common['w13_%d' % L] = f32(d['ffn_w13'][L])
        common['w2_%d' % L] = f32(d['ffn_w2'][L])
    lg = np.concatenate([d['mla_q_lora_norm'][0], d['mla_kv_lora_norm'][0]])
    common.update(wdown=f32(d['mla_w_down'][0]), wuq=f32(d['mla_w_uq'][0]), wukv=f32(d['mla_w_ukv'][0]), lg=fm(lg),
                  gq0=rep128(np.tile(d['mla_q_norm'][0], 8)), gk0=rep128(d['mla_k_norm'][0]), wo0=f32(d['mla_w_o'][0]))
    common.update(wqkv1=f32(d['gqa_w_qkv'][0]), gq1=rep128(np.concatenate([np.tile(d['gqa_q_norm'][0], 8), np.tile(d['gqa_k_norm'][0], 2)])),
                  wo1=f32(d['gqa_w_o'][0]))
    common.update(wqkv2=f32(d['na_w_qkv'][0]), gq2=rep128(np.concatenate([np.tile(d['na_q_norm'][0], 16), np.tile(d['na_k_norm'][0], 16)])),
                  TT=na_tables(np.asarray(d['na_rpb'][0])), wo2=f32(d['na_w_o'][0]))
    p = np.arange(128, dtype=np.float32)
    E = np.stack([p, 127 - p, -p, -(127 - p)], axis=1).astype(np.float32)
    Ec = np.zeros((128, 2, 2), np.float32)
    for tt_ in range(2):
        m = tt_ * 128 + p
        Ec[:, tt_, 0] = 256 - m
        Ec[:, tt_, 1] = m + 1
    wmul = np.zeros((128, 2, ntiles), np.float32)
    for n in range(N):
        wmul[:, 0, 2 + n] = 128 * (N - n)
        wmul[:, 1, 2 + n] = 128 * (n + 1)
    ldrep = np.ascontiguousarray(np.broadcast_to(np.stack([d['ret_log_decay_fwd'][0], d['ret_log_decay_bwd'][0]])[None], (128, 2, 4))).astype(np.float32)
    tq = np.arange(128)
    maskD = np.stack([(tq[:, None] <= tq[None, :]), (tq[:, None] >= tq[None, :])]).astype(np.float32).astype(bf)
    common.update(wqkvg=f32(d['ret_w_qkvg'][0]), ldrep=ldrep, E=E, Ec=Ec, wmul=wmul, maskD=maskD, onorm=rep128(d['ret_out_norm'][0]),
                  wo3=f32(d['ret_w_o'][0]))
    in_maps = []
    for c in range(NC):
        b, j = c // R, c % R
        m = dict(common)
        m['hin'] = np.ascontiguousarray(np.concatenate([d['ctx'][b], d['x'][b, j * nlat:(j + 1) * nlat]], 0).astype(np.float32))
        m['cin'] = np.ascontiguousarray(np.stack([fm(d['c'][b]), fm(d['c_ctx'])], axis=-1))
        m['mod_w'] = np.ascontiguousarray(d['mod_w'][:, :, j * 1536:(j + 1) * 1536])
        m['mod_b'] = np.ascontiguousarray(d['mod_b'][:, j * 1536:(j + 1) * 1536])
        for dim in (64, 128, 256):
            cs, sn = rope_tables_gen(dim, j, nlat)
            m['cos%d' % dim] = cs
            m['sin%d' % dim] = sn
        mQ, mK = na_masks(j * rows_core, rows_total, nqb=nlat // 512)
        m['maskQ'] = mQ
        m['maskK'] = mK
        hw = np.zeros((128, 2, R), np.float32)
        if j - 1 >= 0:
            hw[:, 0, j - 1] = 1.0
        if j + 1 < R:
            hw[:, 1, j + 1] = 1.0
        m['hw'] = hw
        cm = np.zeros((128, 2, R + 1, 2), np.float32)
        for i in range(R):
            if i < j:
                cm[:, 0, i] = (128 * N * (j - 1 - i), 1.0)
            if i > j:
                cm[:, 1, i] = (128 * N * (i - 1 - j), 1.0)
        cm[:, 0, R] = (128 * N * j, 1.0)
        cm[:, 1, R] = (128 * N * (R - 1 - j), 1.0)
        m['coefm'] = cm
        in_maps.append(m)
    return in_maps


ARENA_BYTES = 200 * 1024


def build_program(in_map0, ntiles, R, groups):
    nc = bass.Bass("TRN2", target_bir_lowering=False)
    S = Sched(nc)
    S.use_arena(ARENA_BYTES)
    X = {}
    for name, arr in in_map0.items():
        X[name] = nc.dram_tensor(name, list(arr.shape), NPDT[arr.dtype], kind="ExternalInput").ap()
    X['out'] = nc.dram_tensor('out', [(ntiles - 2) * 128, 1024], F32, kind="ExternalOutput").ap()
    build_fused(S, X, nc, ntiles=ntiles, R=R, groups=groups)
    S.emit()
    return nc, S


def kernel(**inputs):
    d = {k: np.asarray(v) for k, v in inputs.items()}
    NC, R, ntiles = 8, 4, 18
    in_maps = fused_inputs(d, NC, ntiles, R)
    nc, S = build_program(in_maps[0], ntiles, R, GROUPS)
    res = run_bass_kernel_spmd(nc, in_maps, core_ids=list(range(NC)))
    nlat = (ntiles - 2) * 128
    out = np.zeros((2, R * nlat, 1024), np.float32)
    for c in range(NC):
        b, j = c // R, c % R
        out[b, j * nlat:(j + 1) * nlat] = np.asarray(res.results[c]['out'])
    return out
```

```python
import bisect
import ml_dtypes


import numpy as np
import concourse.bass as bass
import concourse.mybir as mybir
from concourse.bass_utils import run_bass_kernel_spmd

F32 = mybir.dt.float32
BF16 = mybir.dt.bfloat16
AF = mybir.ActivationFunctionType
ALU = mybir.AluOpType
AX = mybir.AxisListType

SEM_CHUNK = 30000
N_DMA_SEMS = 12


DTSIZE = {}


def _box(ap):
    t = ap.tensor
    dims = list(ap.ap)
    off = ap.offset
    shape = list(t.shape)
    cls = type(t).__name__
    if cls.startswith('DRam'):
        lo = off
        hi = off
        for st, cnt in dims:
            if st >= 0:
                hi += st * (cnt - 1)
            else:
                lo += st * (cnt - 1)
        return (0, 1, lo, hi + 1)
    row = 1
    for s in shape[1:]:
        row *= s
    if cls.startswith('PSum'):
        return (0, 128, 0, 1 << 20)
    pst, pcnt = dims[0]
    if pst == 0:
        pst = row
    p0 = off // row
    f0 = off % row
    assert pst % row == 0 or pcnt == 1, (pst, row, dims)
    p1 = p0 + (pst // row) * (pcnt - 1) + 1
    f1 = f0
    for st, cnt in dims[1:]:
        assert st >= 0
        f1 += st * (cnt - 1)
    if t.name == 'arena':
        z = mybir.dt.size(ap.dtype)
        return (p0, p1, f0 * z, (f1 + 1) * z)
    return (p0, p1, f0, f1 + 1)


def _ovl(a, b):
    return a[0] < b[1] and b[0] < a[1] and a[2] < b[3] and b[2] < a[3]


def _cov(a, b):
    return a[0] <= b[0] and a[1] >= b[1] and a[2] <= b[2] and a[3] >= b[3]


class Op:
    __slots__ = ('eng', 'idx', 'fn', 'deps', 'is_dma', 'signal', 'dsem', 'dval', 'dprev', 'sigcount')

    def __init__(self, eng, idx, fn, is_dma):
        self.eng = eng
        self.idx = idx
        self.fn = fn
        self.deps = []
        self.is_dma = is_dma
        self.signal = False
        self.dsem = None
        self.dval = 0
        self.dprev = None
        self.sigcount = 0


class Sched:
    ENGS = ('pe', 'act', 'dve', 'pool', 'sp')

    def __init__(self, nc, same_engine_sync=True):
        self.nc = nc
        self.ops = {e: [] for e in self.ENGS}
        self.rec = {}
        self.same = same_engine_sync
        self.wm = {e: {} for e in self.ENGS}
        self.dma_rr = {e: 0 for e in self.ENGS}
        self.dma_last = {}
        self.dma_cnt = {}
        self.out_dmas = []
        self.nalloc = 0
        self.final_names = None
        self.arena_bytes = 0
        self.arena = None
        self.arena_views = {}
        self.bump = 0
        self.phase = 0
        self.allocs = []
        self.banks = None
        self.nbank = 0
        self.ccs = []

    def use_arena(self, nbytes):
        self.arena_bytes = nbytes
        self.arena = self.nc.alloc_sbuf_tensor('arena', [128, nbytes // 2], BF16)
        self.arena_views = {BF16: self.arena, F32: self.arena.bitcast(F32)}
        self.banks = [self.nc.alloc_psum_tensor('bank%d' % i, [128, 512], F32) for i in range(8)]

    def new_phase(self):
        self.barrier()
        self.phase += 1
        self.bump = 0
        self.allocs = []
        self.nbank = 0

    def barrier(self):
        lasts = {}
        for e in self.ENGS:
            for o in reversed(self.ops[e]):
                if (not o.is_dma) and o.fn is not None and o.fn != 'final':
                    lasts[e] = o
                    break
        for e in self.ENGS:
            b = Op(e, len(self.ops[e]), None, False)
            for e2, o in lasts.items():
                if e2 != e:
                    self._add_dep(b, o)
            for key, o in self.dma_last.items():
                self._add_dep(b, o)
            for o in self.ccs:
                self._add_dep(b, o)
            self.ops[e].append(b)

    def sb(self, name, shape, dt):
        self.nalloc += 1
        if self.arena is None:
            return self.nc.alloc_sbuf_tensor('sb_' + name, list(shape), dt)
        z = mybir.dt.size(dt)
        n = 1
        for x in shape[1:]:
            n *= x
        nbytes = (n * z + 63) // 64 * 64
        off = self.bump
        self.bump += nbytes
        assert self.bump <= self.arena_bytes, ('arena overflow', name, self.bump)
        self.allocs.append((off, off + nbytes))
        h = self.arena_views[dt]
        row = self.arena_bytes // z
        dims = [[row, shape[0]]]
        st = n
        for x in shape[1:]:
            st //= x
            dims.append([st, x])
        return bass.AP(h, off // z, dims)

    def ps(self, name, shape, dt=F32):
        if self.banks is None:
            return self.nc.alloc_psum_tensor('ps_' + name, list(shape), dt)
        b = self.banks[self.nbank]
        self.nbank += 1
        assert self.nbank <= 8
        if dt == BF16:
            h = b.bitcast(BF16)
            return bass.AP(h, 0, [[1024, shape[0]], [1, shape[1]]])
        return bass.AP(b, 0, [[512, shape[0]], [1, shape[1]]])

    def _key(self, ap, bx):
        nm = ap.tensor.name
        if nm != 'arena':
            return nm
        i = bisect.bisect_right(self.allocs, (bx[2], 1 << 60)) - 1
        assert i >= 0 and self.allocs[i][0] <= bx[2] and bx[3] <= self.allocs[i][1], (bx, self.allocs[i] if i >= 0 else None)
        return 'a%d_%d' % (self.phase, i)

    def cc(self, kind, groups, in_ap, out_ap, barrier=True):
        o = Op('pool', len(self.ops['pool']), None, True)
        o.dsem = 'cc%d' % len(self.ccs)
        o.dval = 1
        o.fn = ('cc', kind, groups, in_ap, out_ap)
        self.dma_cnt[('pool', o.dsem)] = 1
        self._track(o, [in_ap], [out_ap])
        self.ops['pool'].append(o)
        if barrier:
            self.ccs.append(o)
        return o

    def _add_dep(self, op, dep):
        if dep is op:
            return
        if dep.is_dma:
            key = ('d', dep.eng, dep.dsem)
            val = dep.dval
        else:
            if dep.eng == op.eng and not op.is_dma:
                if not self.same:
                    return
                if op.eng == 'pe':
                    return
            key = ('e', dep.eng)
            val = dep.idx
        w = self.wm[op.eng]
        if w.get(key, -1) >= val:
            return
        w[key] = val
        op.deps.append(dep)
        dep.signal = True

    def _track(self, op, aps_r, aps_w):
        for ap in aps_r:
            bx = _box(ap)
            nm = self._key(ap, bx)
            lst = self.rec.setdefault(nm, [])
            isps = type(ap.tensor).__name__.startswith('PSum')
            for r in lst:
                if r[2] and _ovl(r[0], bx):
                    self._add_dep(op, r[1])
                elif isps and (not r[2]) and r[1].eng != op.eng:
                    self._add_dep(op, r[1])
        for ap in aps_w:
            bx = _box(ap)
            nm = self._key(ap, bx)
            lst = self.rec.setdefault(nm, [])
            for r in lst:
                if _ovl(r[0], bx):
                    self._add_dep(op, r[1])
        for ap in aps_r:
            bx = _box(ap)
            nm = self._key(ap, bx)
            lst = self.rec[nm]
            new = []
            for r in lst:
                if (not r[2]) and r[1].eng == op.eng and (not r[1].is_dma) and (not op.is_dma) and _cov(bx, r[0]):
                    continue
                new.append(r)
            new.append([bx, op, False])
            self.rec[nm] = new
        for ap in aps_w:
            bx = _box(ap)
            nm = self._key(ap, bx)
            lst = self.rec[nm]
            new = [r for r in lst if not _cov(bx, r[0])]
            new.append([bx, op, True])
            self.rec[nm] = new

    def op(self, eng, fn, reads=(), writes=()):
        o = Op(eng, len(self.ops[eng]), fn, False)
        self._track(o, list(reads), list(writes))
        self.ops[eng].append(o)
        return o

    def dma(self, eng, out, in_, is_output=False):
        o = Op(eng, len(self.ops[eng]), None, True)
        slot = self.dma_rr[eng] % N_DMA_SEMS
        self.dma_rr[eng] += 1
        o.dsem = slot
        prev = self.dma_last.get((eng, slot))
        o.dprev = prev
        cnt = self.dma_cnt.get((eng, slot), 0) + 1
        self.dma_cnt[(eng, slot)] = cnt
        o.dval = cnt
        self.dma_last[(eng, slot)] = o
        o.fn = (out, in_)
        if prev is not None:
            w = self.wm[eng]
            key = ('d', eng, slot)
            if w.get(key, -1) < prev.dval:
                w[key] = prev.dval
                o.deps.append(prev)
        self._track(o, [in_], [out])
        self.ops[eng].append(o)
        if is_output and (self.final_names is None or out.tensor.name in self.final_names):
            self.out_dmas.append(o)
        return o

    def emit(self):
        nc = self.nc
        engobj = {'pe': nc.tensor, 'act': nc.scalar, 'dve': nc.vector, 'pool': nc.gpsimd, 'sp': nc.sync}
        fin = Op('sp', len(self.ops['sp']), 'final', False)
        for d in self.out_dmas:
            fin.deps.append(d)
        self.ops['sp'].append(fin)
        sems = {}
        nsig = {}
        for e in self.ENGS:
            k = 0
            for o in self.ops[e]:
                if (not o.is_dma) and o.signal:
                    k += 1
                    o.sigcount = k
            nsig[e] = k
            sems[e] = [nc.alloc_semaphore(f'sem_{e}_{i}') for i in range((k + SEM_CHUNK - 1) // SEM_CHUNK)]
        dsems = {}
        for (e, slot) in self.dma_cnt:
            dsems[(e, slot)] = nc.alloc_semaphore(f'dsem_{e}_{slot}')
        self.stats = {e: (len(self.ops[e]), nsig[e]) for e in self.ENGS}

        def emit_engine(e, eng):
            for o in self.ops[e]:
                for d in o.deps:
                    if d.is_dma:
                        mult = 1 if isinstance(d.dsem, str) else 16
                        eng.wait_ge(dsems[(d.eng, d.dsem)], mult * d.dval)
                    else:
                        k = d.sigcount - 1
                        eng.wait_ge(sems[d.eng][k // SEM_CHUNK], k % SEM_CHUNK + 1)
                if o.fn == 'final' or o.fn is None:
                    continue
                if o.is_dma and isinstance(o.fn[0], str):
                    _, kind, groups, in_ap, out_ap = o.fn
                    eng.collective_compute(kind, ALU.bypass, groups, [in_ap], [out_ap]).then_inc(dsems[(e, o.dsem)])
                    continue
                if o.is_dma:
                    out, in_ = o.fn
                    eng.dma_start(out=out, in_=in_, allow_slow_non_contiguous=True).then_inc(dsems[(e, o.dsem)], 16)
                else:
                    ins = o.fn(eng)
                    if o.signal:
                        k = o.sigcount - 1
                        ins.then_inc(sems[e][k // SEM_CHUNK], 1)

        with nc.Block() as block:
            @block.tensor
            def _(eng):
                emit_engine('pe', eng)

            @block.scalar
            def _(eng):
                emit_engine('act', eng)

            @block.vector
            def _(eng):
                emit_engine('dve', eng)

            @block.gpsimd
            def _(eng):
                emit_engine('pool', eng)

            @block.sync
            def _(eng):
                emit_engine('sp', eng)


def V(t, off, *dims, p0=0, npart=128):
    if isinstance(t, bass.AP):
        row = t.ap[0][0]
        return bass.AP(t.tensor, t.offset + p0 * row + off, [[row, npart]] + [list(d) for d in dims])
    row = 1
    for s in list(t.shape)[1:]:
        row *= s
    return bass.AP(t, p0 * row + off, [[row, npart]] + [list(d) for d in dims])


def o_tt(S, eng, out, in0, in1, op):
    return S.op(eng, lambda e: e.tensor_tensor(out=out, in0=in0, in1=in1, op=op), [in0, in1], [out])


def o_ts(S, eng, out, in0, s1, s2, op0, op1=None):
    rd = [in0] + [x for x in (s1, s2) if isinstance(x, bass.AP)]
    if op1 is None:
        return S.op(eng, lambda e: e.tensor_scalar(out=out, in0=in0, scalar1=s1, scalar2=None, op0=op0), rd, [out])
    return S.op(eng, lambda e: e.tensor_scalar(out=out, in0=in0, scalar1=s1, scalar2=s2, op0=op0, op1=op1), rd, [out])


def o_stt(S, eng, out, in0, scalar, in1, op0, op1):
    rd = [in0, in1] + ([scalar] if isinstance(scalar, bass.AP) else [])
    return S.op(eng, lambda e: e.scalar_tensor_tensor(out=out, in0=in0, scalar=scalar, in1=in1, op0=op0, op1=op1), rd, [out])


def o_act(S, out, in_, func, bias=None, scale=1.0, accum=None):
    rd = [in_] + [x for x in (bias, scale) if isinstance(x, bass.AP)]
    wr = [out] + ([accum] if accum is not None else [])
    kw = {}
    if bias is not None:
        kw['bias'] = bias
    if accum is not None:
        kw['accum_out'] = accum
    return S.op('act', lambda e: e.activation(out=out, in_=in_, func=func, scale=scale, **kw), rd, wr)


def o_copy(S, eng, out, in_):
    if eng == 'act':
        return S.op('act', lambda e: e.copy(out=out, in_=in_), [in_], [out])
    return S.op(eng, lambda e: e.tensor_copy(out=out, in_=in_), [in_], [out])


def o_red(S, eng, out, in_, op=None):
    op = op or ALU.add
    return S.op(eng, lambda e: e.tensor_reduce(out=out, in_=in_, axis=AX.X, op=op), [in_], [out])


def o_recip(S, out, in_):
    return S.op('dve', lambda e: e.reciprocal(out=out, in_=in_), [in_], [out])


def o_memset(S, eng, out, val):
    return S.op(eng, lambda e: e.memset(out, val), [], [out])


def o_mm(S, out, lhsT, rhs, start, stop):
    return S.op('pe', lambda e: e.matmul(out, lhsT=lhsT, rhs=rhs, start=start, stop=stop), [lhsT, rhs], [out])


def o_tr(S, out, in_, ident):
    return S.op('pe', lambda e: e.transpose(out=out, in_=in_, identity=ident), [in_, ident], [out])


class Ctx:
    def __init__(self, S, D, npT=2):
        self.S = S
        self.identf = S.sb('identf', [128, 128], F32)
        self.ident = S.sb('identb', [128, 128], BF16)
        self.eps = S.sb('eps', [128, 1], F32)
        self.ones = S.sb('ones', [128, 128], BF16)
        self.junk = S.sb('junk', [128, 1024], F32)
        S.dma('sp', self.identf[:, :], D['ident'])
        o_copy(S, 'dve', self.ident[:, :], self.identf[:, :])
        o_memset(S, 'dve', self.eps[:, :], 1e-6)
        o_memset(S, 'dve', self.ones[:, :], 1.0)
        self.pT = [S.ps('pT%d' % i, [128, 1024], BF16) for i in range(npT)]
        self.nT = 0
        self.evac_rr = 0

    def next_pT(self):
        t = self.pT[self.nT % 2]
        self.nT += 1
        return t

    def evac_eng(self):
        self.evac_rr += 1
        return 'act' if self.evac_rr % 2 else 'dve'


def rstd_rows(S, C, ss, out, n, inv_d):
    o_act(S, out, ss, AF.Sqrt, bias=C.eps[:, 0:1], scale=inv_d)
    o_recip(S, out, out)


def norm_T(S, C, hsrc, xT_dst, Gp, Sh, scr):
    ss = scr['ss']
    rs = scr['rs']
    xs = scr['xs']
    o_act(S, C.junk[:, :], hsrc, AF.Square, scale=1.0 / 32.0, accum=ss[:, 0:1])
    rstd_rows(S, C, ss[:, 0:1], rs[:, 0:1], 1, 1.0)
    o_act(S, xs[:, :], hsrc, AF.Copy, scale=rs[:, 0:1])
    pT = C.next_pT()
    for k in range(8):
        o_tr(S, pT[:, k * 128:(k + 1) * 128], xs[:, k * 128:(k + 1) * 128], C.ident[:, :])
    tmp = scr['xtmp']
    gb = bass.AP(Gp.tensor, Gp.offset, [list(Gp.ap[0]), [1, 8], [0, 128]])
    sb_ = bass.AP(Sh.tensor, Sh.offset, [list(Sh.ap[0]), [1, 8], [0, 128]])
    pv = V(pT, 0, (128, 8), (1, 128))
    tv = V(tmp, 0, (128, 8), (1, 128))
    o_tt(S, 'dve', tv, pv, gb, ALU.mult)
    o_tt(S, 'pool', xT_dst, tv, sb_, ALU.add)


def load_mod(S, D, layer, scr_name, which, norm_ap):
    t = S.sb(scr_name, [128, 2, 2, 8], F32)
    raw = S.sb(scr_name + '_raw', [128, 2, 2, 8], F32)
    nrm = S.sb(scr_name + '_n', [128, 8], F32)
    modv = D['modv']
    S.dma('sp', nrm[:, :], norm_ap)
    base = which * 3 * 1024
    for s in range(2):
        S.dma('sp', raw[:, s, 1, :], modv[layer, s, base:base + 1024].rearrange('(k p) -> p k', p=128))
        S.dma('sp', raw[:, s, 0, :], modv[layer, s, base + 1024:base + 2048].rearrange('(k p) -> p k', p=128))
        o_stt(S, 'dve', t[:, s, 0, :], raw[:, s, 0, :], 1.0, nrm[:, :], ALU.add, ALU.mult)
        o_copy(S, 'dve', t[:, s, 1, :], raw[:, s, 1, :])
    return t


def load_gate_rep(S, D, layer, name, which):
    t = S.sb(name, [128, 2, 1024], F32)
    modv = D['modv']
    base = (which * 3 + 2) * 1024
    for s in range(2):
        S.dma('sp', t[:, s, :], modv[layer, s, base:base + 1024].partition_broadcast(128))
    return t


def build_M(S, D, ncol=768, nrow=3):
    C = Ctx(S, D)
    cin = S.sb('cin', [128, 8, nrow], F32)
    sc = S.sb('sc', [128, 8, nrow], BF16)
    S.dma('sp', cin[:, :, :], D['cin'])
    scf = S.sb('scf', [128, 8, nrow], F32)
    o_act(S, scf[:, :, :], cin[:, :, :], AF.Silu)
    o_copy(S, 'dve', sc[:, :, :], scf[:, :, :])
    bias = S.sb('mbias', [nrow, 4, ncol], F32)
    res = S.sb('mres', [nrow, 4, ncol], F32)
    for s in range(nrow):
        S.dma('sp', bias[s:s + 1, :, :], D['mod_b'].partition_broadcast(1))
    wt = [S.sb('mw%d' % i, [128, 8, ncol], BF16) for i in range(2)]
    pM = [S.ps('pM%d' % i, [128, 512], F32) for i in range(2)]
    n = 0
    for i in range(4):
        w = wt[i % 2]
        S.dma('pool', w[:, :, :], D['mod_w'][i].rearrange('(k p) m -> p k m', p=128))
        c0 = 0
        while c0 < ncol:
            wd = min(512, ncol - c0)
            p = pM[n % 2]
            n += 1
            for k in range(8):
                o_mm(S, p[0:nrow, 0:wd], sc[:, k, :], w[:, k, c0:c0 + wd], k == 0, k == 7)
            o_tt(S, 'dve', res[0:nrow, i, c0:c0 + wd], p[0:nrow, 0:wd], bias[0:nrow, i, c0:c0 + wd], ALU.add)
            c0 += wd
    S.dma('sp', D['modv'], res[0:nrow, :, :], is_output=True)


def norm_a(S, C, hsrc, xs, ss, rs):
    o_act(S, C.junk[:, :], hsrc, AF.Square, scale=1.0 / 32.0, accum=ss)
    rstd_rows(S, C, ss, rs, 1, 1.0)
    o_act(S, xs, hsrc, AF.Copy, scale=rs)


def norm_b(S, C, xs, xT_dst, Gp, Sh, tmp):
    pT = C.next_pT()
    for k in range(8):
        o_tr(S, pT[:, k * 128:(k + 1) * 128], xs[:, k * 128:(k + 1) * 128], C.ident[:, :])
    gb = bass.AP(Gp.tensor, Gp.offset, [list(Gp.ap[0]), [1, 8], [0, 128]])
    sb_ = bass.AP(Sh.tensor, Sh.offset, [list(Sh.ap[0]), [1, 8], [0, 128]])
    pv = V(pT, 0, (128, 8), (1, 128))
    tv = V(tmp, 0, (128, 8), (1, 128))
    o_tt(S, 'dve', tv, pv, gb, ALU.mult)
    o_tt(S, 'pool', xT_dst, tv, sb_, ALU.add)


def build_C(S, D, layer, ntiles=18, nctx=2):
    C = Ctx(S, D)
    md = load_mod(S, D, layer, 'md2', 1, D['norm2'])
    gate = load_gate_rep(S, D, layer, 'g2rep', 1)
    w13 = S.sb('w13', [128, 8, 2, 1408], BF16)
    w2 = S.sb('w2', [128, 11, 1024], BF16)
    ntok_all = ntiles * 128
    xTall = S.sb('xTall', [128, 8, ntok_all], BF16)
    ss = [S.sb('ss%d' % i, [128, 1], F32) for i in range(2)]
    rs = [S.sb('rs%d' % i, [128, 1], F32) for i in range(2)]
    xs = [S.sb('xs%d' % i, [128, 1024], BF16) for i in range(2)]
    xtmp = S.sb('xtmp', [128, 1024], F32)
    ht = [S.sb('ht%d' % i, [128, 1024], F32) for i in range(4)]
    hacc = [S.sb('hacc%d' % i, [128, 1024], F32) for i in range(2)]
    gT = S.sb('gT', [128, 11, 512], BF16)
    s1 = [S.sb('s1_%d' % i, [128, 512], F32) for i in range(2)]
    otmp = [S.sb('otmp%d' % i, [128, 512], F32) for i in range(2)]
    ho = [S.sb('ho%d' % i, [128, 1024], F32) for i in range(2)]
    pA = [S.ps('pA%d' % i, [128, 512], F32) for i in range(4)]
    pO = [S.ps('pO%d' % i, [128, 512], F32) for i in range(2)]
    hin = D['hin']
    hout = D['hout']
    nblk = (ntiles + 3) // 4
    na = 0
    no = 0

    def nA(t):
        S.dma('sp', ht[t % 4][:, :], hin[t * 128:(t + 1) * 128, :])
        norm_a(S, C, ht[t % 4][:, :], xs[t % 2][:, :], ss[t % 2][:, 0:1], rs[t % 2][:, 0:1])

    def nB(t):
        s = 1 if t < nctx else 0
        norm_b(S, C, xs[t % 2], V(xTall, t * 128, (ntok_all, 8), (1, 128)), md[:, s, 0, :], md[:, s, 1, :], xtmp)

    def blk_tiles(b):
        return list(range(b * 4, min(ntiles, b * 4 + 4)))

    for ps_ in range(2):
        for k in range(8):
            for u in range(2):
                S.dma('pool', w13[:, k, u, :], D['w13'][k * 128:(k + 1) * 128, u * 2816 + ps_ * 1408:u * 2816 + (ps_ + 1) * 1408])
        S.dma('pool', w2[:, :, :], D['w2'][ps_ * 1408:(ps_ + 1) * 1408, :].rearrange('(f p) m -> p f m', p=128))
        if ps_ == 0:
            for t in blk_tiles(0):
                nA(t)
                nB(t)
        for b in range(nblk):
            tl = blk_tiles(b)
            t0 = tl[0]
            nt = len(tl)
            ntok = nt * 128
            c0 = t0 * 128
            nxt = blk_tiles(b + 1) if (ps_ == 0 and b + 1 < nblk) else []
            for f in range(11):
                p1 = pA[na % 4]
                p3 = pA[(na + 1) % 4]
                na += 2
                for k in range(8):
                    o_mm(S, p1[:, 0:ntok], w13[:, k, 0, f * 128:(f + 1) * 128], xTall[:, k, c0:c0 + ntok], k == 0, k == 7)
                for k in range(8):
                    o_mm(S, p3[:, 0:ntok], w13[:, k, 1, f * 128:(f + 1) * 128], xTall[:, k, c0:c0 + ntok], k == 0, k == 7)
                if f % 2 == 1 and f // 2 < len(nxt):
                    nA(nxt[f // 2])
                if f % 2 == 0 and f >= 2 and f // 2 - 1 < len(nxt):
                    nB(nxt[f // 2 - 1])
                st = s1[f % 2]
                o_act(S, st[:, 0:ntok], p1[:, 0:ntok], AF.Silu)
                o_tt(S, 'dve', gT[:, f, 0:ntok], p3[:, 0:ntok], st[:, 0:ntok], ALU.mult)
            for j in range(nt):
                s = 1 if (t0 + j) < nctx else 0
                ha = hacc[no % 2]
                hob = ho[no % 2]
                src = hin if ps_ == 0 else hout
                S.dma('sp', ha[:, :], src[(t0 + j) * 128:(t0 + j + 1) * 128, :])
                for hf in range(2):
                    p = pO[(2 * no + hf) % 2]
                    ot = otmp[(2 * no + hf) % 2]
                    for f in range(11):
                        o_mm(S, p[:, :], gT[:, f, j * 128:(j + 1) * 128], w2[:, f, hf * 512:(hf + 1) * 512], f == 0, f == 10)
                    o_tt(S, 'dve', ot[:, :], p[:, :], gate[:, s, hf * 512:(hf + 1) * 512], ALU.mult)
                    o_tt(S, 'pool', hob[:, hf * 512:(hf + 1) * 512], ot[:, :], ha[:, hf * 512:(hf + 1) * 512], ALU.add)
                no += 1
                S.dma('sp', hout[(t0 + j) * 128:(t0 + j + 1) * 128, :], hob[:, :], is_output=True)


def proj_tm(S, C, pP, np_, xTt, W, ncols, Psb):
    c0 = 0
    while c0 < ncols:
        w = min(512, ncols - c0)
        p = pP[np_[0] % len(pP)]
        np_[0] += 1
        for k in range(8):
            o_mm(S, p[:, 0:w], xTt[:, k, :], W[:, k, c0:c0 + w], k == 0, k == 7)
        o_copy(S, C.evac_eng(), Psb[:, c0:c0 + w], p[:, 0:w])
        c0 += w


def head_norm(S, C, src, nh, d, gain_rep, dst, scr):
    (st, so) = src
    (dt_, do) = dst
    sq = scr['sq']
    ss = scr['ssh']
    rs = scr['rsh']
    sv = V(st, so, (d, nh), (1, d))
    qv = V(sq, 0, (d, nh), (1, d))
    o_tt(S, 'pool', qv, sv, sv, ALU.mult)
    o_red(S, 'dve', ss[:, 0:nh], qv)
    rstd_rows(S, C, ss[:, 0:nh], rs[:, 0:nh], nh, 1.0 / d)
    rb = V(rs, 0, (1, nh), (0, d))
    o_tt(S, 'dve', qv, sv, rb, ALU.mult)
    gv = V(gain_rep, 0, (d, nh), (1, d))
    dv_ = V(dt_, do, (d, nh), (1, d))
    o_tt(S, 'pool', dv_, qv, gv, ALU.mult)


def rope_tm(S, src, nh, d, cos, sin, dst, scr):
    (st, so) = src
    (dt_, do) = dst
    hd = d // 2
    x1 = V(st, so, (d, nh), (1, hd))
    x2 = V(st, so + hd, (d, nh), (1, hd))
    o1 = V(dt_, do, (d, nh), (1, hd))
    o2 = V(dt_, do + hd, (d, nh), (1, hd))
    cb = bass.AP(cos.tensor, cos.offset, [list(cos.ap[0]), [0, nh], [1, hd]])
    sb_ = bass.AP(sin.tensor, sin.offset, [list(sin.ap[0]), [0, nh], [1, hd]])
    ta = V(scr['ra'], 0, (hd, nh), (1, hd))
    tb = V(scr['rb'], 0, (hd, nh), (1, hd))
    tc = V(scr['rc'], 0, (hd, nh), (1, hd))
    td = V(scr['rd'], 0, (hd, nh), (1, hd))
    o_tt(S, 'dve', ta, x1, cb, ALU.mult)
    o_tt(S, 'pool', tb, x2, sb_, ALU.mult)
    o_tt(S, 'dve', o1, ta, tb, ALU.subtract)
    o_tt(S, 'pool', tc, x1, sb_, ALU.mult)
    o_tt(S, 'dve', td, x2, cb, ALU.mult)
    o_tt(S, 'pool', o2, tc, td, ALU.add)


def transposes_to(S, C, src_bf, ncolblk, dsts):
    i = 0
    while i < ncolblk:
        n = min(8, ncolblk - i)
        pT = C.next_pT()
        for j in range(n):
            o_tr(S, pT[:, j * 128:(j + 1) * 128], src_bf[:, (i + j) * 128:(i + j + 1) * 128], C.ident[:, :])
        for j in range(n):
            o_copy(S, C.evac_eng(), dsts[i + j], pT[:, j * 128:(j + 1) * 128])
        i += n


def build_A_gqa(S, D, layer, ntiles=18):
    C = Ctx(S, D)
    md = load_mod(S, D, layer, 'md1', 0, D['norm1'])
    W = S.sb('wqkv', [128, 8, 1536], BF16)
    for k in range(8):
        S.dma('pool', W[:, k, :], D['wqkv'][k * 128:(k + 1) * 128, :])
    gq = S.sb('gq', [128, 1280], F32)
    S.dma('sp', gq[:, :], D['gq'])
    ct = S.sb('cost', [128, ntiles - 2, 64], F32)
    sn = S.sb('sint', [128, ntiles - 2, 64], F32)
    S.dma('sp', ct[:, :, :], D['cos'])
    S.dma('sp', sn[:, :, :], D['sin'])
    scr = dict(ss=S.sb('ss', [128, 1], F32), rs=S.sb('rs', [128, 1], F32), xs=S.sb('xs', [128, 1024], BF16),
               xtmp=S.sb('xtmp', [128, 1024], F32), sq=S.sb('sq', [128, 1280], F32), ssh=S.sb('ssh', [128, 16], F32),
               rsh=S.sb('rsh', [128, 16], F32), ra=S.sb('ra', [128, 640], F32), rb=S.sb('rb', [128, 640], F32),
               rc=S.sb('rc', [128, 640], F32), rd=S.sb('rd', [128, 640], F32))
    ht = [S.sb('ht%d' % i, [128, 1024], F32) for i in range(2)]
    xTt = S.sb('xTt', [128, 8, 128], BF16)
    Psb = S.sb('Psb', [128, 1536], F32)
    qn = S.sb('qn', [128, 1280], F32)
    qb = S.sb('qb', [128, 1280], BF16)
    QTs = S.sb('QTs', [128, 8, ntiles * 128], BF16)
    KTs = S.sb('KTs', [128, 2, ntiles * 128], BF16)
    Vs = S.sb('Vs', [128, ntiles, 256], BF16)
    pP = [S.ps('pP%d' % i, [128, 512], F32) for i in range(3)]
    np_ = [0]
    Psb2 = [Psb, S.sb('Psb_b', [128, 1536], F32)]

    def front(t):
        s = 1 if t < 2 else 0
        h = ht[t % 2]
        S.dma('sp', h[:, :], D['hin'][t * 128:(t + 1) * 128, :])
        norm_T(S, C, h[:, :], V(xTt, 0, (128, 8), (1, 128)), md[:, s, 0, :], md[:, s, 1, :], scr)
        proj_tm(S, C, pP, np_, xTt, W, 1536, Psb2[t % 2])

    def back(t):
        Psb = Psb2[t % 2]
        head_norm(S, C, (Psb, 0), 10, 128, gq, (qn, 0), scr)
        if t >= 2:
            rope_tm(S, (qn, 0), 10, 128, ct[:, t - 2, :], sn[:, t - 2, :], (qb, 0), scr)
        else:
            o_copy(S, 'dve', qb[:, :], qn[:, :])
        dsts = [QTs[:, hh, t * 128:(t + 1) * 128] for hh in range(8)] + [KTs[:, hh, t * 128:(t + 1) * 128] for hh in range(2)]
        transposes_to(S, C, qb, 10, dsts)
        o_copy(S, 'act', Vs[:, t, :], Psb[:, 1280:1536])

    front(0)
    for t in range(ntiles):
        if t + 1 < ntiles:
            front(t + 1)
        back(t)
    S.dma('sp', D['QT'].rearrange('h p n -> p h n'), QTs[:, :, :], is_output=True)
    S.dma('sp', D['KT'].rearrange('h p n -> p h n'), KTs[:, :, :], is_output=True)
    S.dma('sp', D['Vt'].rearrange('(t p) m -> p t m', p=128), Vs[:, :, :], is_output=True)


class AttnBufs:
    def __init__(self, S, nS=2):
        self.pS = [S.ps('pS%d' % i, [128, 512], F32) for i in range(nS)]
        self.pO = [S.ps('pO%d' % i, [128, 512], F32) for i in range(2)]
        self.pM = [S.ps('pSm%d' % i, [128, 512], F32) for i in range(2)]
        self.pt = [S.sb('pt%d' % i, [128, 512], BF16) for i in range(4)]
        self.rc = [S.sb('rcp%d' % i, [128, 512], F32) for i in range(2)]
        self.acc = [S.sb('sacc%d' % i, [128, 512], F32) for i in range(2)]
        self.hi = S.sb('shi', [128, 512], BF16)
        self.lo = S.sb('slo', [128, 512], BF16)
        self.ns = 0
        self.no = 0


def attn_head_block(S, C, A, nq, chunks, out_ap, dv, scale, orow=None):
    pO = A.pO[A.no % 2]
    pM = A.pM[A.no % 2]
    rc = A.rc[A.no % 2]
    A.no += 1
    n = len(chunks)
    nS = len(A.pS)
    PD = nS - 1
    base = A.ns
    A.ns += n

    def scores(ci):
        pieces, v_ap, nk = chunks[ci]
        pS = A.pS[(base + ci) % nS]
        for pi, (l, r) in enumerate(pieces):
            o_mm(S, pS[0:nk, 0:nq], l, r, pi == 0, pi == len(pieces) - 1)

    for ci in range(min(PD, n)):
        scores(ci)
    acc = A.acc[(A.no - 1) % 2]
    for ci, (pieces, v_ap, nk) in enumerate(chunks):
        if ci + PD < n:
            scores(ci + PD)
        pS = A.pS[(base + ci) % nS]
        pt = A.pt[(base + ci) % len(A.pt)]
        o_act(S, pt[0:nk, 0:nq], pS[0:nk, 0:nq], AF.Exp, scale=scale)
        o_mm(S, pO[0:dv, 0:nq], v_ap, pt[0:nk, 0:nq], ci == 0, ci == n - 1)
        if ci % 2 == 0:
            o_mm(S, pM[0:dv, 0:nq], C.ones[0:nk, 0:dv], pt[0:nk, 0:nq], ci == 0, False)
        elif ci == 1:
            o_copy(S, 'dve', acc[0:nk, 0:nq], pt[0:nk, 0:nq])
        else:
            o_tt(S, 'dve', acc[0:nk, 0:nq], acc[0:nk, 0:nq], pt[0:nk, 0:nq], ALU.add)
    nk0 = chunks[0][2]
    hi = A.hi
    lo = A.lo
    o_copy(S, 'dve', hi[0:nk0, 0:nq], acc[0:nk0, 0:nq])
    o_tt(S, 'dve', lo[0:nk0, 0:nq], acc[0:nk0, 0:nq], hi[0:nk0, 0:nq], ALU.subtract)
    o_mm(S, pM[0:dv, 0:nq], C.ones[0:nk0, 0:dv], hi[0:nk0, 0:nq], False, False)
    o_mm(S, pM[0:dv, 0:nq], C.ones[0:nk0, 0:dv], lo[0:nk0, 0:nq], False, True)
    r0, r1 = orow if orow is not None else (0, dv)
    o_recip(S, rc[r0:r1, 0:nq], pM[r0:r1, 0:nq])
    o_tt(S, 'dve', out_ap, pO[r0:r1, 0:nq], rc[r0:r1, 0:nq], ALU.mult)


def out_proj_residual(S, C, A, D, OT, nk, wo, gate, ntiles, hin, hout, kslice=None):
    ht = [S.sb('oht%d' % i, [128, 1024], F32) for i in range(2)]
    ho = [S.sb('oho%d' % i, [128, 1024], F32) for i in range(2)]
    ot = [S.sb('oot%d' % i, [128, 512], F32) for i in range(2)]
    n = 0
    for t in range(ntiles):
        s = 1 if t < 2 else 0
        h = ht[t % 2]
        hb = ho[t % 2]
        S.dma('sp', h[:, :], hin[t * 128:(t + 1) * 128, :])
        for hf in range(2):
            p = A.pS[n % 2]
            o_ = ot[n % 2]
            n += 1
            for k in range(nk):
                lhs = OT[:, k, t * 128:(t + 1) * 128] if kslice is None else kslice(k, t)
                o_mm(S, p[:, :], lhs, wo[:, k, hf * 512:(hf + 1) * 512], k == 0, k == nk - 1)
            o_tt(S, 'dve', o_[:, :], p[:, :], gate[:, s, hf * 512:(hf + 1) * 512], ALU.mult)
            o_tt(S, 'pool', hb[:, hf * 512:(hf + 1) * 512], o_[:, :], h[:, hf * 512:(hf + 1) * 512], ALU.add)
        S.dma('sp', hout[t * 128:(t + 1) * 128, :], hb[:, :], is_output=True)


def build_B_gqa(S, D, layer, ntiles=18, nkeys=8448):
    C = Ctx(S, D)
    A = AttnBufs(S)
    gate = load_gate_rep(S, D, layer, 'g1rep', 0)
    ntok = ntiles * 128
    nch = nkeys // 128
    QT = S.sb('QT', [128, 8, ntok], BF16)
    KT = S.sb('KT', [128, 2, nkeys], BF16)
    Vf = S.sb('Vf', [128, nch, 256], BF16)
    OT = S.sb('OT', [128, 8, ntok], BF16)
    wo = S.sb('wo', [128, 8, 1024], BF16)
    S.dma('sp', QT[:, :, :], D['QT'].rearrange('h p n -> p h n'))
    S.dma('sp', KT[:, :, :], D['KTf'].rearrange('h p n -> p h n'))
    S.dma('sp', Vf[:, :, :], D['Vf'].rearrange('(c p) m -> p c m', p=128))
    S.dma('pool', wo[:, :, :], D['wo'].rearrange('(k p) m -> p k m', p=128))
    scale = 128.0 ** -0.5
    for h in range(8):
        kv = h // 4
        chunks = [([(KT[:, kv, c * 128:(c + 1) * 128], QT[:, h, 0:256])], Vf[:, c, kv * 128:(kv + 1) * 128], 128) for c in range(2)]
        attn_head_block(S, C, A, 256, chunks, OT[:, h, 0:256], 128, scale)
        nqb = (ntok - 256) // 512
        for qb in range(nqb):
            q0 = 256 + qb * 512
            chunks = [([(KT[:, kv, c * 128:(c + 1) * 128], QT[:, h, q0:q0 + 512])], Vf[:, c, kv * 128:(kv + 1) * 128], 128) for c in range(nch)]
            attn_head_block(S, C, A, 512, chunks, OT[:, h, q0:q0 + 512], 128, scale)
    out_proj_residual(S, C, A, D, OT, 8, wo, gate, ntiles, D['hin'], D['hmid'])


def build_A_na(S, D, layer, ntiles=18):
    C = Ctx(S, D)
    md = load_mod(S, D, layer, 'md1', 0, D['norm1'])
    W = S.sb('wqkv', [128, 8, 3072], BF16)
    for k in range(8):
        S.dma('pool', W[:, k, :], D['wqkv'][k * 128:(k + 1) * 128, :])
    gq = S.sb('gq', [128, 2048], F32)
    S.dma('sp', gq[:, :], D['gq'])
    o_ts(S, 'dve', gq[:, 0:1024], gq[:, 0:1024], 0.125, None, ALU.mult)
    scr = dict(ss=S.sb('ss', [128, 1], F32), rs=S.sb('rs', [128, 1], F32), xs=S.sb('xs', [128, 1024], BF16),
               xtmp=S.sb('xtmp', [128, 1024], F32), sq=S.sb('sq', [128, 2048], F32), ssh=S.sb('ssh', [128, 32], F32),
               rsh=S.sb('rsh', [128, 32], F32))
    ht = [S.sb('ht%d' % i, [128, 1024], F32) for i in range(2)]
    xTt = S.sb('xTt', [128, 8, 128], BF16)
    Psb = S.sb('Psb', [128, 3072], F32)
    qn = S.sb('qn', [128, 2048], F32)
    qb = S.sb('qb', [128, 2048], BF16)
    QTs = S.sb('QTs', [128, 8, ntiles * 128], BF16)
    KTs = S.sb('KTs', [128, 8, ntiles * 128], BF16)
    Vs = S.sb('Vs', [128, 2, 1024], BF16)
    pP = [S.ps('pP%d' % i, [128, 512], F32) for i in range(3)]
    np_ = [0]
    Psb2 = [Psb, S.sb('Psb_b', [128, 3072], F32)]

    def front(t):
        s = 1 if t < 2 else 0
        h = ht[t % 2]
        S.dma('sp', h[:, :], D['hin'][t * 128:(t + 1) * 128, :])
        norm_T(S, C, h[:, :], V(xTt, 0, (128, 8), (1, 128)), md[:, s, 0, :], md[:, s, 1, :], scr)
        proj_tm(S, C, pP, np_, xTt, W, 3072, Psb2[t % 2])

    def back(t):
        Psb = Psb2[t % 2]
        head_norm(S, C, (Psb, 0), 32, 64, gq, (qn, 0), scr)
        o_copy(S, 'dve', qb[:, :], qn[:, :])
        dsts = [QTs[:, hh, t * 128:(t + 1) * 128] for hh in range(8)] + [KTs[:, hh, t * 128:(t + 1) * 128] for hh in range(8)]
        transposes_to(S, C, qb, 16, dsts)
        o_copy(S, 'act', Vs[:, t % 2, :], Psb[:, 2048:3072])
        S.dma('sp', D['Vt'][t * 128:(t + 1) * 128, :], Vs[:, t % 2, :], is_output=True)
        if 'Vh' in D:
            if t in (2, 3):
                S.dma('sp', D['Vh'][(t - 2) * 128:(t - 1) * 128, :], Vs[:, t % 2, :])
            if t in (ntiles - 2, ntiles - 1):
                S.dma('sp', D['Vh'][256 + (t - ntiles + 2) * 128:256 + (t - ntiles + 3) * 128, :], Vs[:, t % 2, :])

    front(0)
    for t in range(ntiles):
        if t + 1 < ntiles:
            front(t + 1)
        back(t)
    S.dma('sp', D['QT'].rearrange('h p n -> p h n'), QTs[:, :, :], is_output=True)
    S.dma('sp', D['KT'].rearrange('h p n -> p h n'), KTs[:, :, :], is_output=True)
    if 'KTh' in D:
        S.dma('sp', D['KTh'][:, :, 0:256].rearrange('h p n -> p h n'), KTs[:, :, 256:512])
        S.dma('sp', D['KTh'][:, :, 256:512].rearrange('h p n -> p h n'), KTs[:, :, ntiles * 128 - 256:ntiles * 128])


def build_B_na(S, D, layer, ntiles=18):
    C = Ctx(S, D)
    A = AttnBufs(S)
    gate = load_gate_rep(S, D, layer, 'g1rep', 0)
    ntok = ntiles * 128
    nqb = (ntok - 256) // 512
    QT = S.sb('QT', [128, 8, ntok], BF16)
    KTp = [S.sb('KTp%d' % i, [128, 2816], BF16) for i in range(2)]
    Vp = [S.sb('Vp%d' % i, [128, 22, 128], BF16) for i in range(2)]
    wo = S.sb('wo', [128, 8, 1024], BF16)
    TT = [S.sb('TT%d' % i, [128, 2, 1408], BF16) for i in range(2)]
    mK = S.sb('mK', [2, 128], BF16)
    mQs = [S.sb('mQ%d' % i, [2, 8 * 512], BF16) for i in range(2)]
    S.dma('sp', QT[:, :, :], D['QT'].rearrange('h p n -> p h n'))
    S.dma('pool', wo[:, :, :], D['wo'].rearrange('(k p) m -> p k m', p=128))
    S.dma('sp', mK[:, :], D['maskK'])
    nm = 0
    for pair in range(8):
        tt = TT[pair % 2]
        KT = KTp[pair % 2]
        Vf = Vp[pair % 2]
        S.dma('sp', tt[:, :, :], D['TT'][2 * pair:2 * pair + 2].rearrange('h p n -> p h n'))
        S.dma('sp', KT[:, 0:256], D['KTc'][pair])
        S.dma('sp', KT[:, 256:2816], D['KTl'][pair])
        S.dma('sp', Vf[:, 0:2, :], D['Vc'][:, pair * 128:(pair + 1) * 128].rearrange('(c p) m -> p c m', p=128))
        S.dma('sp', Vf[:, 2:22, :], D['Vl'][:, pair * 128:(pair + 1) * 128].rearrange('(c p) m -> p c m', p=128))
        for hh in range(2):
            pb = hh * 64
            chunks = [([(KT[pb:pb + 64, c * 128:(c + 1) * 128], QT[pb:pb + 64, pair, 0:256])],
                       Vf[:, c, :], 128) for c in range(2)]
            attn_head_block(S, C, A, 256, chunks, QT[pb:pb + 64, pair, 0:256], 128, 1.0, orow=(pb, pb + 64))
            for qb in range(nqb):
                q0 = 256 + qb * 512
                mQ = mQs[nm % 2]
                nm += 1
                S.dma('sp', mQ[:, :], D['maskQ'][:, qb * 4096:(qb + 1) * 4096])
                chunks = []
                for c in range(2):
                    chunks.append(([(KT[pb:pb + 64, c * 128:(c + 1) * 128], QT[pb:pb + 64, pair, q0:q0 + 512])],
                                   Vf[:, c, :], 128))
                for c in range(8):
                    m = 4 * qb + c
                    i0 = 14 - 2 * c
                    pieces = [(KT[pb:pb + 64, 256 + m * 128:256 + (m + 1) * 128], QT[pb:pb + 64, pair, q0:q0 + 512]),
                              (C.ident[:, :], tt[:, hh, i0 * 64:i0 * 64 + 512]),
                              (mK[0:2, :], mQ[0:2, c * 512:(c + 1) * 512])]
                    chunks.append((pieces, Vf[:, 2 + m, :], 128))
                attn_head_block(S, C, A, 512, chunks, QT[pb:pb + 64, pair, q0:q0 + 512], 128, 1.0, orow=(pb, pb + 64))
    out_proj_residual(S, C, A, D, QT, 8, wo, gate, ntiles, D['hin'], D['hmid'])


def proj_tm2(S, C, pP, np_, lhs_list, Wfn, ncols, Psb, pcol0=0):
    c0 = 0
    nk = len(lhs_list)
    while c0 < ncols:
        w = min(512, ncols - c0)
        p = pP[np_[0] % len(pP)]
        np_[0] += 1
        for k in range(nk):
            o_mm(S, p[:, 0:w], lhs_list[k], Wfn(k, c0, w), k == 0, k == nk - 1)
        o_copy(S, C.evac_eng(), Psb[:, pcol0 + c0:pcol0 + c0 + w], p[:, 0:w])
        c0 += w


def rope_gen(S, src, dst, nh, hstride_s, hstride_d, hd, cos, sin, scr):
    (st, so) = src
    (dt_, do) = dst
    x1 = V(st, so, (hstride_s, nh), (1, hd))
    x2 = V(st, so + hd, (hstride_s, nh), (1, hd))
    o1 = V(dt_, do, (hstride_d, nh), (1, hd))
    o2 = V(dt_, do + hd, (hstride_d, nh), (1, hd))
    cb = bass.AP(cos.tensor, cos.offset, [list(cos.ap[0]), [0, nh], [1, hd]])
    sb_ = bass.AP(sin.tensor, sin.offset, [list(sin.ap[0]), [0, nh], [1, hd]])
    ta = V(scr['ra'], 0, (hd, nh), (1, hd))
    tb = V(scr['rb'], 0, (hd, nh), (1, hd))
    tc = V(scr['rc'], 0, (hd, nh), (1, hd))
    td = V(scr['rd'], 0, (hd, nh), (1, hd))
    o_tt(S, 'dve', ta, x1, cb, ALU.mult)
    o_tt(S, 'pool', tb, x2, sb_, ALU.mult)
    o_tt(S, 'dve', o1, ta, tb, ALU.subtract)
    o_tt(S, 'pool', tc, x1, sb_, ALU.mult)
    o_tt(S, 'dve', td, x2, cb, ALU.mult)
    o_tt(S, 'pool', o2, tc, td, ALU.add)


def transposes_gen(S, C, items):
    i = 0
    n_it = len(items)
    while i < n_it:
        n = min(8, n_it - i)
        pT = C.next_pT()
        for j in range(n):
            src, dst = items[i + j]
            w = src.shape[-1]
            o_tr(S, pT[0:w, j * 128:(j + 1) * 128], src, C.ident[:, :])
        for j in range(n):
            src, dst = items[i + j]
            w = src.shape[-1]
            o_copy(S, C.evac_eng(), dst, pT[0:w, j * 128:(j + 1) * 128])
        i += n


def build_A_mla(S, D, layer, ntiles=18):
    C = Ctx(S, D)
    md = load_mod(S, D, layer, 'md1', 0, D['norm1'])
    Wd = S.sb('wdown', [128, 8, 704], BF16)
    Wq = S.sb('wuq', [128, 3, 1536], BF16)
    Wkv = S.sb('wukv', [128, 2, 2048], BF16)
    S.dma('pool', Wd[:, :, :], D['wdown'].rearrange('(k p) m -> p k m', p=128))
    S.dma('pool', Wq[:, :, :], D['wuq'].rearrange('(k p) m -> p k m', p=128))
    S.dma('pool', Wkv[:, :, :], D['wukv'].rearrange('(k p) m -> p k m', p=128))
    lg = S.sb('lg', [128, 5], F32)
    S.dma('sp', lg[:, :], D['lg'])
    gq = S.sb('gq', [128, 1536], F32)
    gk = S.sb('gk', [128, 192], F32)
    S.dma('sp', gq[:, :], D['gq'])
    S.dma('sp', gk[:, :], D['gk'])
    o_ts(S, 'dve', gq[:, :], gq[:, :], 192.0 ** -0.5, None, ALU.mult)
    ct = S.sb('cost', [128, ntiles - 2, 32], F32)
    sn = S.sb('sint', [128, ntiles - 2, 32], F32)
    S.dma('sp', ct[:, :, :], D['cos'])
    S.dma('sp', sn[:, :, :], D['sin'])
    scr = dict(ss=S.sb('ss', [128, 1], F32), rs=S.sb('rs', [128, 1], F32), xs=S.sb('xs', [128, 1024], BF16),
               xtmp=S.sb('xtmp', [128, 1024], F32), sq=S.sb('sq', [128, 1536], F32), ssh=S.sb('ssh', [128, 16], F32),
               rsh=S.sb('rsh', [128, 16], F32), ra=S.sb('ra', [128, 256], F32), rb=S.sb('rb', [128, 256], F32),
               rc=S.sb('rc', [128, 256], F32), rd=S.sb('rd', [128, 256], F32))
    ht = [S.sb('ht%d' % i, [128, 1024], F32) for i in range(2)]
    xTt = S.sb('xTt', [128, 8, 128], BF16)
    Psb = S.sb('Psb', [128, 704], F32)
    ss2 = S.sb('ss2', [128, 2], F32)
    rs2 = S.sb('rs2', [128, 2], F32)
    cs = S.sb('cs', [128, 640], BF16)
    cT = S.sb('cT', [128, 5, 128], BF16)
    Qsb = S.sb('Qsb', [128, 1536], F32)
    KVsb = S.sb('KVsb', [128, 2048], F32)
    qn = S.sb('qn', [128, 1536], F32)
    qb = S.sb('qb', [128, 1536], BF16)
    ksq = S.sb('ksq', [128, 1024], F32)
    ssk = S.sb('ssk', [128, 8], F32)
    ssr = S.sb('ssr', [128, 1], F32)
    rsk = S.sb('rsk', [128, 8], F32)
    kr = S.sb('kr', [128, 8, 64], F32)
    kb = S.sb('kb', [128, 8, 192], BF16)
    vb = [S.sb('vb%d' % i, [128, 1024], BF16) for i in range(2)]
    stg = [dict(qn=S.sb('sqn%d' % i, [128, 8, 128], BF16), qr=S.sb('sqr%d' % i, [128, 8, 128], BF16),
                kn=S.sb('skn%d' % i, [128, 8, 128], BF16), kr=S.sb('skr%d' % i, [128, 8, 128], BF16)) for i in range(2)]
    pP = [S.ps('pP%d' % i, [128, 512], F32) for i in range(3)]
    np_ = [0]
    for t in range(ntiles):
        s = 1 if t < 2 else 0
        h = ht[t % 2]
        S.dma('sp', h[:, :], D['hin'][t * 128:(t + 1) * 128, :])
        norm_T(S, C, h[:, :], V(xTt, 0, (128, 8), (1, 128)), md[:, s, 0, :], md[:, s, 1, :], scr)
        proj_tm2(S, C, pP, np_, [xTt[:, k, :] for k in range(8)], lambda k, c0, w: Wd[:, k, c0:c0 + w], 704, Psb)
        o_act(S, C.junk[:, 0:384], Psb[:, 0:384], AF.Square, scale=384.0 ** -0.5, accum=ss2[:, 0:1])
        o_act(S, C.junk[:, 0:256], Psb[:, 384:640], AF.Square, scale=1.0 / 16.0, accum=ss2[:, 1:2])
        rstd_rows(S, C, ss2[:, 0:2], rs2[:, 0:2], 2, 1.0)
        o_act(S, cs[:, 0:384], Psb[:, 0:384], AF.Copy, scale=rs2[:, 0:1])
        o_act(S, cs[:, 384:640], Psb[:, 384:640], AF.Copy, scale=rs2[:, 1:2])
        pT = C.next_pT()
        for k in range(5):
            o_tr(S, pT[:, k * 128:(k + 1) * 128], cs[:, k * 128:(k + 1) * 128], C.ident[:, :])
        o_tt(S, 'dve', V(cT, 0, (128, 5), (1, 128)), V(pT, 0, (128, 5), (1, 128)), V(lg, 0, (1, 5), (0, 128)), ALU.mult)
        proj_tm2(S, C, pP, np_, [cT[:, k, :] for k in range(3)], lambda k, c0, w: Wq[:, k, c0:c0 + w], 1536, Qsb)
        proj_tm2(S, C, pP, np_, [cT[:, 3 + k, :] for k in range(2)], lambda k, c0, w: Wkv[:, k, c0:c0 + w], 2048, KVsb)
        head_norm(S, C, (Qsb, 0), 8, 192, gq, (qn, 0), scr)
        o_copy(S, 'act', qb[:, :], qn[:, :])
        if t >= 2:
            rope_gen(S, (qn, 128), (qb, 128), 8, 192, 192, 32, ct[:, t - 2, :], sn[:, t - 2, :], scr)
        knope = V(KVsb, 0, (256, 8), (1, 128))
        ksqv = V(ksq, 0, (128, 8), (1, 128))
        o_tt(S, 'pool', ksqv, knope, knope, ALU.mult)
        o_red(S, 'dve', ssk[:, 0:8], ksqv)
        o_act(S, C.junk[:, 0:64], Psb[:, 640:704], AF.Square, accum=ssr[:, 0:1])
        o_ts(S, 'dve', ssk[:, 0:8], ssk[:, 0:8], ssr[:, 0:1], None, ALU.add)
        rstd_rows(S, C, ssk[:, 0:8], rsk[:, 0:8], 8, 1.0 / 192.0)
        o_tt(S, 'dve', ksqv, knope, V(rsk, 0, (1, 8), (0, 128)), ALU.mult)
        o_tt(S, 'pool', V(kb, 0, (192, 8), (1, 128)), ksqv, V(gk, 0, (0, 8), (1, 128)), ALU.mult)
        krv = V(kr, 0, (64, 8), (1, 64))
        o_tt(S, 'dve', krv, V(Psb, 640, (0, 8), (1, 64)), V(rsk, 0, (1, 8), (0, 64)), ALU.mult)
        o_tt(S, 'pool', krv, krv, V(gk, 128, (0, 8), (1, 64)), ALU.mult)
        if t >= 2:
            rope_gen(S, (kr, 0), (kb, 128), 8, 64, 192, 32, ct[:, t - 2, :], sn[:, t - 2, :], scr)
        else:
            o_copy(S, 'dve', V(kb, 128, (192, 8), (1, 64)), krv)
        vbt = vb[t % 2]
        o_copy(S, 'act', V(vbt, 0, (128, 8), (1, 128)), V(KVsb, 128, (256, 8), (1, 128)))
        if 'Vhm' in D:
            S.dma('sp', D['Vhm'][:, t * 128:(t + 1) * 128, :].rearrange('h p m -> p h m'), V(vbt, 0, (128, 8), (1, 128)))
        else:
            S.dma('sp', D['Vt'][t * 128:(t + 1) * 128, :], vbt[:, :], is_output=True)
        sg = stg[t % 2]
        items = []
        for hh in range(8):
            items.append((qb[:, hh * 192:hh * 192 + 128], sg['qn'][:, hh, :]))
        for hh in range(8):
            items.append((qb[:, hh * 192 + 128:hh * 192 + 192], sg['qr'][0:64, hh, :]))
        for hh in range(8):
            items.append((kb[:, hh, 0:128], sg['kn'][:, hh, :]))
        for hh in range(8):
            items.append((kb[:, hh, 128:192], sg['kr'][0:64, hh, :]))
        transposes_gen(S, C, items)
        S.dma('sp', D['QTn'][:, :, t * 128:(t + 1) * 128].rearrange('h p n -> p h n'), sg['qn'][:, :, :], is_output=True)
        S.dma('sp', D['QTr'][:, :, t * 128:(t + 1) * 128].rearrange('h p n -> p h n'), sg['qr'][0:64, :, :], is_output=True)
        S.dma('sp', D['KTn'][:, :, t * 128:(t + 1) * 128].rearrange('h p n -> p h n'), sg['kn'][:, :, :], is_output=True)
        S.dma('sp', D['KTr'][:, :, t * 128:(t + 1) * 128].rearrange('h p n -> p h n'), sg['kr'][0:64, :, :], is_output=True)


def build_B_mla(S, D, layer, ntiles=18, nkeys=8448):
    C = Ctx(S, D)
    A = AttnBufs(S)
    gate = load_gate_rep(S, D, layer, 'g1rep', 0)
    ntok = ntiles * 128
    nch = nkeys // 128
    nqb = (ntok - 256) // 512
    QTn = S.sb('QTn', [128, 8, ntok], BF16)
    QTr = S.sb('QTr', [128, 8, ntok], BF16)
    KTn = S.sb('KTn', [128, nkeys], BF16)
    KTr = S.sb('KTr', [128, nkeys], BF16)
    Vh = S.sb('Vh', [128, nch, 128], BF16)
    wo = S.sb('wo', [128, 8, 1024], BF16)
    S.dma('sp', QTn[:, :, :], D['QTn'].rearrange('h p n -> p h n'))
    S.dma('sp', QTr[0:64, :, :], D['QTr'].rearrange('h p n -> p h n'))
    S.dma('pool', wo[:, :, :], D['wo'].rearrange('(k p) m -> p k m', p=128))
    for h in range(8):
        S.dma('sp', KTn[:, :], D['KTnf'][h])
        S.dma('sp', KTr[0:64, :], D['KTrf'][h])
        S.dma('sp', Vh[:, :, :], D['Vf'][:, h * 128:(h + 1) * 128].rearrange('(c p) m -> p c m', p=128))
        chunks = [([(KTn[:, c * 128:(c + 1) * 128], QTn[:, h, 0:256]), (KTr[0:64, c * 128:(c + 1) * 128], QTr[0:64, h, 0:256])],
                   Vh[:, c, :], 128) for c in range(2)]
        attn_head_block(S, C, A, 256, chunks, QTn[:, h, 0:256], 128, 1.0)
        for qb in range(nqb):
            q0 = 256 + qb * 512
            chunks = [([(KTn[:, c * 128:(c + 1) * 128], QTn[:, h, q0:q0 + 512]), (KTr[0:64, c * 128:(c + 1) * 128], QTr[0:64, h, q0:q0 + 512])],
                       Vh[:, c, :], 128) for c in range(nch)]
            attn_head_block(S, C, A, 512, chunks, QTn[:, h, q0:q0 + 512], 128, 1.0)
    out_proj_residual(S, C, A, D, QTn, 8, wo, gate, ntiles, D['hin'], D['hmid'])


def build_A_ret(S, D, layer, ntiles=18):
    C = Ctx(S, D)
    md = load_mod(S, D, layer, 'md1', 0, D['norm1'])
    W = S.sb('wqkvg', [128, 8, 6144], BF16)
    for k in range(8):
        S.dma('pool', W[:, k, :], D['wqkvg'][k * 128:(k + 1) * 128, :])
    ld = S.sb('ld', [128, 2, 4], F32)
    S.dma('sp', ld[:, :, :], D['ldrep'])
    E = S.sb('E', [128, 4], F32)
    Ec = S.sb('Ec', [128, 2, 2], F32)
    S.dma('sp', E[:, :], D['E'])
    S.dma('sp', Ec[:, :, :], D['Ec'])
    dec = S.sb('dec', [128, 4, 4], F32)
    decc = S.sb('decc', [128, 2, 2, 4], F32)
    for kind in range(4):
        o_ts(S, 'dve', dec[:, kind, :], ld[:, kind % 2, :], E[:, kind:kind + 1], None, ALU.mult)
    for tt_ in range(2):
        for dr in range(2):
            o_ts(S, 'dve', decc[:, tt_, dr, :], ld[:, dr, :], Ec[:, tt_, dr:dr + 1], None, ALU.mult)
    o_act(S, dec[:, :, :], dec[:, :, :], AF.Exp)
    o_act(S, decc[:, :, :, :], decc[:, :, :, :], AF.Exp)
    o_ts(S, 'dve', dec[:, 2:4, :], dec[:, 2:4, :], 1.0 / 16.0, None, ALU.mult)
    o_ts(S, 'dve', decc[:, :, :, :], decc[:, :, :, :], 1.0 / 16.0, None, ALU.mult)
    scr = dict(ss=S.sb('ss', [128, 1], F32), rs=S.sb('rs', [128, 1], F32), xs=S.sb('xs', [128, 1024], BF16),
               xtmp=S.sb('xtmp', [128, 1024], F32), ra=S.sb('ra', [128, 1024], F32), rb=S.sb('rb', [128, 1024], F32),
               rc=S.sb('rc', [128, 1024], F32), rd=S.sb('rd', [128, 1024], F32))
    ht = [S.sb('ht%d' % i, [128, 1024], F32) for i in range(2)]
    cst = [S.sb('cst%d' % i, [128, 2, 128], F32) for i in range(2)]
    xTt = S.sb('xTt', [128, 8, 128], BF16)
    Psb = S.sb('Psb', [128, 2048], F32)
    qkr = S.sb('qkr', [128, 2048], F32)
    scb = [S.sb('scb%d' % i, [128, 1024], BF16) for i in range(4)]
    vb = S.sb('vb', [128, 2048], BF16)
    gs = S.sb('gs', [128, 2048], BF16)
    stg = [S.sb('stg%d' % i, [128, 8, 128], BF16) for i in range(4)]
    pP = [S.ps('pP%d' % i, [128, 512], F32) for i in range(4)]
    np_ = [0]
    names = ['QfT', 'QbT', 'KfT', 'KbT']
    for t in range(ntiles):
        s = 1 if t < 2 else 0
        h = ht[t % 2]
        S.dma('sp', h[:, :], D['hin'][t * 128:(t + 1) * 128, :])
        norm_T(S, C, h[:, :], V(xTt, 0, (128, 8), (1, 128)), md[:, s, 0, :], md[:, s, 1, :], scr)
        lhs = [xTt[:, k, :] for k in range(8)]
        if t >= 2:
            n = t - 2
            cs_ = cst[t % 2]
            S.dma('sp', cs_[:, 0, :], D['cos'][:, n, :])
            S.dma('sp', cs_[:, 1, :], D['sin'][:, n, :])
            proj_tm2(S, C, pP, np_, lhs, lambda k, c0, w: W[:, k, c0:c0 + w], 2048, Psb)
            rope_gen(S, (Psb, 0), (qkr, 0), 8, 256, 256, 128, cs_[:, 0, :], cs_[:, 1, :], scr)
            for kind in range(4):
                src_off = 0 if kind < 2 else 1024
                o_tt(S, 'dve' if kind % 2 == 0 else 'pool', V(scb[kind], 0, (256, 4), (1, 256)), V(qkr, src_off, (256, 4), (1, 256)),
                     V(dec, kind * 4, (1, 4), (0, 256)), ALU.mult)
        else:
            proj_tm2(S, C, pP, np_, lhs, lambda k, c0, w: W[:, k, 1024 + c0:1024 + c0 + w], 1024, Psb, pcol0=1024)
            for dr in range(2):
                o_tt(S, 'dve' if dr == 0 else 'pool', V(scb[2 + dr], 0, (256, 4), (1, 256)), V(Psb, 1024, (256, 4), (1, 256)),
                     V(decc, (t * 2 + dr) * 4, (1, 4), (0, 256)), ALU.mult)
        S.dma('sp', D['Kf'][t * 128:(t + 1) * 128, :], scb[2][:, :], is_output=True)
        S.dma('sp', D['Kb'][t * 128:(t + 1) * 128, :], scb[3][:, :], is_output=True)
        if t >= 2:
            n = t - 2
            items = []
            for kind in range(4):
                for hh in range(4):
                    for hf in range(2):
                        items.append((scb[kind][:, hh * 256 + hf * 128:hh * 256 + (hf + 1) * 128], stg[kind][:, hh * 2 + hf, :]))
            transposes_gen(S, C, items)
            for kind in range(4):
                S.dma('sp', D[names[kind]][:, :, :, n * 128:(n + 1) * 128].rearrange('h f p n -> p (h f) n'), stg[kind][:, :, :], is_output=True)
        for c in range(4):
            p = pP[np_[0] % 4]
            np_[0] += 1
            for k in range(8):
                o_mm(S, p[:, :], lhs[k], W[:, k, 2048 + c * 512:2048 + (c + 1) * 512], k == 0, k == 7)
            o_copy(S, C.evac_eng(), vb[:, c * 512:(c + 1) * 512], p[:, :])
        S.dma('sp', D['Vt'][t * 128:(t + 1) * 128, :], vb[:, :], is_output=True)
        if t >= 2:
            n = t - 2
            for c in range(4):
                p = pP[np_[0] % 4]
                np_[0] += 1
                for k in range(8):
                    o_mm(S, p[:, :], lhs[k], W[:, k, 4096 + c * 512:4096 + (c + 1) * 512], k == 0, k == 7)
                o_act(S, gs[:, c * 512:(c + 1) * 512], p[:, :], AF.Silu)
            S.dma('sp', D['Gs'][n * 128:(n + 1) * 128, :], gs[:, :], is_output=True)


def build_A2_ret(S, D, layer, ntiles=18):
    C = Ctx(S, D)
    ld = S.sb('ld', [128, 2, 4], F32)
    S.dma('sp', ld[:, :, :], D['ldrep'])
    wm = S.sb('wm', [128, 2, ntiles], F32)
    S.dma('sp', wm[:, :, :], D['wmul'])
    wt = S.sb('wt', [128, 2, ntiles, 4], F32)
    for dr in range(2):
        for t in range(ntiles):
            o_ts(S, 'dve', wt[:, dr, t, :], ld[:, dr, :], wm[:, dr, t:t + 1], None, ALU.mult)
    o_act(S, wt[:, :, :, :], wt[:, :, :, :], AF.Exp)
    U = [[S.sb('U%d_%d' % (a, dr), [128, 4, 2, 512], F32) for dr in range(2)] for a in range(2)]
    for a in range(2):
        for dr in range(2):
            o_memset(S, 'pool', U[a][dr][:, :, :, :], 0.0)
    Kt = [[S.sb('Kt%d_%d' % (i, dr), [128, 1024], BF16) for dr in range(2)] for i in range(2)]
    Vt = [S.sb('Vt%d' % i, [128, 2048], BF16) for i in range(2)]
    pD = [S.ps('pD%d' % i, [128, 512], F32) for i in range(4)]
    nd = 0
    for t in range(ntiles):
        a = 0 if t < 2 else 1
        S.dma('sp', Kt[t % 2][0][:, :], D['Kf'][t * 128:(t + 1) * 128, :])
        S.dma('sp', Kt[t % 2][1][:, :], D['Kb'][t * 128:(t + 1) * 128, :])
        S.dma('sp', Vt[t % 2][:, :], D['Vt'][t * 128:(t + 1) * 128, :])
        for dr in range(2):
            for hh in range(4):
                for hf in range(2):
                    p = pD[nd % 4]
                    nd += 1
                    o_mm(S, p[:, :], Kt[t % 2][dr][:, hh * 256 + hf * 128:hh * 256 + (hf + 1) * 128], Vt[t % 2][:, hh * 512:(hh + 1) * 512], True, True)
                    u = U[a][dr][:, hh, hf, :]
                    o_stt(S, 'dve', u, p[:, :], wt[:, dr, t, hh:hh + 1], u, ALU.mult, ALU.add)
    if 'Lfb' in D:
        for dr in range(2):
            S.dma('sp', D['Sc'][dr].rearrange('h f p n -> p h f n'), U[0][dr][:, :, :, :], is_output=True)
            S.dma('sp', D['Lfb'][dr].rearrange('h f p n -> p h f n'), U[1][dr][:, :, :, :], is_output=True)
        return
    outs = [['Scf', 'Scb'], ['Lf', 'Lb']]
    for a in range(2):
        for dr in range(2):
            S.dma('sp', D[outs[a][dr]].rearrange('h f p n -> p h f n'), U[a][dr][:, :, :, :], is_output=True)


def build_B_ret(S, D, layer, nchunks=16, nsrc=5):
    C = Ctx(S, D)
    gate = S.sb('g1rep', [128, 1024], F32)
    S.dma('sp', gate[:, :], D['modv'][layer, 0, 2048:3072].partition_broadcast(128))
    ld = S.sb('ld', [128, 2, 4], F32)
    S.dma('sp', ld[:, :, :], D['ldrep'])
    cm = S.sb('cm', [128, 2, nsrc, 2], F32)
    S.dma('sp', cm[:, :, :, :], D['coefm'])
    coef = S.sb('coef', [128, 2, nsrc, 4], F32)
    for dr in range(2):
        for sidx in range(nsrc):
            o_ts(S, 'dve', coef[:, dr, sidx, :], ld[:, dr, :], cm[:, dr, sidx, 0:1], None, ALU.mult)
    o_act(S, coef[:, :, :, :], coef[:, :, :, :], AF.Exp)
    for dr in range(2):
        for sidx in range(nsrc):
            o_ts(S, 'dve', coef[:, dr, sidx, :], coef[:, dr, sidx, :], cm[:, dr, sidx, 1:2], None, ALU.mult)
    d128 = S.sb('d128', [128, 2, 4], F32)
    o_act(S, d128[:, :, :], ld[:, :, :], AF.Exp, scale=128.0)
    St = [S.sb('St%d' % dr, [128, 4, 2, 512], F32) for dr in range(2)]
    Sb = [S.sb('Sb%d' % dr, [128, 4, 2, 512], BF16) for dr in range(2)]
    Lp = [S.sb('Lp%d' % i, [128, 2, 512], F32) for i in range(2)]
    nl = 0
    for dr in range(2):
        o_memset(S, 'pool', St[dr][:, :, :, :], 0.0)
        for sidx in range(nsrc):
            for hh in range(4):
                lp = Lp[nl % 2]
                nl += 1
                if 'G_L' in D:
                    src = D['G_L'][sidx, dr, hh] if sidx < nsrc - 1 else D['Sc'][dr, hh]
                else:
                    src = D['Lall'][sidx, dr, hh]
                S.dma('sp', lp[:, :, :], src.rearrange('f p n -> p f n'))
                u = St[dr][:, hh, :, :]
                o_stt(S, 'dve', u, lp[:, :, :], coef[:, dr, sidx, hh:hh + 1], u, ALU.mult, ALU.add)
        o_copy(S, 'act', Sb[dr][:, :, :, :], St[dr][:, :, :, :])
    mD = S.sb('mD', [128, 2, 128], BF16)
    S.dma('sp', mD[:, :, :], D['maskD'].rearrange('d p n -> p d n'))
    onr = S.sb('onr', [128, 2048], F32)
    S.dma('sp', onr[:, :], D['onorm'])
    wo = S.sb('wo', [128, 16, 1024], BF16)
    S.dma('pool', wo[:, :, :], D['wo'].rearrange('(k p) m -> p k m', p=128))
    QTc = [S.sb('QTc%d' % i, [128, 8, 128], BF16) for i in range(2)]
    KTc = [S.sb('KTc%d' % i, [128, 8, 128], BF16) for i in range(2)]
    Kc = [S.sb('Kc%d' % i, [128, 1024], BF16) for i in range(2)]
    Vc = [S.sb('Vc%d' % i, [128, 2048], BF16) for i in range(2)]
    AT = [S.sb('AT%d' % i, [128, 128], BF16) for i in range(2)]
    yf = S.sb('yf', [128, 2048], F32)
    y = S.sb('y', [128, 2048], F32)
    sq = S.sb('sq', [128, 2048], F32)
    yb = S.sb('yb', [128, 2048], BF16)
    gsb = S.sb('gsb', [128, 2048], BF16)
    YT = S.sb('YT', [128, 16, 128], BF16)
    st4 = S.sb('st4', [128, 4], F32)
    mean = S.sb('mean', [128, 4], F32)
    rs4 = S.sb('rs4', [128, 4], F32)
    tmpS2 = [S.sb('tmpS%d' % i, [128, 512], F32) for i in range(2)]
    htl = [S.sb('htl%d' % i, [128, 1024], F32) for i in range(2)]
    hol = [S.sb('hol%d' % i, [128, 1024], F32) for i in range(2)]
    oot = [S.sb('oot%d' % i, [128, 512], F32) for i in range(2)]
    pS = [S.ps('pS%d' % i, [128, 512], F32) for i in range(2)]
    pY = [S.ps('pY%d' % i, [128, 512], F32) for i in range(2)]
    pD = [S.ps('pD%d' % i, [128, 512], F32) for i in range(2)]
    nm = ['Q%sT', 'K%sT', 'K%s']
    cnt = dict(s=0, y=0, d=0, o=0, it=0)
    for dr in range(2):
        sfx = 'f' if dr == 0 else 'b'
        order = list(range(nchunks)) if dr == 0 else list(range(nchunks - 1, -1, -1))
        for n in order:
            it = cnt['it']
            cnt['it'] += 1
            qt = QTc[it % 2]
            kt = KTc[it % 2]
            kc = Kc[it % 2]
            vc = Vc[it % 2]
            S.dma('sp', qt[:, :, :], D['Q%sT' % sfx][:, :, :, n * 128:(n + 1) * 128].rearrange('h f p n -> p (h f) n'))
            S.dma('sp', kt[:, :, :], D['K%sT' % sfx][:, :, :, n * 128:(n + 1) * 128].rearrange('h f p n -> p (h f) n'))
            S.dma('sp', kc[:, :], D['K%s' % sfx][256 + n * 128:256 + (n + 1) * 128, :])
            S.dma('sp', vc[:, :], D['Vt'][256 + n * 128:256 + (n + 1) * 128, :])
            if dr == 1:
                S.dma('sp', yf[:, :], D['yf'][n * 128:(n + 1) * 128, :])
                S.dma('sp', gsb[:, :], D['Gs'][n * 128:(n + 1) * 128, :])
            def scores(hh, slot):
                ps_ = pS[slot % 2]
                for hf in range(2):
                    o_mm(S, ps_[:, 0:128], kt[:, hh * 2 + hf, :], qt[:, hh * 2 + hf, :], hf == 0, hf == 1)

            sbase = cnt['s']
            cnt['s'] += 4
            scores(0, sbase)
            for hh in range(4):
                if hh + 1 < 4:
                    scores(hh + 1, sbase + hh + 1)
                pds = []
                for hf in range(2):
                    pd = pD[hf]
                    o_mm(S, pd[:, :], kc[:, hh * 256 + hf * 128:hh * 256 + (hf + 1) * 128], vc[:, hh * 512:(hh + 1) * 512], True, True)
                    pds.append(pd)
                ps_ = pS[(sbase + hh) % 2]
                at = AT[(sbase + hh) % 2]
                o_tt(S, 'dve', at[:, :], ps_[:, 0:128], mD[:, dr, :], ALU.mult)
                py = pY[cnt['y'] % 2]
                cnt['y'] += 1
                o_mm(S, py[:, :], at[:, :], vc[:, hh * 512:(hh + 1) * 512], True, False)
                for hf in range(2):
                    o_mm(S, py[:, :], qt[:, hh * 2 + hf, :], Sb[dr][:, hh, hf, :], False, hf == 1)
                if dr == 0:
                    o_copy(S, 'act', yf[:, hh * 512:(hh + 1) * 512], py[:, :])
                else:
                    o_tt(S, 'dve', y[:, hh * 512:(hh + 1) * 512], py[:, :], yf[:, hh * 512:(hh + 1) * 512], ALU.add)
                for hf in range(2):
                    pd = pds[hf]
                    cnt['d'] += 1
                    u = St[dr][:, hh, hf, :]
                    ts_ = tmpS2[cnt['d'] % 2]
                    o_tt(S, 'dve', ts_[:, :], pd[:, :], u, ALU.add)
                    o_act(S, u, ts_[:, :], AF.Copy, scale=d128[:, dr, hh:hh + 1])
                    o_act(S, Sb[dr][:, hh, hf, :], ts_[:, :], AF.Copy, scale=d128[:, dr, hh:hh + 1])
            if dr == 0:
                S.dma('sp', D['yf'][n * 128:(n + 1) * 128, :], yf[:, :], is_output=True)
            else:
                yv = V(y, 0, (512, 4), (1, 512))
                sv = V(sq, 0, (512, 4), (1, 512))
                o_red(S, 'dve', st4[:, 0:4], yv)
                o_ts(S, 'dve', mean[:, 0:4], st4[:, 0:4], 1.0 / 512.0, None, ALU.mult)
                o_tt(S, 'dve', yv, yv, V(mean, 0, (1, 4), (0, 512)), ALU.subtract)
                o_tt(S, 'pool', sv, yv, yv, ALU.mult)
                o_red(S, 'dve', st4[:, 0:4], sv)
                rstd_rows(S, C, st4[:, 0:4], rs4[:, 0:4], 4, 1.0 / 512.0)
                o_tt(S, 'dve', yv, yv, V(rs4, 0, (1, 4), (0, 512)), ALU.mult)
                o_tt(S, 'pool', sq[:, :], y[:, :], onr[:, :], ALU.mult)
                o_tt(S, 'dve', yb[:, :], sq[:, :], gsb[:, :], ALU.mult)
                items = [(yb[:, k * 128:(k + 1) * 128], YT[:, k, :]) for k in range(16)]
                transposes_gen(S, C, items)
                hti = htl[cnt['o'] % 2]
                hoi = hol[cnt['o'] % 2]
                cnt['o'] += 1
                S.dma('sp', hti[:, :], D['hin'][256 + n * 128:256 + (n + 1) * 128, :])
                for hf in range(2):
                    p = pS[cnt['s'] % 2]
                    cnt['s'] += 1
                    o_ = oot[hf]
                    for k in range(16):
                        o_mm(S, p[:, :], YT[:, k, :], wo[:, k, hf * 512:(hf + 1) * 512], k == 0, k == 15)
                    o_tt(S, 'dve', o_[:, :], p[:, :], gate[:, hf * 512:(hf + 1) * 512], ALU.mult)
                    o_tt(S, 'pool', hoi[:, hf * 512:(hf + 1) * 512], o_[:, :], hti[:, hf * 512:(hf + 1) * 512], ALU.add)
                S.dma('sp', D['hmid'][n * 128:(n + 1) * 128, :], hoi[:, :], is_output=True)


def ld_KT(S, dst, G, hsel, ntok, R):
    nlat = ntok - 256
    S.dma('sp', dst[:, :, 0:256], G[0, hsel, :, 0:256].rearrange('h p n -> p h n'))
    for r in range(R):
        S.dma('sp', dst[:, :, 256 + r * nlat:256 + (r + 1) * nlat], G[r, hsel, :, 256:ntok].rearrange('h p n -> p h n'))


def ld_KT1(S, dst, G, h, ntok, R):
    nlat = ntok - 256
    S.dma('sp', dst[:, 0:256], G[0, h, :, 0:256])
    for r in range(R):
        S.dma('sp', dst[:, 256 + r * nlat:256 + (r + 1) * nlat], G[r, h, :, 256:ntok])


def ld_V(S, dst, G, c0, c1, ntok, R):
    nlt = (ntok - 256) // 128
    S.dma('sp', dst[:, 0:2, :], G[0, 0:256, c0:c1].rearrange('(c p) m -> p c m', p=128))
    for r in range(R):
        S.dma('sp', dst[:, 2 + r * nlt:2 + (r + 1) * nlt, :], G[r, 256:ntok, c0:c1].rearrange('(c p) m -> p c m', p=128))


def build_B_gqa_f(S, D, layer, ntiles, R):
    C = Ctx(S, D, npT=0)
    A = AttnBufs(S, nS=4)
    gate = load_gate_rep(S, D, layer, 'g1rep', 0)
    ntok = ntiles * 128
    nkeys = 256 + R * (ntok - 256)
    nch = nkeys // 128
    QT = S.sb('QT', [128, 8, ntok], BF16)
    KT = S.sb('KT', [128, 2, nkeys], BF16)
    Vf = S.sb('Vf', [128, nch, 256], BF16)
    wo = S.sb('wo', [128, 8, 1024], BF16)
    S.dma('sp', QT[:, :, :], D['QT'].rearrange('h p n -> p h n'))
    ld_KT(S, KT, D['G_KT'], slice(0, 2), ntok, R)
    ld_V(S, Vf, D['G_V'], 0, 256, ntok, R)
    S.dma('pool', wo[:, :, :], D['wo'].rearrange('(k p) m -> p k m', p=128))
    scale = 128.0 ** -0.5
    nqb = (ntok - 256) // 512
    for h in range(8):
        kv = h // 4
        chunks = [([(KT[:, kv, c * 128:(c + 1) * 128], QT[:, h, 0:256])], Vf[:, c, kv * 128:(kv + 1) * 128], 128) for c in range(2)]
        attn_head_block(S, C, A, 256, chunks, QT[:, h, 0:256], 128, scale)
        for qb in range(nqb):
            q0 = 256 + qb * 512
            chunks = [([(KT[:, kv, c * 128:(c + 1) * 128], QT[:, h, q0:q0 + 512])], Vf[:, c, kv * 128:(kv + 1) * 128], 128) for c in range(nch)]
            attn_head_block(S, C, A, 512, chunks, QT[:, h, q0:q0 + 512], 128, scale)
    out_proj_residual(S, C, A, D, QT, 8, wo, gate, ntiles, D['hin'], D['hmid'])


def build_B_mla_f(S, D, layer, ntiles, R):
    C = Ctx(S, D, npT=0)
    A = AttnBufs(S, nS=4)
    gate = load_gate_rep(S, D, layer, 'g1rep', 0)
    ntok = ntiles * 128
    nkeys = 256 + R * (ntok - 256)
    nch = nkeys // 128
    nqb = (ntok - 256) // 512
    QTn = S.sb('QTn', [128, 8, ntok], BF16)
    QTr = S.sb('QTr', [128, 8, ntok], BF16)
    KTn = S.sb('KTn', [128, nkeys], BF16)
    KTr = S.sb('KTr', [128, nkeys], BF16)
    Vh = S.sb('Vh', [128, nch, 128], BF16)
    wo = S.sb('wo', [128, 8, 1024], BF16)
    S.dma('sp', QTn[:, :, :], D['QTn'].rearrange('h p n -> p h n'))
    S.dma('sp', QTr[0:64, :, :], D['QTr'].rearrange('h p n -> p h n'))
    S.dma('pool', wo[:, :, :], D['wo'].rearrange('(k p) m -> p k m', p=128))
    for h in range(8):
        ld_KT1(S, KTn, D['G_KTn'], h, ntok, R)
        ld_KT1(S, KTr[0:64, :], D['G_KTr'], h, ntok, R)
        ld_V(S, Vh, D['G_V'], h * 128, (h + 1) * 128, ntok, R)
        chunks = [([(KTn[:, c * 128:(c + 1) * 128], QTn[:, h, 0:256]), (KTr[0:64, c * 128:(c + 1) * 128], QTr[0:64, h, 0:256])],
                   Vh[:, c, :], 128) for c in range(2)]
        attn_head_block(S, C, A, 256, chunks, QTn[:, h, 0:256], 128, 1.0)
        for qb in range(nqb):
            q0 = 256 + qb * 512
            chunks = [([(KTn[:, c * 128:(c + 1) * 128], QTn[:, h, q0:q0 + 512]), (KTr[0:64, c * 128:(c + 1) * 128], QTr[0:64, h, q0:q0 + 512])],
                       Vh[:, c, :], 128) for c in range(nch)]
            attn_head_block(S, C, A, 512, chunks, QTn[:, h, q0:q0 + 512], 128, 1.0)
    out_proj_residual(S, C, A, D, QTn, 8, wo, gate, ntiles, D['hin'], D['hmid'])


def build_B_na_f(S, D, layer, ntiles, R):
    C = Ctx(S, D, npT=0)
    A = AttnBufs(S, nS=4)
    gate = load_gate_rep(S, D, layer, 'g1rep', 0)
    ntok = ntiles * 128
    nlat = ntok - 256
    nlt = nlat // 128
    nqb = nlat // 512
    nloc = nlat + 512
    QT = S.sb('QT', [128, 8, ntok], BF16)
    KTp = [S.sb('KTp%d' % i, [128, 256 + nloc], BF16) for i in range(2)]
    Vp = [S.sb('Vp%d' % i, [128, 2 + nloc // 128, 128], BF16) for i in range(2)]
    wo = S.sb('wo', [128, 8, 1024], BF16)
    TT = [S.sb('TT%d' % i, [128, 2, 1408], BF16) for i in range(2)]
    mK = S.sb('mK', [2, 128], BF16)
    mQs = [S.sb('mQ%d' % i, [2, 8 * 512], BF16) for i in range(2)]
    hw = S.sb('hw', [128, 2, R], F32)
    slabK = [S.sb('slabK%d' % i, [128, R, 256], BF16) for i in range(2)]
    slabV = [S.sb('slabV%d' % i, [128, R, 2, 128], BF16) for i in range(2)]
    S.dma('sp', hw[:, :, :], D['hw'])
    S.dma('sp', QT[:, :, :], D['QT'].rearrange('h p n -> p h n'))
    S.dma('pool', wo[:, :, :], D['wo'].rearrange('(k p) m -> p k m', p=128))
    S.dma('sp', mK[:, :], D['maskK'])
    GK = D['G_KT']
    GV = D['G_V']
    nm = 0
    for pair in range(8):
        tt = TT[pair % 2]
        KT = KTp[pair % 2]
        Vf = Vp[pair % 2]
        pc0, pc1 = pair * 128, (pair + 1) * 128
        S.dma('sp', tt[:, :, :], D['TT'][2 * pair:2 * pair + 2].rearrange('h p n -> p h n'))
        S.dma('sp', KT[:, 0:256], D['KT'][pair, :, 0:256])
        S.dma('sp', KT[:, 512:512 + nlat], D['KT'][pair, :, 256:ntok])
        S.dma('sp', Vf[:, 0:2, :], D['Vt'][0:256, pc0:pc1].rearrange('(c p) m -> p c m', p=128))
        S.dma('sp', Vf[:, 4:4 + nlt, :], D['Vt'][256:ntok, pc0:pc1].rearrange('(c p) m -> p c m', p=128))
        for a in range(2):
            sk = slabK[a]
            sv = slabV[a]
            if a == 0:
                S.dma('sp', sk[:, :, :], GK[:, pair, :, 256:512].rearrange('r p n -> p r n'))
                for r in range(R):
                    S.dma('sp', sv[:, r, :, :], GV[r, 256:512, pc0:pc1].rearrange('(c p) m -> p c m', p=128))
                kd = KT[:, 256:512]
                vd = Vf[:, 2:4, :]
            else:
                S.dma('sp', sk[:, :, :], GK[:, pair, :, 0:256].rearrange('r p n -> p r n'))
                for r in range(R):
                    S.dma('sp', sv[:, r, :, :], GV[r, 0:256, pc0:pc1].rearrange('(c p) m -> p c m', p=128))
                kd = KT[:, 512 + nlat:768 + nlat]
                vd = Vf[:, 4 + nlt:6 + nlt, :]
            for r in range(R):
                w = hw[:, a, r:r + 1]
                if r == 0:
                    o_ts(S, 'dve', kd, sk[:, 0, :], w, None, ALU.mult)
                    o_ts(S, 'pool', vd, sv[:, 0, :, :], w, None, ALU.mult)
                else:
                    o_stt(S, 'dve', kd, sk[:, r, :], w, kd, ALU.mult, ALU.add)
                    o_stt(S, 'dve', vd, sv[:, r, :, :], w, vd, ALU.mult, ALU.add)
        for hh in range(2):
            pb = hh * 64
            chunks = [([(KT[pb:pb + 64, c * 128:(c + 1) * 128], QT[pb:pb + 64, pair, 0:256])],
                       Vf[:, c, :], 128) for c in range(2)]
            attn_head_block(S, C, A, 256, chunks, QT[pb:pb + 64, pair, 0:256], 128, 1.0, orow=(pb, pb + 64))
            for qb in range(nqb):
                q0 = 256 + qb * 512
                mQ = mQs[nm % 2]
                nm += 1
                S.dma('sp', mQ[:, :], D['maskQ'][:, qb * 4096:(qb + 1) * 4096])
                chunks = []
                for c in range(2):
                    chunks.append(([(KT[pb:pb + 64, c * 128:(c + 1) * 128], QT[pb:pb + 64, pair, q0:q0 + 512])],
                                   Vf[:, c, :], 128))
                for c in range(8):
                    m = 4 * qb + c
                    i0 = 14 - 2 * c
                    pieces = [(KT[pb:pb + 64, 256 + m * 128:256 + (m + 1) * 128], QT[pb:pb + 64, pair, q0:q0 + 512]),
                              (C.ident[:, :], tt[:, hh, i0 * 64:i0 * 64 + 512]),
                              (mK[0:2, :], mQ[0:2, c * 512:(c + 1) * 512])]
                    chunks.append((pieces, Vf[:, 2 + m, :], 128))
                attn_head_block(S, C, A, 512, chunks, QT[pb:pb + 64, pair, q0:q0 + 512], 128, 1.0, orow=(pb, pb + 64))
    out_proj_residual(S, C, A, D, QT, 8, wo, gate, ntiles, D['hin'], D['hmid'])


GROUPS = [[0, 1, 2, 3], [4, 5, 6, 7]]


def build_fused(S, X, nc, ntiles=18, R=4, groups=None):
    groups = groups or GROUPS
    bf = BF16
    ntok = ntiles * 128
    nlat = ntok - 256
    N = ntiles - 2

    def dram(name, shape, dt):
        return nc.dram_tensor(name, list(shape), dt).ap()

    def gather(name, src, rows, cols, dt):
        g = dram(name, [R * rows, cols], dt)
        z = mybir.dt.size(dt)
        maxrows = max(16, ((1 << 20) // (cols * z)) // 16 * 16)
        if rows <= maxrows:
            S.cc('AllGather', groups, src.opt(), g.opt())
            return g
        p0 = 0
        while p0 < rows:
            rp = min(maxrows, rows - p0)
            tmp = dram('%s_t%d' % (name, p0), [R * rp, cols], dt)
            S.cc('AllGather', groups, src[p0:p0 + rp, :].opt(), tmp.opt())
            for r in range(R):
                S.dma('sp', g[r * rows + p0:r * rows + p0 + rp, :], tmp[r * rp:(r + 1) * rp, :])
            p0 += rp
        return g

    S.final_names = {'out'}
    base = dict(ident=X['ident'])
    mpart = dram('mpart', [8, 1536], F32)
    Dm = dict(base, cin=X['cin'], mod_w=X['mod_w'], mod_b=X['mod_b'], modv=mpart.rearrange('(s i) c -> s i c', s=2))
    build_M(S, Dm, ncol=1536, nrow=2)
    gm = gather('G_mod', mpart, 8, 1536, F32)
    modv = dram('modv', [4, 2, 6144], F32)
    for r in range(R):
        for s_ in range(2):
            S.dma('sp', modv[:, s_, r * 1536:(r + 1) * 1536], gm[r * 8 + s_ * 4:r * 8 + s_ * 4 + 4, :])
    base['modv'] = modv
    hcur = X['hin']
    S.new_phase()
    QTn = dram('QTn0', [8, 128, ntok], bf)
    QTr = dram('QTr0', [8, 64, ntok], bf)
    KTn = dram('KTn0', [8, 128, ntok], bf)
    KTr = dram('KTr0', [8, 64, ntok], bf)
    Vt = dram('Vt0', [ntok, 1024], bf)
    build_A_mla(S, dict(base, norm1=X['norm1_0'], wdown=X['wdown'], wuq=X['wuq'], wukv=X['wukv'], lg=X['lg'], gq=X['gq0'], gk=X['gk0'],
                        cos=X['cos64'], sin=X['sin64'], hin=hcur, QTn=QTn, QTr=QTr, KTn=KTn, KTr=KTr, Vt=Vt), 0, ntiles=ntiles)
    gkn = gather('G_KTn0', KTn.rearrange('h p n -> (h p) n'), 8 * 128, ntok, bf).rearrange('(r h p) n -> r h p n', r=R, h=8)
    gkr = gather('G_KTr0', KTr.rearrange('h p n -> (h p) n'), 8 * 64, ntok, bf).rearrange('(r h p) n -> r h p n', r=R, h=8)
    gv = gather('G_V0', Vt, ntok, 1024, bf).rearrange('(r t) m -> r t m', r=R)
    S.new_phase()
    hmid = dram('hmid0', [ntok, 1024], F32)
    build_B_mla_f(S, dict(base, QTn=QTn, QTr=QTr, G_KTn=gkn, G_KTr=gkr, G_V=gv, wo=X['wo0'], hin=hcur, hmid=hmid), 0, ntiles, R)
    S.new_phase()
    h1 = dram('h1', [ntok, 1024], F32)
    build_C(S, dict(base, norm2=X['norm2_0'], w13=X['w13_0'], w2=X['w2_0'], hin=hmid, hout=h1), 0, ntiles=ntiles)
    hcur = h1
    S.new_phase()
    QT = dram('QT1', [8, 128, ntok], bf)
    KT = dram('KT1', [2, 128, ntok], bf)
    Vt = dram('Vt1', [ntok, 256], bf)
    build_A_gqa(S, dict(base, norm1=X['norm1_1'], wqkv=X['wqkv1'], gq=X['gq1'], cos=X['cos128'], sin=X['sin128'], hin=hcur, QT=QT, KT=KT, Vt=Vt),
                1, ntiles=ntiles)
    gk = gather('G_KT1', KT.rearrange('h p n -> (h p) n'), 2 * 128, ntok, bf).rearrange('(r h p) n -> r h p n', r=R, h=2)
    gv = gather('G_V1', Vt, ntok, 256, bf).rearrange('(r t) m -> r t m', r=R)
    S.new_phase()
    hmid = dram('hmid1', [ntok, 1024], F32)
    build_B_gqa_f(S, dict(base, QT=QT, G_KT=gk, G_V=gv, wo=X['wo1'], hin=hcur, hmid=hmid), 1, ntiles, R)
    S.new_phase()
    h2 = dram('h2', [ntok, 1024], F32)
    build_C(S, dict(base, norm2=X['norm2_1'], w13=X['w13_1'], w2=X['w2_1'], hin=hmid, hout=h2), 1, ntiles=ntiles)
    hcur = h2
    S.new_phase()
    QT = dram('QT2', [8, 128, ntok], bf)
    KT = dram('KT2', [8, 128, ntok], bf)
    Vt = dram('Vt2', [ntok, 1024], bf)
    KTh = dram('KTh2', [8, 128, 512], bf)
    Vh = dram('Vh2', [512, 1024], bf)
    build_A_na(S, dict(base, norm1=X['norm1_2'], wqkv=X['wqkv2'], gq=X['gq2'], hin=hcur, QT=QT, KT=KT, Vt=Vt, KTh=KTh, Vh=Vh), 2, ntiles=ntiles)
    gk = gather('G_KT2', KTh.rearrange('h p n -> (h p) n'), 8 * 128, 512, bf).rearrange('(r h p) n -> r h p n', r=R, h=8)
    gv = gather('G_V2', Vh, 512, 1024, bf).rearrange('(r t) m -> r t m', r=R)
    S.new_phase()
    hmid = dram('hmid2', [ntok, 1024], F32)
    build_B_na_f(S, dict(base, QT=QT, KT=KT, Vt=Vt, G_KT=gk, G_V=gv, TT=X['TT'], maskK=X['maskK'], maskQ=X['maskQ'], hw=X['hw'],
                         wo=X['wo2'], hin=hcur, hmid=hmid), 2, ntiles, R)
    S.new_phase()
    h3 = dram('h3', [ntok, 1024], F32)
    build_C(S, dict(base, norm2=X['norm2_2'], w13=X['w13_2'], w2=X['w2_2'], hin=hmid, hout=h3), 2, ntiles=ntiles)
    hcur = h3
    S.new_phase()
    xt = [4, 2, 128, nlat]
    T = dict(QfT=dram('QfT', xt, bf), QbT=dram('QbT', xt, bf), KfT=dram('KfT', xt, bf), KbT=dram('KbT', xt, bf),
             Kf=dram('Kf', [ntok, 1024], bf), Kb=dram('Kb', [ntok, 1024], bf), Vt=dram('Vt3', [ntok, 2048], bf), Gs=dram('Gs', [nlat, 2048], bf))
    build_A_ret(S, dict(base, norm1=X['norm1_3'], wqkvg=X['wqkvg'], ldrep=X['ldrep'], E=X['E'], Ec=X['Ec'], cos=X['cos256'], sin=X['sin256'],
                        hin=hcur, **T), 3, ntiles=ntiles)
    S.new_phase()
    Lfb = dram('Lfb', [2, 4, 2, 128, 512], F32)
    Sc = dram('Sc', [2, 4, 2, 128, 512], F32)
    build_A2_ret(S, dict(base, ldrep=X['ldrep'], wmul=X['wmul'], Kf=T['Kf'], Kb=T['Kb'], Vt=T['Vt'], Lfb=Lfb, Sc=Sc), 3, ntiles=ntiles)
    gl = gather('G_L', Lfb.rearrange('d h f p n -> (d h f p) n'), 2048, 512, F32).rearrange('(r d h f p) n -> r d h f p n', r=R, d=2, h=4, f=2)
    S.new_phase()
    hmid = dram('hmid3', [nlat, 1024], F32)
    yf = dram('yf', [nlat, 2048], F32)
    build_B_ret(S, dict(base, ldrep=X['ldrep'], coefm=X['coefm'], G_L=gl, Sc=Sc, maskD=X['maskD'], onorm=X['onorm'], wo=X['wo3'], hin=hcur,
                        hmid=hmid, yf=yf, **T), 3, nchunks=N, nsrc=R + 1)
    S.new_phase()
    build_C(S, dict(base, norm2=X['norm2_3'], w13=X['w13_3'], w2=X['w2_3'], hin=hmid, hout=X['out']), 3, ntiles=N, nctx=0)


NPDT = {np.dtype(np.float32): F32, np.dtype(ml_dtypes.bfloat16): BF16}


def launch(build_fn, in_maps, outs, **kw):
    nc = bass.Bass("TRN2", target_bir_lowering=False)
    S = Sched(nc)
    D = {}
    for name, arr in in_maps[0].items():
        D[name] = nc.dram_tensor(name, list(arr.shape), NPDT[arr.dtype], kind="ExternalInput").ap()
    for name, (shape, dt) in outs.items():
        D[name] = nc.dram_tensor(name, list(shape), NPDT[np.dtype(dt)], kind="ExternalOutput").ap()
    build_fn(S, D, **kw)
    S.emit()
    res = run_bass_kernel_spmd(nc, in_maps, core_ids=list(range(len(in_maps))))
    return [{k: np.asarray(r[k]) for k in outs} for r in res.results]


def fm(vec):
    return np.ascontiguousarray(np.asarray(vec, np.float32).reshape(-1, 128).T)


IDENT = np.eye(128, dtype=np.float32)


GRID_W = 64


def rope_tables_np(dim, j):
    t = np.arange(j * 2048, (j + 1) * 2048)
    row = (t // GRID_W).astype(np.float32)
    col = (t % GRID_W).astype(np.float32)
    quarter = dim // 4
    inv_freq = (10000.0 ** (-np.arange(quarter, dtype=np.float32) / quarter)).astype(np.float32)
    ang = np.concatenate([row[:, None] * inv_freq, col[:, None] * inv_freq], axis=-1).astype(np.float32)
    c = np.cos(ang).astype(np.float32).reshape(16, 128, dim // 2).transpose(1, 0, 2)
    s = np.sin(ang).astype(np.float32).reshape(16, 128, dim // 2).transpose(1, 0, 2)
    return np.ascontiguousarray(c), np.ascontiguousarray(s)


def rep128(v):
    return np.ascontiguousarray(np.broadcast_to(np.asarray(v, np.float32).reshape(1, -1), (128, v.size)))


def run_layer_gqa(LA, d, L, hcat, modv, NC=8):
    j0 = L // 4
    in_maps = []
    for c in range(NC):
        b, j = c // 4, c % 4
        cs, sn = rope_tables_np(128, j)
        gq = np.concatenate([np.tile(d['gqa_q_norm'][j0], 8), np.tile(d['gqa_k_norm'][j0], 2)])
        in_maps.append(dict(ident=IDENT, modv=modv[c], norm1=fm(d['norm1'][L]), wqkv=d['gqa_w_qkv'][j0], gq=rep128(gq),
                            cos=cs, sin=sn, hin=hcat[c]))
    bf = ml_dtypes.bfloat16
    resA = LA(build_A_gqa, in_maps, dict(QT=((8, 128, 2304), bf), KT=((2, 128, 2304), bf), Vt=((2304, 256), bf)), layer=L)
    in_maps = []
    for c in range(NC):
        b, j = c // 4, c % 4
        grp = [resA[b * 4 + jj] for jj in range(4)]
        KTf = np.concatenate([resA[c]['KT'][:, :, 0:256]] + [g['KT'][:, :, 256:] for g in grp], axis=2)
        Vf = np.concatenate([resA[c]['Vt'][0:256]] + [g['Vt'][256:] for g in grp], axis=0)
        in_maps.append(dict(ident=IDENT, modv=modv[c], QT=resA[c]['QT'], KTf=np.ascontiguousarray(KTf), Vf=np.ascontiguousarray(Vf),
                            wo=d['gqa_w_o'][j0], hin=hcat[c]))
    resB = LA(build_B_gqa, in_maps, dict(hmid=((2304, 1024), np.float32)), layer=L)
    return [x['hmid'] for x in resB]


def run_ffn(LA, d, L, hmid, modv, NC=8, nctx=2):
    in_maps = []
    nrow = hmid[0].shape[0]
    for c in range(NC):
        in_maps.append(dict(ident=IDENT, modv=modv[c], norm2=fm(d['norm2'][L]), w13=d['ffn_w13'][L], w2=d['ffn_w2'][L], hin=hmid[c]))
    res = LA(build_C, in_maps, dict(hout=((nrow, 1024), np.float32)), layer=L, ntiles=nrow // 128, nctx=nctx)
    return [x['hout'] for x in res]


def run_mod(LA, d, NC=8):
    cin = np.ascontiguousarray(np.stack([fm(d['c'][0]), fm(d['c'][1]), fm(d['c_ctx'])], axis=-1))
    ncol = 6144 // NC
    in_maps = []
    for c in range(NC):
        in_maps.append(dict(ident=IDENT, cin=cin, mod_w=np.ascontiguousarray(d['mod_w'][:, :, c * ncol:(c + 1) * ncol]),
                            mod_b=np.ascontiguousarray(d['mod_b'][:, c * ncol:(c + 1) * ncol])))
    res = LA(build_M, in_maps, dict(modv=((3, 4, ncol), np.float32)), ncol=ncol)
    full = np.concatenate([r_['modv'] for r_ in res], axis=2)
    out = []
    for c in range(NC):
        b = c // 4
        out.append(np.ascontiguousarray(np.stack([full[b], full[2]], axis=1)))
    return out


def kernel_unfused(**inputs):
    d = {k: np.asarray(v) for k, v in inputs.items()}
    NC = 8
    modv = run_mod(launch, d, NC)
    hcat = []
    for c in range(NC):
        b, j = c // 4, c % 4
        hcat.append(np.ascontiguousarray(np.concatenate([d['ctx'][b], d['x'][b, j * 2048:(j + 1) * 2048]], 0).astype(np.float32)))
    hmid = run_layer_mla(launch, d, 0, hcat, modv)
    hcat = run_ffn(launch, d, 0, hmid, modv)
    hmid = run_layer_gqa(launch, d, 1, hcat, modv)
    hcat = run_ffn(launch, d, 1, hmid, modv)
    hmid = run_layer_na(launch, d, 2, hcat, modv)
    hcat = run_ffn(launch, d, 2, hmid, modv)
    hmid = run_layer_ret(launch, d, 3, hcat, modv)
    hfin = run_ffn(launch, d, 3, hmid, modv, nctx=0)
    out = np.zeros((2, 8192, 1024), np.float32)
    for c in range(NC):
        b, j = c // 4, c % 4
        out[b, j * 2048:(j + 1) * 2048] = hfin[c]
    return out


def na_tables(rpb):
    H = rpb.shape[0]
    kc = np.arange(64)[:, None]
    qc = np.arange(64)[None, :]
    cs = np.clip(qc - 8, 0, 48)
    colvalid = (kc >= cs) & (kc < cs + 16)
    crel = np.clip(kc - qc + 15, 0, 30)
    TT = np.zeros((H, 2, 64, 22, 64), np.float32)
    for a in range(2):
        for i2 in range(22):
            i = i2 - 3 - a
            if 0 <= i <= 14:
                blk = rpb[:, 14 - i][:, crel]
                blk = np.where(colvalid[None], blk, np.float32(-30000.0))
            else:
                blk = np.broadcast_to(np.where(colvalid, np.float32(0), np.float32(-30000.0))[None], (H, 64, 64))
            TT[:, a, :, i2, :] = blk
    return np.ascontiguousarray(TT.reshape(H, 128, 22 * 64)).astype(ml_dtypes.bfloat16)


def na_masks(row_base, rows_total, nqb=4):
    mQ = np.zeros((2, nqb, 8, 8, 64), np.float32)
    for qb in range(nqb):
        R = row_base + 8 * qb
        for c in range(8):
            kr0 = R - 4 + 2 * c
            for a in range(2):
                kr = kr0 + a
                for t in range(8):
                    qr = R + t
                    r0 = min(max(qr - 4, 0), rows_total - 8)
                    ok = (0 <= kr < rows_total) and (r0 <= kr < r0 + 8)
                    mQ[a, qb, c, t, :] = 0.0 if ok else -30000.0
    mK = np.zeros((2, 2, 64), np.float32)
    mK[0, 0] = 1.0
    mK[1, 1] = 1.0
    bf = ml_dtypes.bfloat16
    return np.ascontiguousarray(mQ.reshape(2, -1)).astype(bf), np.ascontiguousarray(mK.reshape(2, 128)).astype(bf)


def run_layer_na(LA, d, L, hcat, modv, NC=8, ntiles=18, rows_total=128, cores_per_batch=4):
    j0 = L // 4
    bf = ml_dtypes.bfloat16
    ntok = ntiles * 128
    nlat = ntok - 256
    rows_core = nlat // 64
    in_maps = []
    gq = np.concatenate([np.tile(d['na_q_norm'][j0], 16), np.tile(d['na_k_norm'][j0], 16)])
    for c in range(NC):
        in_maps.append(dict(ident=IDENT, modv=modv[c], norm1=fm(d['norm1'][L]), wqkv=d['na_w_qkv'][j0], gq=rep128(gq), hin=hcat[c]))
    resA = LA(build_A_na, in_maps, dict(QT=((8, 128, ntok), bf), KT=((8, 128, ntok), bf), Vt=((ntok, 1024), bf)), layer=L, ntiles=ntiles)
    TT = na_tables(np.asarray(d['na_rpb'][j0]))
    in_maps = []
    for c in range(NC):
        b, j = c // cores_per_batch, c % cores_per_batch
        grp = [resA[b * cores_per_batch + jj] for jj in range(cores_per_batch)]
        Kall = np.concatenate([g['KT'][:, :, 256:] for g in grp], axis=2)
        Vall = np.concatenate([g['Vt'][256:] for g in grp], axis=0)
        row_base = j * rows_core
        KTl = np.zeros((8, 128, 2560), bf)
        Vl = np.zeros((2560, 1024), bf)
        lo = row_base - 4
        for lr in range(40):
            gr = lo + lr
            if 0 <= gr < rows_total:
                KTl[:, :, lr * 64:(lr + 1) * 64] = Kall[:, :, gr * 64:(gr + 1) * 64]
                Vl[lr * 64:(lr + 1) * 64] = Vall[gr * 64:(gr + 1) * 64]
        mQ, mK = na_masks(row_base, rows_total, nqb=4)
        in_maps.append(dict(ident=IDENT, modv=modv[c], QT=resA[c]['QT'], KTc=np.ascontiguousarray(resA[c]['KT'][:, :, 0:256]), KTl=KTl,
                            Vc=np.ascontiguousarray(resA[c]['Vt'][0:256]), Vl=Vl, TT=TT, maskK=mK, maskQ=mQ,
                            wo=d['na_w_o'][j0], hin=hcat[c]))
    resB = LA(build_B_na, in_maps, dict(hmid=((ntok, 1024), np.float32)), layer=L, ntiles=ntiles)
    return [x['hmid'] for x in resB]


def run_layer_mla(LA, d, L, hcat, modv, NC=8, ntiles=18, cores_per_batch=4):
    j0 = L // 4
    bf = ml_dtypes.bfloat16
    ntok = ntiles * 128
    lg = np.concatenate([d['mla_q_lora_norm'][j0], d['mla_kv_lora_norm'][j0]])
    in_maps = []
    for c in range(NC):
        b, j = c // cores_per_batch, c % cores_per_batch
        cs, sn = rope_tables_np(64, j)
        in_maps.append(dict(ident=IDENT, modv=modv[c], norm1=fm(d['norm1'][L]), wdown=d['mla_w_down'][j0], wuq=d['mla_w_uq'][j0],
                            wukv=d['mla_w_ukv'][j0], lg=fm(lg), gq=rep128(np.tile(d['mla_q_norm'][j0], 8)), gk=rep128(d['mla_k_norm'][j0]),
                            cos=cs, sin=sn, hin=hcat[c]))
    resA = LA(build_A_mla, in_maps, dict(QTn=((8, 128, ntok), bf), QTr=((8, 64, ntok), bf), KTn=((8, 128, ntok), bf),
                                         KTr=((8, 64, ntok), bf), Vt=((ntok, 1024), bf)), layer=L, ntiles=ntiles)
    in_maps = []
    for c in range(NC):
        b, j = c // cores_per_batch, c % cores_per_batch
        grp = [resA[b * cores_per_batch + jj] for jj in range(cores_per_batch)]
        KTnf = np.concatenate([resA[c]['KTn'][:, :, 0:256]] + [g['KTn'][:, :, 256:] for g in grp], axis=2)
        KTrf = np.concatenate([resA[c]['KTr'][:, :, 0:256]] + [g['KTr'][:, :, 256:] for g in grp], axis=2)
        Vf = np.concatenate([resA[c]['Vt'][0:256]] + [g['Vt'][256:] for g in grp], axis=0)
        in_maps.append(dict(ident=IDENT, modv=modv[c], QTn=resA[c]['QTn'], QTr=resA[c]['QTr'], KTnf=np.ascontiguousarray(KTnf),
                            KTrf=np.ascontiguousarray(KTrf), Vf=np.ascontiguousarray(Vf), wo=d['mla_w_o'][j0], hin=hcat[c]))
    nkeys = in_maps[0]['Vf'].shape[0]
    resB = LA(build_B_mla, in_maps, dict(hmid=((ntok, 1024), np.float32)), layer=L, ntiles=ntiles, nkeys=nkeys)
    return [x['hmid'] for x in resB]


def run_layer_ret(LA, d, L, hcat, modv, NC=8, ntiles=18, cores_per_batch=4):
    j0 = L // 4
    bf = ml_dtypes.bfloat16
    ntok = ntiles * 128
    N = ntiles - 2
    nlat = N * 128
    J = cores_per_batch
    p = np.arange(128, dtype=np.float32)
    E = np.stack([p, 127 - p, -p, -(127 - p)], axis=1).astype(np.float32)
    Ec = np.zeros((128, 2, 2), np.float32)
    for tt_ in range(2):
        m = tt_ * 128 + p
        Ec[:, tt_, 0] = 256 - m
        Ec[:, tt_, 1] = m + 1
    wmul = np.zeros((128, 2, ntiles), np.float32)
    for n in range(N):
        wmul[:, 0, 2 + n] = 128 * (N - n)
        wmul[:, 1, 2 + n] = 128 * (n + 1)
    ldrep = np.ascontiguousarray(np.broadcast_to(np.stack([d['ret_log_decay_fwd'][j0], d['ret_log_decay_bwd'][j0]])[None], (128, 2, 4))).astype(np.float32)
    tq = np.arange(128)
    maskD = np.stack([(tq[:, None] <= tq[None, :]), (tq[:, None] >= tq[None, :])]).astype(np.float32).astype(bf)
    in_maps = []
    for c in range(NC):
        b, j = c // J, c % J
        cs, sn = rope_tables_np(256, j)
        in_maps.append(dict(ident=IDENT, modv=modv[c], norm1=fm(d['norm1'][L]), wqkvg=d['ret_w_qkvg'][j0], ldrep=ldrep, E=E, Ec=Ec,
                            cos=cs[:, :N], sin=sn[:, :N], hin=hcat[c]))
        in_maps[-1]['cos'] = np.ascontiguousarray(in_maps[-1]['cos'])
        in_maps[-1]['sin'] = np.ascontiguousarray(in_maps[-1]['sin'])
    xt = (4, 2, 128, nlat)
    resA = LA(build_A_ret, in_maps, dict(QfT=(xt, bf), QbT=(xt, bf), KfT=(xt, bf), KbT=(xt, bf), Kf=((ntok, 1024), bf), Kb=((ntok, 1024), bf),
                                         Vt=((ntok, 2048), bf), Gs=((nlat, 2048), bf)), layer=L, ntiles=ntiles)
    in_maps = [dict(ident=IDENT, ldrep=ldrep, wmul=wmul, Kf=resA[c]['Kf'], Kb=resA[c]['Kb'], Vt=resA[c]['Vt']) for c in range(NC)]
    st = (4, 2, 128, 512)
    resA2 = LA(build_A2_ret, in_maps, dict(Scf=(st, np.float32), Scb=(st, np.float32), Lf=(st, np.float32), Lb=(st, np.float32)), layer=L, ntiles=ntiles)
    in_maps = []
    for c in range(NC):
        b, j = c // J, c % J
        grp = [resA2[b * J + jj] for jj in range(J)]
        Lall = np.zeros((J + 1, 2, 4, 2, 128, 512), np.float32)
        for i in range(J):
            Lall[i, 0] = grp[i]['Lf']
            Lall[i, 1] = grp[i]['Lb']
        Lall[J, 0] = resA2[c]['Scf']
        Lall[J, 1] = resA2[c]['Scb']
        cm = np.zeros((128, 2, J + 1, 2), np.float32)
        for i in range(J):
            if i < j:
                cm[:, 0, i] = (128 * N * (j - 1 - i), 1.0)
            if i > j:
                cm[:, 1, i] = (128 * N * (i - 1 - j), 1.0)
        cm[:, 0, J] = (128 * N * j, 1.0)
        cm[:, 1, J] = (128 * N * (J - 1 - j), 1.0)
        a = resA[c]
        in_maps.append(dict(ident=IDENT, modv=modv[c], ldrep=ldrep, coefm=cm, Lall=Lall, maskD=maskD, onorm=rep128(d['ret_out_norm'][j0]),
                            wo=d['ret_w_o'][j0], QfT=a['QfT'], QbT=a['QbT'], KfT=a['KfT'], KbT=a['KbT'], Kf=a['Kf'], Kb=a['Kb'], Vt=a['Vt'],
                            Gs=a['Gs'], hin=hcat[c]))
    resB = LA(build_B_ret, in_maps, dict(hmid=((nlat, 1024), np.float32), yf=((nlat, 2048), np.float32)), layer=L, nchunks=N, nsrc=J + 1)
    return [x['hmid'] for x in resB]


def rope_tables_gen(dim, j, nlat):
    t = np.arange(j * nlat, (j + 1) * nlat)
    row = (t // GRID_W).astype(np.float32)
    col = (t % GRID_W).astype(np.float32)
    quarter = dim // 4
    inv_freq = (10000.0 ** (-np.arange(quarter, dtype=np.float32) / quarter)).astype(np.float32)
    ang = np.concatenate([row[:, None] * inv_freq, col[:, None] * inv_freq], axis=-1).astype(np.float32)
    nt = nlat // 128
    c = np.cos(ang).astype(np.float32).reshape(nt, 128, dim // 2).transpose(1, 0, 2)
    s = np.sin(ang).astype(np.float32).reshape(nt, 128, dim // 2).transpose(1, 0, 2)
    return np.ascontiguousarray(c), np.ascontiguousarray(s)


def fused_inputs(d, NC=8, ntiles=18, R=4):
    bf = ml_dtypes.bfloat16
    ntok = ntiles * 128
    nlat = ntok - 256
    N = ntiles - 2
    rows_core = nlat // 64
    rows_total = rows_core * R
    f32 = lambda a: np.ascontiguousarray(np.asarray(a, np.float32))
    common = dict(ident=IDENT)
    for L in range(4):
        common['norm1_%d' % L] = fm(d['norm1'][L])
        common['norm2_%d' % L] = fm(d['norm2'][L])
        common['w13_%d' % L] = f32(d['ffn_w13'][L])
        common['w2_%d' % L] = f32(d['ffn_w2'][L])
    lg = np.concatenate([d['mla_q_lora_norm'][0], d['mla_kv_lora_norm'][0]])
    common.update(wdown=f32(d['mla_w_down'][0]), wuq=f32(d['mla_w_uq'][0]), wukv=f32(d['mla_w_ukv'][0]), lg=fm(lg),
                  gq0=rep128(np.tile(d['mla_q_norm'][0], 8)), gk0=rep128(d['mla_k_norm'][0]), wo0=f32(d['mla_w_o'][0]))
    common.update(wqkv1=f32(d['gqa_w_qkv'][0]), gq1=rep128(np.concatenate([np.tile(d['gqa_q_norm'][0], 8), np.tile(d['gqa_k_norm'][0], 2)])),
                  wo1=f32(d['gqa_w_o'][0]))
    common.update(wqkv2=f32(d['na_w_qkv'][0]), gq2=rep128(np.concatenate([np.tile(d['na_q_norm'][0], 16), np.tile(d['na_k_norm'][0], 16)])),
                  TT=na_tables(np.asarray(d['na_rpb'][0])), wo2=f32(d['na_w_o'][0]))
    p = np.arange(128, dtype=np.float32)
    E = np.stack([p, 127 - p, -p, -(127 - p)], axis=1).astype(np.float32)
    Ec = np.zeros((128, 2, 2), np.float32)
    for tt_ in range(2):
        m = tt_ * 128 + p
        Ec[:, tt_, 0] = 256 - m
        Ec[:, tt_, 1] = m + 1
    wmul = np.zeros((128, 2, ntiles), np.float32)
    for n in range(N):
        wmul[:, 0, 2 + n] = 128 * (N - n)
        wmul[:, 1, 2 + n] = 128 * (n + 1)
    ldrep = np.ascontiguousarray(np.broadcast_to(np.stack([d['ret_log_decay_fwd'][0], d['ret_log_decay_bwd'][0]])[None], (128, 2, 4))).astype(np.float32)
    tq = np.arange(128)
    maskD = np.stack([(tq[:, None] <= tq[None, :]), (tq[:, None] >= tq[None, :])]).astype(np.float32).astype(bf)
    common.update(wqkvg=f32(d['ret_w_qkvg'][0]), ldrep=ldrep, E=E, Ec=Ec, wmul=wmul, maskD=maskD, onorm=rep128(d['ret_out_norm'][0]),
                  wo3=f32(d['ret_w_o'][0]))
    in_maps = []
    for c in range(NC):
        b, j = c // R, c % R
        m = dict(common)
        m['hin'] = np.ascontiguousarray(np.concatenate([d['ctx'][b], d['x'][b, j * nlat:(j + 1) * nlat]], 0).astype(np.float32))
        m['cin'] = np.ascontiguousarray(np.stack([fm(d['c'][b]), fm(d['c_ctx'])], axis=-1))
        m['mod_w'] = np.ascontiguousarray(d['mod_w'][:, :, j * 1536:(j + 1) * 1536])
        m['mod_b'] = np.ascontiguousarray(d['mod_b'][:, j * 1536:(j + 1) * 1536])
        for dim in (64, 128, 256):
            cs, sn = rope_tables_gen(dim, j, nlat)
            m['cos%d' % dim] = cs
            m['sin%d' % dim] = sn
        mQ, mK = na_masks(j * rows_core, rows_total, nqb=nlat // 512)
        m['maskQ'] = mQ
        m['maskK'] = mK
        hw = np.zeros((128, 2, R), np.float32)
        if j - 1 >= 0:
            hw[:, 0, j - 1] = 1.0
        if j + 1 < R:
            hw[:, 1, j + 1] = 1.0
        m['hw'] = hw
        cm = np.zeros((128, 2, R + 1, 2), np.float32)
        for i in range(R):
            if i < j:
                cm[:, 0, i] = (128 * N * (j - 1 - i), 1.0)
            if i > j:
                cm[:, 1, i] = (128 * N * (i - 1 - j), 1.0)
        cm[:, 0, R] = (128 * N * j, 1.0)
        cm[:, 1, R] = (128 * N * (R - 1 - j), 1.0)
        m['coefm'] = cm
        in_maps.append(m)
    return in_maps


ARENA_BYTES = 200 * 1024


def build_program(in_map0, ntiles, R, groups):
    nc = bass.Bass("TRN2", target_bir_lowering=False)
    S = Sched(nc)
    S.use_arena(ARENA_BYTES)
    X = {}
    for name, arr in in_map0.items():
        X[name] = nc.dram_tensor(name, list(arr.shape), NPDT[arr.dtype], kind="ExternalInput").ap()
    X['out'] = nc.dram_tensor('out', [(ntiles - 2) * 128, 1024], F32, kind="ExternalOutput").ap()
    build_fused(S, X, nc, ntiles=ntiles, R=R, groups=groups)
    S.emit()
    return nc, S


def kernel(**inputs):
    d = {k: np.asarray(v) for k, v in inputs.items()}
    NC, R, ntiles = 8, 4, 18
    in_maps = fused_inputs(d, NC, ntiles, R)
    nc, S = build_program(in_maps[0], ntiles, R, GROUPS)
    res = run_bass_kernel_spmd(nc, in_maps, core_ids=list(range(NC)))
    nlat = (ntiles - 2) * 128
    out = np.zeros((2, R * nlat, 1024), np.float32)
    for c in range(NC):
        b, j = c // R, c % R
        out[b, j * nlat:(j + 1) * nlat] = np.asarray(res.results[c]['out'])
    return out
```
